# Optimizing a Trainium2 kernel written in Bass

```python
import jax, jax.numpy as jnp
from jax import lax
import numpy as np

D_MODEL = 1024
BATCH = 2
SEQ = 8192
DEPTH = 2

N_META = 16
BLOCK = 128
N_PAD = BLOCK - N_META
FOX_HEADS = 8
FOX_HEAD_DIM = D_MODEL // 16
FOX_WIDTH = FOX_HEADS * FOX_HEAD_DIM
GLA_HEADS = 4
GLA_WIDTH = D_MODEL // 2
GLA_DV = GLA_WIDTH // GLA_HEADS
GLA_DK = GLA_DV // 2
GLA_KWIDTH = GLA_HEADS * GLA_DK
GLA_GATE_RANK = 16
GLA_GATE_TEMP = 16.0
D_FF = ((8 * D_MODEL // 3 + 127) // 128) * 128
RMS_EPS = 1e-6
SPLIT_SIZES = (FOX_WIDTH, FOX_WIDTH, FOX_WIDTH, FOX_HEADS, GLA_KWIDTH, GLA_KWIDTH, GLA_WIDTH, GLA_GATE_RANK, GLA_WIDTH, D_MODEL, D_MODEL)
N_IN = 3 * FOX_WIDTH + FOX_HEADS + 2 * GLA_KWIDTH + 2 * GLA_WIDTH + GLA_GATE_RANK + 2 * D_MODEL

kernel_name = "hybrid_fox_gla_macaron_meta"


def rmsnorm(x, gain):
    xf = x.astype(jnp.float32)
    y = xf * lax.rsqrt(jnp.mean(xf * xf, axis=-1, keepdims=True) + RMS_EPS)
    return (y * gain.astype(jnp.float32)).astype(x.dtype)


def swiglu(h, w_gu, w_down):
    gate, up = jnp.split(h @ w_gu, 2, axis=-1)
    return (jax.nn.silu(gate) * up) @ w_down


def to_heads(t, n_heads):
    b, l, _ = t.shape
    return t.reshape(b, l, n_heads, -1).transpose(0, 2, 1, 3)


def from_heads(t):
    b, h, l, d = t.shape
    return t.transpose(0, 2, 1, 3).reshape(b, l, h * d)


def fox_attention(q, k, v, log_f, key_valid):
    lp = q.shape[2]
    cum = jnp.cumsum(log_f, axis=-1)
    scale = FOX_HEAD_DIM ** -0.5
    outs = []
    for start in range(0, lp, BLOCK):
        end = start + BLOCK
        s = jnp.einsum('bhqd,bhkd->bhqk', q[:, :, start:end], k[:, :, :end]).astype(jnp.float32) * scale
        s = s + cum[:, :, start:end, None] - cum[:, :, None, :end]
        qpos = jnp.arange(start, end)[:, None]
        kpos = jnp.arange(end)[None, :]
        allowed = (kpos <= qpos) & (key_valid[None, :end] | (kpos == qpos))
        p = jax.nn.softmax(jnp.where(allowed, s, -jnp.inf), axis=-1)
        outs.append(jnp.einsum('bhqk,bhkd->bhqd', p.astype(v.dtype), v[:, :, :end]))
    return jnp.concatenate(outs, axis=2)


def gla_chunked(q, k, v, log_a):
    b, h, lp, dk = q.shape
    dv = v.shape[-1]
    n = lp // BLOCK

    def to_chunks(t):
        return jnp.moveaxis(t.reshape(b, h, n, BLOCK, t.shape[-1]), 2, 0)

    causal = jnp.tril(jnp.ones((BLOCK, BLOCK), dtype=bool))[:, :, None]

    def step(state, inp):
        qi, ki, vi, ai = inp
        cb = jnp.cumsum(ai, axis=2)
        qf, kf, vf = qi.astype(jnp.float32), ki.astype(jnp.float32), vi.astype(jnp.float32)
        inter = jnp.einsum('bhtk,bhkv->bhtv', qf * jnp.exp(cb), state)
        diff = cb[:, :, :, None, :] - cb[:, :, None, :, :]
        decay = jnp.exp(jnp.where(causal, diff, -jnp.inf))
        att = jnp.einsum('bhtk,bhsk,bhtsk->bhts', qf, kf, decay)
        intra = jnp.einsum('bhts,bhsv->bhtv', att, vf)
        last = cb[:, :, -1:, :]
        new_state = jnp.exp(last[:, :, 0, :, None]) * state + jnp.einsum('bhsk,bhsv->bhkv', kf * jnp.exp(last - cb), vf)
        return new_state, (inter + intra).astype(vi.dtype)

    s0 = jnp.zeros((b, h, dk, dv), jnp.float32)
    _, o = lax.scan(step, s0, (to_chunks(q), to_chunks(k), to_chunks(v), to_chunks(log_a)))
    return jnp.moveaxis(o, 0, 2).reshape(b, h, lp, dv)


def token_mixer(h, key_valid, w_in, w_alpha_up, b_alpha, b_f, g_gla_out, w_fox_o, w_gla_o, w_out):
    valid = key_valid.astype(jnp.float32)[None, :, None]
    points = np.cumsum(SPLIT_SIZES)[:-1].tolist()
    fq, fk, fv, ff, gq, gk, gv, ga, gr, g_merge_a, g_merge_b = jnp.split(h @ w_in, points, axis=-1)
    log_f = jax.nn.log_sigmoid(ff.astype(jnp.float32) + b_f.astype(jnp.float32)) * valid
    o_fox = fox_attention(to_heads(fq, FOX_HEADS), to_heads(fk, FOX_HEADS), to_heads(fv, FOX_HEADS),
                          log_f.transpose(0, 2, 1), key_valid)
    log_a = jax.nn.log_sigmoid((ga @ w_alpha_up).astype(jnp.float32) + b_alpha.astype(jnp.float32)) * (valid / GLA_GATE_TEMP)
    gk = gk * valid.astype(gk.dtype)
    o_gla = gla_chunked(to_heads(gq * (GLA_DK ** -0.5), GLA_HEADS), to_heads(gk, GLA_HEADS),
                        to_heads(gv, GLA_HEADS), to_heads(log_a, GLA_HEADS))
    o_gla = rmsnorm(o_gla, g_gla_out.reshape(GLA_HEADS, 1, GLA_DV))
    o_gla = from_heads(o_gla) * jax.nn.silu(gr)
    y = jax.nn.sigmoid(g_merge_a) * (from_heads(o_fox) @ w_fox_o) + jax.nn.sigmoid(g_merge_b) * (o_gla @ w_gla_o)
    return y @ w_out


def setup_inputs(seed: int = 0) -> dict:
    key = jax.random.key(seed)
    ks = jax.random.split(key, 21)

    def nrm(k, shape, scale):
        return jax.random.normal(k, shape, jnp.float32) * scale

    def gain(k):
        return 1.0 + nrm(k, (DEPTH, D_MODEL), 0.02)

    return {
        "x": nrm(ks[0], (BATCH, SEQ, D_MODEL), 1.0),
        "meta_tokens": nrm(ks[1], (N_META, D_MODEL), 1.0),
        "w_in": nrm(ks[2], (DEPTH, D_MODEL, N_IN), D_MODEL ** -0.5),
        "w_alpha_up": nrm(ks[3], (DEPTH, GLA_GATE_RANK, GLA_KWIDTH), GLA_GATE_RANK ** -0.5),
        "b_alpha": nrm(ks[4], (DEPTH, GLA_KWIDTH), 0.1),
        "b_f": 2.0 + nrm(ks[5], (DEPTH, FOX_HEADS), 0.1),
        "g_gla_out": 1.0 + nrm(ks[6], (DEPTH, GLA_WIDTH), 0.02),
        "w_fox_o": nrm(ks[7], (DEPTH, FOX_WIDTH, D_MODEL), FOX_WIDTH ** -0.5),
        "w_gla_o": nrm(ks[8], (DEPTH, GLA_WIDTH, D_MODEL), GLA_WIDTH ** -0.5),
        "w_out": nrm(ks[9], (DEPTH, D_MODEL, D_MODEL), D_MODEL ** -0.5),
        "w_ffn1_gu": nrm(ks[10], (DEPTH, D_MODEL, 2 * D_FF), D_MODEL ** -0.5),
        "w_ffn1_down": nrm(ks[11], (DEPTH, D_FF, D_MODEL), D_FF ** -0.5),
        "w_ffn2_gu": nrm(ks[12], (DEPTH, D_MODEL, 2 * D_FF), D_MODEL ** -0.5),
        "w_ffn2_down": nrm(ks[13], (DEPTH, D_FF, D_MODEL), D_FF ** -0.5),
        "g_pre_ffn1": gain(ks[14]),
        "g_post_ffn1": gain(ks[15]),
        "g_pre_mix": gain(ks[16]),
        "g_post_mix": gain(ks[17]),
        "g_pre_ffn2": gain(ks[18]),
        "g_post_ffn2": gain(ks[19]),
    }


def reference(x, meta_tokens, w_in, w_alpha_up, b_alpha, b_f, g_gla_out, w_fox_o, w_gla_o, w_out,
              w_ffn1_gu, w_ffn1_down, w_ffn2_gu, w_ffn2_down,
              g_pre_ffn1, g_post_ffn1, g_pre_mix, g_post_mix, g_pre_ffn2, g_post_ffn2):
    b = x.shape[0]
    pad = jnp.zeros((b, N_PAD, D_MODEL), x.dtype)
    meta = jnp.broadcast_to(meta_tokens.astype(x.dtype)[None], (b, N_META, D_MODEL))
    h = jnp.concatenate([pad, meta, x], axis=1)
    key_valid = jnp.arange(h.shape[1]) >= N_PAD
    for l in range(DEPTH):
        h = h + 0.5 * rmsnorm(swiglu(rmsnorm(h, g_pre_ffn1[l]), w_ffn1_gu[l], w_ffn1_down[l]), g_post_ffn1[l])
        mix = token_mixer(rmsnorm(h, g_pre_mix[l]), key_valid, w_in[l], w_alpha_up[l], b_alpha[l], b_f[l],
                          g_gla_out[l], w_fox_o[l], w_gla_o[l], w_out[l])
        h = h + rmsnorm(mix, g_post_mix[l])
        h = h + 0.5 * rmsnorm(swiglu(rmsnorm(h, g_pre_ffn2[l]), w_ffn2_gu[l], w_ffn2_down[l]), g_post_ffn2[l])
    return h[:, BLOCK:]
```

```python
import numpy as np
import concourse.bass as bass
import concourse.mybir as mybir
from concourse.bass_utils import run_bass_kernel_spmd

F32 = mybir.dt.float32
BF16 = mybir.dt.bfloat16
ALU = mybir.AluOpType
AF = mybir.ActivationFunctionType

EPOCH = 24000
N_DMA_SEMS = 24
D = 1024
DC = 8
FF = 2816
FC = 22
NIN = 5144
C_FQ, C_FK, C_FV, C_FF, C_GQ, C_GK, C_GV, C_GA, C_GR, C_MA, C_MB = 0, 512, 1024, 1536, 1544, 1800, 2056, 2568, 2584, 3096, 4120
EPS = 1e-6
NSM = 316


class Sched:
    ENGS = ("pe", "act", "dve", "pool", "sp")

    def __init__(self, nc):
        self.nc = nc
        self.ops = []
        self.last_write = {}
        self.readers = {}
        self.pending_barrier = None

    def op(self, eng, fn, reads=(), writes=(), dma=False, cc=False):
        deps = set()
        for r in reads:
            lw = self.last_write.get(r)
            if lw is not None:
                deps.add(lw)
        for w in writes:
            lw = self.last_write.get(w)
            if lw is not None:
                deps.add(lw)
            for rd in self.readers.get(w, ()):
                deps.add(rd)
        oid = len(self.ops)
        if self.pending_barrier is not None:
            pb = self.pending_barrier
            if eng not in pb["done"]:
                deps |= pb["deps"]
                pb["done"].add(eng)
        self.ops.append(dict(eng=eng, fn=fn, deps=deps, dma=dma, cc=cc))
        for r in reads:
            lst = self.readers.setdefault(r, [])
            if not dma:
                lst[:] = [x for x in lst if self.ops[x]["dma"] or self.ops[x]["eng"] != eng]
            lst.append(oid)
        for w in writes:
            self.last_write[w] = oid
            self.readers[w] = []
        return oid

    def barrier(self):
        deps = set()
        seen = set()
        nd = 0
        for i in range(len(self.ops) - 1, -1, -1):
            o = self.ops[i]
            if o["dma"]:
                if nd < N_DMA_SEMS:
                    deps.add(i)
                    nd += 1
            elif o["eng"] not in seen:
                seen.add(o["eng"])
                deps.add(i)
            if len(seen) == 5 and nd >= N_DMA_SEMS:
                break
        self.pending_barrier = dict(deps=deps, done=set())

    def emit(self, final_ops=()):
        nc = self.nc
        ops = self.ops
        n = len(ops)
        needs = [False] * n
        for o in ops:
            for d in o["deps"]:
                if ops[d]["eng"] == "pe" and o["eng"] == "pe" and not ops[d]["dma"] and not o["dma"]:
                    continue
                needs[d] = True
        for d in final_ops:
            needs[d] = True
        cnt = {e: 0 for e in self.ENGS}
        for i, o in enumerate(ops):
            if not o["dma"] and needs[i]:
                cnt[o["eng"]] += 1
        sems = {e: [nc.alloc_semaphore(f"s_{e}_{i}") for i in range(cnt[e] // EPOCH + 1)] for e in self.ENGS}
        dsems = [nc.alloc_semaphore(f"s_dma_{i}") for i in range(N_DMA_SEMS)]
        dcount = [0] * N_DMA_SEMS
        dlast = [None] * N_DMA_SEMS
        token = [None] * n
        ecount = {e: 0 for e in self.ENGS}
        dk = 0
        for i, o in enumerate(ops):
            if o["cc"]:
                token[i] = (nc.alloc_semaphore(f"s_cc_{i}"), 1, 1)
            elif o["dma"]:
                k = dk % N_DMA_SEMS
                dk += 1
                if dlast[k] is not None:
                    o["deps"].add(dlast[k])
                dcount[k] += 16
                token[i] = (dsems[k], dcount[k], 16)
                dlast[k] = i
            elif needs[i]:
                c = ecount[o["eng"]]
                ecount[o["eng"]] = c + 1
                token[i] = (sems[o["eng"]][c // EPOCH], c % EPOCH + 1, 1)
        streams = {e: [] for e in self.ENGS}
        for i, o in enumerate(ops):
            streams[o["eng"]].append(i)
        self.n_waits = 0

        def run_stream(e, eng):
            waited = {}
            for i in streams[e]:
                o = ops[i]
                for d in sorted(o["deps"]):
                    od = ops[d]
                    if e == "pe" and od["eng"] == "pe" and not od["dma"] and not o["dma"]:
                        continue
                    sem, val, _ = token[d]
                    key = id(sem)
                    if waited.get(key, 0) >= val:
                        continue
                    eng.wait_ge(sem, val)
                    self.n_waits += 1
                    waited[key] = val
                ins = o["fn"](eng)
                if token[i] is not None:
                    sem, val, step = token[i]
                    ins.then_inc(sem, step)
            if e == "sp":
                for d in final_ops:
                    sem, val, _ = token[d]
                    eng.wait_ge(sem, val)

        with nc.Block() as block:
            @block.tensor
            def _(eng):
                run_stream("pe", eng)

            @block.scalar
            def _(eng):
                run_stream("act", eng)

            @block.vector
            def _(eng):
                run_stream("dve", eng)

            @block.gpsimd
            def _(eng):
                run_stream("pool", eng)

            @block.sync
            def _(eng):
                run_stream("sp", eng)


class Arena:
    def __init__(self, nc, nbytes):
        self.t = nc.alloc_sbuf_tensor("arena", [128, nbytes // 2], BF16)
        self.n = nbytes // 2
        self.top = 0
        self.uid = 0

    def alloc(self, shape, dtype):
        free = 1
        for s in shape[1:]:
            free *= s
        ne = free * (2 if dtype == F32 else 1)
        ne = (ne + 15) // 16 * 16
        assert self.top + ne <= self.n, f"arena overflow {self.top + ne} > {self.n}"
        ap = self.t[:, self.top:self.top + ne]
        self.top += ne
        if dtype == F32:
            ap = ap.bitcast(F32)
        ap = ap[:, 0:free]
        if len(shape) == 3:
            ap = ap.rearrange("p (a b) -> p a b", a=shape[1])
        self.uid += 1
        return ap, ("ar", self.uid)

    def mark(self):
        return self.top

    def release(self, m):
        self.top = m


def build_program(NB, L=2, dumps=None, stages=None, mix_stop=None):
    nc = bass.Bass("TRN2", target_bir_lowering=False)
    SEG = NB * 128
    TL = 16 + SEG
    NPB = NB * 8 + 8
    NPU = NB * 258
    S = Sched(nc)

    def dram_in(name, shape, dt=F32):
        return nc.dram_tensor(name, shape, dt, kind="ExternalInput").ap()

    xT = dram_in("xT", [D, SEG])
    metaT = dram_in("metaT", [D, 16])
    cvec_d = dram_in("cvec", [128, 8])
    consts_d = dram_in("consts", [128, 5 * 128 + 4 * 512])
    smalls_d = dram_in("smalls", [128, L * NSM])
    w_in = dram_in("w_in", [L, D, NIN])
    w_au = dram_in("w_alpha_up", [L, 16, 256])
    w_fo = dram_in("w_fox_o", [L, 512, D])
    w_go = dram_in("w_gla_o", [L, 512, D])
    w_out = dram_in("w_out", [L, D, D])
    w_gu = [dram_in("w_ffn1_gu", [L, D, 2 * FF]), dram_in("w_ffn2_gu", [L, D, 2 * FF])]
    w_dn = [dram_in("w_ffn1_down", [L, FF, D]), dram_in("w_ffn2_down", [L, FF, D])]
    outT = nc.dram_tensor("outT", [D, SEG], F32, kind="ExternalOutput").ap()
    CH = min(4, NB)
    NCH = NB // CH
    PK = [[nc.dram_tensor(f"PK{l}_{c}", [512, CH * 128], BF16).ap() for c in range(NCH)] for l in range(L)]
    PV = [[nc.dram_tensor(f"PV{l}_{c}", [1024, CH * 64], BF16).ap() for c in range(NCH)] for l in range(L)]
    PB = [nc.dram_tensor(f"PB{l}", [128, NPB], F32).ap() for l in range(L)]
    PU = [[nc.dram_tensor(f"PU{l}_{c}", [128, CH * 258], F32).ap() for c in range(NCH)] for l in range(L)]
    GK = [[nc.dram_tensor(f"GK{l}_{c}", [4 * 512, CH * 128], BF16).ap() for c in range(NCH)] for l in range(L)]
    GV = [[nc.dram_tensor(f"GV{l}_{c}", [4 * 1024, CH * 64], BF16).ap() for c in range(NCH)] for l in range(L)]
    GB = [nc.dram_tensor(f"GB{l}", [4 * 128, NPB], F32).ap() for l in range(L)]
    GU = [[nc.dram_tensor(f"GU{l}_{c}", [4 * 128, CH * 258], F32).ap() for c in range(NCH)] for l in range(L)]
    dump_outs = {}

    def sb(name, shape, dt):
        return nc.alloc_sbuf_tensor("sb_" + name, shape, dt)

    hT = sb("hT", [128, DC, TL], F32)
    constS = sb("constS", [128, 5 * 128], F32)
    maskA = sb("maskA", [128, 4, 512], BF16)
    identb = sb("identb", [128, 128], BF16)
    smalls = sb("smalls", [128, L * NSM], F32)
    cvec = sb("cvec", [128, 8], F32)
    wau = sb("wau", [16, 256], BF16)
    lcS = sb("lcS", [128, NB + 1, 8], F32)
    runS = sb("runS", [128, 8], F32)
    m0S = sb("m0S", [128, 8], F32)
    kmeta = sb("kmeta", [128, 8, 16], BF16)
    vmeta = sb("vmeta", [128, 8, 128], BF16)
    Sst = sb("Sst", [128, 2, 128], F32)
    Sbf = sb("Sbf", [128, 2, 128], BF16)
    U0 = sb("U0", [128, 258], F32)
    gbT = sb("gbT", [128, 3, NPB], F32)
    pbS = sb("pbS", [128, NPB], F32)
    bown = sb("bown", [128, NB + 1, 8], F32)
    bmeta = sb("bmeta", [128, 2, 8], F32)
    Gt = sb("Gt", [128, 4, 8], F32)
    split = sb("split", [128, 8, 67], BF16)
    ps = [nc.alloc_psum_tensor(f"ps{i}", [128, 512], F32) for i in range(8)]
    A = Arena(nc, nc.sbuf_bytes_remaining - 1280)

    ident_f = constS[:, 0:128]
    trin16 = constS[:, 128:256]
    trirev16 = constS[:, 256:384]
    trin1 = constS[:, 384:512]
    trif = constS[:, 512:640]
    PSK = lambda k: ("ps", k)

    def dma(q, out, in_, reads, writes):
        return S.op(q, lambda e: e.dma_start(out=out, in_=in_), reads, writes, dma=True)

    def mm(out, lhsT, rhs, start, stop, reads, writes, **kw):
        return S.op("pe", lambda e: e.matmul(out, lhsT=lhsT, rhs=rhs, start=start, stop=stop, **kw), reads, writes)

    def act(out, in_, func, reads, writes, bias=0.0, scale=1.0):
        return S.op("act", lambda e: e.activation(out=out, in_=in_, func=func, bias=bias, scale=scale), reads, writes)

    def tt(eng, out, in0, in1, op, reads, writes):
        return S.op(eng, lambda e: e.tensor_tensor(out=out, in0=in0, in1=in1, op=op), reads, writes)

    def ts(eng, out, in0, s1, op0, reads, writes, s2=None, op1=None):
        if op1 is None:
            return S.op(eng, lambda e: e.tensor_scalar(out=out, in0=in0, scalar1=s1, scalar2=None, op0=op0), reads, writes)
        return S.op(eng, lambda e: e.tensor_scalar(out=out, in0=in0, scalar1=s1, scalar2=s2, op0=op0, op1=op1), reads, writes)

    def stt(eng, out, in0, scalar, in1, op0, op1, reads, writes):
        return S.op(eng, lambda e: e.scalar_tensor_tensor(out=out, in0=in0, scalar=scalar, in1=in1, op0=op0, op1=op1), reads, writes)

    def cp(eng, out, in_, reads, writes):
        if eng == "act":
            return S.op("act", lambda e: e.copy(out=out, in_=in_), reads, writes)
        return S.op(eng, lambda e: e.tensor_copy(out=out, in_=in_), reads, writes)

    def memset(eng, ap, val, writes):
        return S.op(eng, lambda e: e.memset(ap, val), (), writes)

    def recip(out, in_, reads, writes):
        return S.op("dve", lambda e: e.reciprocal(out=out, in_=in_), reads, writes)

    def dump(name, ap, keys, shape):
        if dumps is None or name not in dumps:
            return
        d = nc.dram_tensor("dbg_" + name, list(shape), F32 if ap.dtype == F32 else BF16, kind="ExternalOutput").ap()
        dump_outs[name] = dma("sp", d, ap, list(keys), [("dbgd", name)])

    bank_rr = [0]

    def bank():
        b = bank_rr[0]
        bank_rr[0] = (b + 1) % 8
        return b

    mk_init = A.mark()
    maskAf, _ = A.alloc([128, 4 * 512], F32)
    dma("sp", constS[:, :], consts_d[:, 0:640], [], ["constS"])
    dma("sp", maskAf[:, :], consts_d[:, 640:640 + 2048], [], ["maskAf"])
    dma("sp", smalls[:, :], smalls_d[:, :], [], ["smalls"])
    dma("sp", cvec[:, :], cvec_d[:, :], [], ["cvec"])
    cp("dve", maskA[:, :, :], maskAf[:, :].rearrange("p (a b) -> p a b", a=4), ["maskAf"], ["maskA"])
    cp("dve", identb[:, :], ident_f, ["constS"], ["identb"])
    S.barrier()
    A.release(mk_init)
    memset("pool", split[:, :, :], 0.0, ["split"])
    memset("pool", lcS[:, :, :], 0.0, [("lcS", b_) for b_ in range(NB + 1)])
    memset("pool", kmeta[:, :, :], 1.0, ["kmeta"])
    memset("pool", vmeta[:, :, :], 1.0, ["vmeta"])
    dma("sp", hT[:, :, 0:16], metaT.rearrange("(c p) t -> p c t", p=128), [], [("hT", 0)])
    tiles = [(0, 0, 16, [0])]
    b = 1
    while b <= NB:
        nbk = min(4, NB - b + 1)
        tiles.append((len(tiles), 16 + (b - 1) * 128, nbk * 128, list(range(b, b + nbk))))
        b += nbk
    for (ti, c0, n, blks) in tiles[1:]:
        dma("sp", hT[:, :, c0:c0 + n], xT.rearrange("(c p) t -> p c t", p=128)[:, :, c0 - 16:c0 - 16 + n], [], [("hT", ti)])

    def bcol(blk):
        return (0, 16) if blk == 0 else (16 + (blk - 1) * 128, 128)

    def sm(l, off, w):
        return smalls[:, l * NSM + off:l * NSM + off + w]

    def norm_rstd(srcs, n, skeys, sqb, sqk, rstd, rstdk, nfeat):
        bk = bank()
        for i, s_ap in enumerate(srcs):
            j = i % 2
            act(sqb[j][:, 0:n], s_ap, AF.Square, list(skeys[i]), [sqk[j]])
            mm(ps[bk][:, 0:n], trif_ones, sqb[j][:, 0:n], i == 0, i == len(srcs) - 1, [sqk[j], "constS2"], [PSK(bk)])
        act(rstd[:, 0:n], ps[bk][:, 0:n], AF.Sqrt, [PSK(bk)], [rstdk], bias=epsb[:, 0:1], scale=1.0 / nfeat)
        recip(rstd[:, 0:n], rstd[:, 0:n], [rstdk], [rstdk])

    ones_t = sb("ones_f", [128, 128], F32)
    memset("dve", ones_t[:, :], 1.0, ["constS2"])
    trif_ones = ones_t[:, :]
    nones_t = sb("nones_f", [128, 128], F32)
    memset("dve", nones_t[:, :], -1.0, ["constS3"])
    epsb = sb("epsb", [128, 1], F32)
    memset("dve", epsb[:, :], EPS, ["epsb"])
    oneb = sb("oneb", [128, 1], F32)
    memset("dve", oneb[:, :], 1.0, ["oneb"])

    def load_w(dst, dstk, w2d, col0, ncols, kc=DC, dcol=0):
        return dma("pool", dst[:, 0:kc, dcol:dcol + ncols], w2d.rearrange("(c p) f -> p c f", p=128)[:, :, col0:col0 + ncols], [], [dstk])

    def prenorm(l, goff, ti, c0, n, hn, hnk, sqb, sqk, rstd, rstdk):
        norm_rstd([hT[:, c, c0:c0 + n] for c in range(DC)], n, [[("hT", ti)]] * DC, sqb, sqk, rstd, rstdk, D)
        for c in range(DC):
            stt("dve", hn[:, c, 0:n], hT[:, c, c0:c0 + n], sm(l, goff + c, 1), rstd[:, 0:n], ALU.mult, ALU.mult,
                [("hT", ti), rstdk, "smalls"], [hnk])

    def postnorm_residual(l, goff, ti, c0, n, oT, oTk, sqb, sqk, rstd, rstdk, tmp, tmpk, scale):
        norm_rstd([oT[:, c, 0:n] for c in range(DC)], n, [[oTk]] * DC, sqb, sqk, rstd, rstdk, D)
        for c in range(DC):
            j = c % 2
            stt("dve", tmp[j][:, 0:n], oT[:, c, 0:n], sm(l, goff + c, 1), rstd[:, 0:n], ALU.mult, ALU.mult, [oTk, rstdk, "smalls"], [tmpk[j]])
            stt("dve", hT[:, c, c0:c0 + n], tmp[j][:, 0:n], scale, hT[:, c, c0:c0 + n], ALU.mult, ALU.add, [tmpk[j], ("hT", ti)], [("hT", ti)])

    def ffn(l, which, shared=None, after_group=None):
        mk = A.mark()
        W = 528
        if shared is None:
            hn, hnk = A.alloc([128, DC, W], BF16)
            sq0, sqk0 = A.alloc([128, 512], F32)
            sq1, sqk1 = A.alloc([128, 512], F32)
            rstd, rstdk = A.alloc([128, W], F32)
            wb = [A.alloc([128, DC, 512], BF16) for _ in range(4)]
        else:
            hn, hnk, (sq0, sq1), (sqk0, sqk1), rstd, rstdk, wb = shared
        actT, actk = A.alloc([128, FC, W], BF16)
        sil0, silk0 = A.alloc([128, 512], F32)
        sil1, silk1 = A.alloc([128, 512], F32)
        oT, oTk = A.alloc([128, DC, W], F32)
        wd = [A.alloc([128, FC, 128], BF16) for _ in range(2)]
        sqb, sqk, sil, silk = [sq0, sq1], [sqk0, sqk1], [sil0, sil1], [silk0, silk1]
        gpre = 0 if which == 0 else 32
        gpost = 8 if which == 0 else 40
        wgu2 = w_gu[which][l]
        wdn2 = w_dn[which][l]
        wi = 0
        di = 0
        tgroups = [[tiles[1] + (0,), tiles[0] + (512,)]] + [[t + (0,)] for t in tiles[2:]]
        for grp in tgroups:
            for (ti, c0, n, blks, bo_) in grp:
                prenorm(l, gpre, ti, c0, n, hn[:, :, bo_:bo_ + n], (hnk, bo_), sqb, sqk, rstd[:, bo_:bo_ + n], (rstdk, bo_))
            for fp in range(FC // 2):
                wt, wk = wb[wi % len(wb)]
                wi += 1
                load_w(wt, wk, wgu2, fp * 256, 256, dcol=0)
                load_w(wt, wk, wgu2, FF + fp * 256, 256, dcol=256)
                for sub in range(2):
                    fi = fp * 2 + sub
                    for (ti, c0, n, blks, bo_) in grp:
                        bg, bu = bank(), bank()
                        for c in range(DC):
                            mm(ps[bg][:, 0:n], wt[:, c, sub * 128:(sub + 1) * 128], hn[:, c, bo_:bo_ + n], c == 0, c == DC - 1, [wk, (hnk, bo_)], [PSK(bg)])
                        for c in range(DC):
                            mm(ps[bu][:, 0:n], wt[:, c, 256 + sub * 128:256 + (sub + 1) * 128], hn[:, c, bo_:bo_ + n], c == 0, c == DC - 1,
                               [wk, (hnk, bo_)], [PSK(bu)])
                        j = fi % 2
                        act(sil[j][:, 0:n], ps[bg][:, 0:n], AF.Silu, [PSK(bg)], [silk[j]])
                        tt("dve", actT[:, fi, bo_:bo_ + n], sil[j][:, 0:n], ps[bu][:, 0:n], ALU.mult, [silk[j], PSK(bu)], [(actk, fi, bo_)])
            for jd in range(DC):
                wt, wk = wd[di % 2]
                di += 1
                dma("pool", wt[:, :, :], wdn2.rearrange("(fc p) d -> p fc d", p=128)[:, :, jd * 128:(jd + 1) * 128], [], [wk])
                for (ti, c0, n, blks, bo_) in grp:
                    bo = bank()
                    for fc in range(FC):
                        mm(ps[bo][:, 0:n], wt[:, fc, :], actT[:, fc, bo_:bo_ + n], fc == 0, fc == FC - 1, [wk, (actk, fc, bo_)], [PSK(bo)])
                    cp("act", oT[:, jd, bo_:bo_ + n], ps[bo][:, 0:n], [PSK(bo)], [(oTk, bo_)])
            for (ti, c0, n, blks, bo_) in grp:
                postnorm_residual(l, gpost, ti, c0, n, oT[:, :, bo_:bo_ + n], (oTk, bo_), sqb, sqk, rstd[:, bo_:bo_ + n], (rstdk, bo_), sil, silk, 0.5)
            if after_group is not None:
                after_group(grp)
        S.barrier()
        A.release(mk)

    def logsig_pos(x_ap, xk, out_ap, outk, n_rows):
        act(out_ap, x_ap, AF.Exp, [xk], [outk], scale=-1.0)
        act(out_ap, out_ap, AF.Ln, [outk], [outk], bias=oneb[0:n_rows, 0:1])

    def mixer(l):
        win = w_in[l]
        mk0 = A.mark()
        hn, hnkb = A.alloc([128, DC, 528], BF16)
        hnk = (hnkb, 0)
        sq0, sqk0 = A.alloc([128, 512], F32)
        sq1, sqk1 = A.alloc([128, 512], F32)
        rstd, rstdkb = A.alloc([128, 528], F32)
        rstdk = (rstdkb, 0)
        sqb, sqk = [sq0, sq1], [sqk0, sqk1]
        wb = [A.alloc([128, DC, 512], BF16) for _ in range(3)]
        wsm, wsmk = A.alloc([128, DC, 32], BF16)
        vtok, vtokk = A.alloc([128, 4, 512], BF16)
        Ltok, Ltokk = A.alloc([128, 4, 256], F32)
        E1, E1k = A.alloc([128, 2, 128], F32)
        E2, E2k = A.alloc([128, 2, 128], F32)
        gaT, gaTk = A.alloc([128, 512], BF16)
        xf, xfk = A.alloc([128, 8], F32)
        wcnt = [0]

        def nextw():
            w = wb[wcnt[0] % 3]
            wcnt[0] += 1
            return w

        load_w(wsm, wsmk, win, C_FF, 8, dcol=0)
        load_w(wsm, wsmk, win, C_GA, 16, dcol=8)
        dma("pool", wau[:, :], w_au[l], [], ["wau"])
        b_alpha = sm(l, 52, 256)
        b_f = sm(l, 308, 8)

        def gla_common(ti, c0, n, blks):
            bk = bank()
            for c in range(DC):
                mm(ps[bk][0:16, 0:n], wsm[:, c, 8:24], hn[:, c, 0:n], c == 0, c == DC - 1, [wsmk, hnk], [PSK(bk)])
            cp("act", gaT[0:16, 0:n], ps[bk][0:16, 0:n], [PSK(bk)], [gaTk])
            wt, wk = nextw()
            load_w(wt, wk, win, C_GV, 512)
            for bi, blk in enumerate(blks):
                bc, nb = bcol(blk)
                lo = bc - c0
                bk = bank()
                for c in range(DC):
                    mm(ps[bk][0:nb, 0:512], hn[:, c, lo:lo + nb], wt[:, c, :], c == 0, c == DC - 1, [wk, hnk], [PSK(bk)])
                cp("act", vtok[0:nb, bi, :], ps[bk][0:nb, 0:512], [PSK(bk)], [(vtokk, bi)])
                bk = bank()
                mm(ps[bk][0:nb, 0:256], gaT[0:16, lo:lo + nb], wau[:, :], True, True, [gaTk, "wau"], [PSK(bk)])
                tt("dve", Ltok[0:nb, bi, :], ps[bk][0:nb, 0:256], b_alpha[0:nb, :], ALU.add, [PSK(bk), "smalls"], [(Ltokk, bi)])
                logsig_pos(Ltok[0:nb, bi, :], (Ltokk, bi), Ltok[0:nb, bi, :], (Ltokk, bi), nb)

        def cb_exp(bi, nb, need_e2):
            for pr in range(2):
                bk = bank()
                mm(ps[bk][:, 0:nb], Ltok[0:nb, bi, pr * 128:(pr + 1) * 128], trin16[0:nb, 0:nb], True, True, [(Ltokk, bi), "constS"], [PSK(bk)])
                act(E1[:, pr, 0:nb], ps[bk][:, 0:nb], AF.Exp, [PSK(bk)], [(E1k, pr)])
                if need_e2:
                    act(E2[:, pr, 0:nb], ps[bk][:, 0:nb], AF.Exp, [PSK(bk)], [(E2k, pr)], scale=-1.0)

        rg = [[0, 1, 2, 3], [4, 5, 6, 7]]
        gathered = set()

        def gather_chunk(c_):
            if c_ in gathered or c_ < 0 or c_ >= NCH:
                return
            gathered.add(c_)
            for (P_, G_, nm) in ((PK[l][c_], GK[l][c_], "K"), (PV[l][c_], GV[l][c_], "V"), (PU[l][c_], GU[l][c_], "U")):
                S.op("pool", lambda e, P_=P_, G_=G_: e.collective_compute("AllGather", ALU.bypass, replica_groups=rg, ins=[P_.opt()], outs=[G_.opt()]),
                     [("P" + nm, l, c_)], [("G" + nm, l, c_)], dma=True, cc=True)

        mk1 = A.mark()
        ktile, ktilek = A.alloc([128, 8, 512], BF16)
        vt, vtk = A.alloc([128, 512], BF16)
        Lf, Lfk = A.alloc([128, 8], F32)
        ktk, ktkk = A.alloc([128, 256], BF16)
        Er, Erk = A.alloc([128, 256], F32)
        usb, usbk = A.alloc([128, 258], F32)
        memset("dve", runS[:, :], 0.0, ["runS"])

        def p1_tile(ti, c0, n, blks):
            prenorm(l, 16, ti, c0, n, hn, hnk, sqb, sqk, rstd, rstdk)
            wt, wk = nextw()
            load_w(wt, wk, win, C_FK, 512)
            for h in range(8):
                bk = bank()
                for c in range(DC):
                    mm(ps[bk][0:64, 0:n], wt[:, c, h * 64:(h + 1) * 64], hn[:, c, 0:n], c == 0, c == DC - 1, [wk, hnk], [PSK(bk)])
                cp("act" if h % 2 == 0 else "dve", ktile[0:64, h, 0:n], ps[bk][0:64, 0:n], [PSK(bk)], [ktilek])
            if ti == 0:
                cp("dve", kmeta[0:64, :, 0:16], ktile[0:64, :, 0:16], [ktilek], ["kmeta"])
            else:
                dma("sp", PK[l][ti - 1].rearrange("(h d) t -> d h t", h=8)[:, :, 0:n], ktile[0:64, :, 0:n], [ktilek], [("PK", l, ti - 1)])
            wt, wk = nextw()
            load_w(wt, wk, win, C_FV, 512)
            for bi, blk in enumerate(blks):
                bc, nb = bcol(blk)
                lo = bc - c0
                bk = bank()
                for c in range(DC):
                    mm(ps[bk][0:nb, 0:512], hn[:, c, lo:lo + nb], wt[:, c, :], c == 0, c == DC - 1, [wk, hnk], [PSK(bk)])
                if blk == 0:
                    cp("act", vmeta[0:16, :, 0:64], ps[bk][0:16, 0:512].rearrange("p (h d) -> p h d", h=8), [PSK(bk)], ["vmeta"])
                else:
                    cp("act", vt[0:nb, :], ps[bk][0:nb, 0:512], [PSK(bk)], [vtk])
                    dma("sp", PV[l][(blk - 1) // CH].rearrange("(h s) (b d) -> s h b d", h=8, b=CH)[:, :, (blk - 1) % CH, :],
                        vt[0:nb, :].rearrange("p (h d) -> p h d", h=8), [vtk], [("PV", l, (blk - 1) // CH)])
                bk = bank()
                for c in range(DC):
                    mm(ps[bk][0:nb, 0:8], hn[:, c, lo:lo + nb], wsm[:, c, 0:8], c == 0, c == DC - 1, [wsmk, hnk], [PSK(bk)])
                tt("dve", xf[0:nb, :], ps[bk][0:nb, 0:8], b_f[0:nb, :], ALU.add, [PSK(bk), "smalls"], [xfk])
                logsig_pos(xf[0:nb, :], xfk, Lf[0:nb, :], Lfk, nb)
                bk = bank()
                mm(ps[bk][0:nb, 0:8], trin1[0:nb, 0:nb], Lf[0:nb, :], True, True, [Lfk, "constS"], [PSK(bk)])
                mm(ps[bk][:, 8:16], nones_t[0:nb, :], Lf[0:nb, :], True, True, [Lfk, "constS3"], [PSK(bk)])
                if blk == 0:
                    cp("dve", lcS[0:nb, 0, :], ps[bk][0:nb, 0:8], [PSK(bk)], [("lcS", 0)])
                    cp("dve", m0S[:, :], ps[bk][:, 8:16], [PSK(bk)], ["m0S"])
                else:
                    tt("dve", lcS[0:nb, blk, :], ps[bk][0:nb, 0:8], runS[0:nb, :], ALU.add, [PSK(bk), "runS"], [("lcS", blk)])
                    tt("dve", runS[:, :], runS[:, :], ps[bk][:, 8:16], ALU.add, [PSK(bk), "runS"], ["runS"])
            gla_common(ti, c0, n, blks)
            wt, wk = nextw()
            load_w(wt, wk, win, C_GK, 256)
            for bi, blk in enumerate(blks):
                bc, nb = bcol(blk)
                lo = bc - c0
                cb_exp(bi, nb, False)
                bk = bank()
                for c in range(DC):
                    mm(ps[bk][0:nb, 0:256], hn[:, c, lo:lo + nb], wt[:, c, 0:256], c == 0, c == DC - 1, [wk, hnk], [PSK(bk)])
                bk2 = bank()
                mm(ps[bk2][0:nb, 0:256], trirev16[0:nb, 0:nb], Ltok[0:nb, bi, :], True, True, [(Ltokk, bi), "constS"], [PSK(bk2)])
                act(Er[0:nb, :], ps[bk2][0:nb, 0:256], AF.Exp, [PSK(bk2)], [Erk])
                tt("dve", ktk[0:nb, :], ps[bk][0:nb, 0:256], Er[0:nb, :], ALU.mult, [PSK(bk), Erk], [ktkk])
                bk = bank()
                for hh in range(4):
                    pr, hb = hh // 2, (hh % 2) * 64
                    mm(ps[bk][hb:hb + 64, pr * 128:(pr + 1) * 128], ktk[0:nb, hh * 64:(hh + 1) * 64], vtok[0:nb, bi, hh * 128:(hh + 1) * 128],
                       True, True, [ktkk, (vtokk, bi)], [PSK(bk)], tile_position=(0, hb))
                dst, dstk = (U0, "U0") if blk == 0 else (usb, usbk)
                cp("act", dst[:, 0:256], ps[bk][:, 0:256], [PSK(bk)], [dstk])
                cp("dve", dst[:, 256:258], E1[:, :, nb - 1], [(E1k, 0), (E1k, 1)], [dstk])
                if blk != 0:
                    dma("sp", PU[l][(blk - 1) // CH][:, ((blk - 1) % CH) * 258:((blk - 1) % CH + 1) * 258], usb[:, :], [usbk], [("PU", l, (blk - 1) // CH)])

        def after_group(grp):
            for t_ in sorted(grp, key=lambda t: t[0]):
                p1_tile(t_[0], t_[1], t_[2], t_[3])
                if t_[0] >= 1:
                    gather_chunk(t_[0] - 1)

        if stages is None or f"ffn0_{l}" in stages:
            ffn(l, 0, shared=(hn, hnkb, (sq0, sq1), (sqk0, sqk1), rstd, rstdkb, wb), after_group=after_group)
        else:
            for t_ in tiles:
                after_group([t_])
        for blk in range(1, NB + 1):
            tt("dve", pbS[:, (blk - 1) * 8:blk * 8], runS[:, :], lcS[:, blk, :], ALU.subtract, ["runS", ("lcS", blk)], ["pbS"])
        cp("dve", pbS[:, NB * 8:NB * 8 + 8], runS[:, :], ["runS"], ["pbS"])
        dma("sp", PB[l][:, :], pbS[:, :], ["pbS"], [("PB", l)])
        if mix_stop == "p1":
            S.barrier()
            A.release(mk0)
            return
        for c_ in range(NCH):
            gather_chunk(c_)
        S.op("pool", lambda e: e.collective_compute("AllGather", ALU.bypass, replica_groups=rg, ins=[PB[l].opt()], outs=[GB[l].opt()]),
             [("PB", l)], [("GB", l)], dma=True, cc=True)
        S.barrier()
        A.release(mk1)
        if mix_stop == "p2":
            S.barrier()
            A.release(mk0)
            return
        for i in range(3):
            dma("sp", gbT[:, i, :], GB[l][i * 128:(i + 1) * 128, :], [("GB", l)], ["gbT"])
        Tb = lambda i: gbT[:, i, NB * 8:NB * 8 + 8]
        e_ = lambda i: cvec[:, i:i + 1]
        vis = lambda i: cvec[:, 3 + i:4 + i]
        ts("dve", Gt[:, 2, :], Tb(2), e_(2), ALU.mult, ["gbT", "cvec"], ["Gt"])
        stt("dve", Gt[:, 0, :], Tb(1), e_(1), Gt[:, 2, :], ALU.mult, ALU.add, ["gbT", "cvec", "Gt"], ["Gt"])
        stt("dve", Gt[:, 3, :], Tb(0), e_(0), Gt[:, 0, :], ALU.mult, ALU.add, ["gbT", "cvec", "Gt"], ["Gt"])
        ts("dve", Gt[:, 1, :], Gt[:, 2, :], vis(1), ALU.add, ["Gt", "cvec"], ["Gt"])
        ts("dve", Gt[:, 0, :], Gt[:, 0, :], vis(0), ALU.add, ["Gt", "cvec"], ["Gt"])
        ts("dve", Gt[:, 2, :], Gt[:, 2, :], 0.0, ALU.mult, ["Gt"], ["Gt"], s2=vis(2), op1=ALU.add)
        for i in range(3):
            tt("dve", gbT[:, i, 0:NB * 8].rearrange("p (b h) -> p b h", h=8), gbT[:, i, 0:NB * 8].rearrange("p (b h) -> p b h", h=8),
               Gt[:, i, :].unsqueeze(1).to_broadcast([128, NB, 8]), ALU.add, ["gbT", "Gt"], ["gbT"])
        ts("dve", bown[:, :, :], lcS[:, :, :], -1.0, ALU.mult, [("lcS", b) for b in range(NB + 1)], ["bown"])
        tt("dve", bmeta[:, 0, :], m0S[:, :], lcS[:, 0, :], ALU.subtract, ["m0S", ("lcS", 0)], ["bmeta"])
        tt("dve", bmeta[:, 0, :], bmeta[:, 0, :], Gt[:, 3, :], ALU.add, ["bmeta", "Gt"], ["bmeta"])
        ts("dve", bmeta[:, 1, :], lcS[:, 0, :], -1.0, ALU.mult, [("lcS", 0)], ["bmeta"])
        mk2 = A.mark()
        ubig = [[A.alloc([128, CH * 258], F32) for _ in range(NCH)] for _ in range(3)]
        for i in range(3):
            for c_ in range(NCH):
                dma("sp", ubig[i][c_][0][:, :], GU[l][c_][i * 128:(i + 1) * 128, :], [("GU", l, c_)], [ubig[i][c_][1]])
        ome, omek = A.alloc([128, 3], F32)
        ts("dve", ome[:, :], cvec[:, 0:3], -1.0, ALU.mult, ["cvec"], [omek], s2=1.0, op1=ALU.add)
        dpa = [A.alloc([128, NB, 2], F32) for _ in range(3)]
        Fs = [A.alloc([128, 2, 128], F32) for _ in range(3)]
        Da = [A.alloc([128, 2], F32) for _ in range(3)]
        for i in range(3):
            for c_ in range(NCH):
                ubt, uk = ubig[i][c_]
                ts("dve", dpa[i][0][:, c_ * CH:(c_ + 1) * CH, :], ubt[:, :].rearrange("p (c w) -> p c w", c=CH)[:, :, 256:258], e_(i), ALU.mult,
                   [uk, "cvec"], [dpa[i][1]], s2=ome[:, i:i + 1], op1=ALU.add)
            memset("pool", Fs[i][0][:, :, :], 0.0, [Fs[i][1]])
            memset("pool", Da[i][0][:, :], 1.0, [Da[i][1]])
        for ck in range(NB):
            for i in range(3):
                tt("dve", Fs[i][0][:, :, :], Fs[i][0][:, :, :], dpa[i][0][:, ck, :].unsqueeze(2).to_broadcast([128, 2, 128]), ALU.mult,
                   [Fs[i][1], dpa[i][1]], [Fs[i][1]])
            for i in range(3):
                ubt, uk = ubig[i][ck // CH]
                u = ubt[:, (ck % CH) * 258:(ck % CH + 1) * 258]
                stt("dve", Fs[i][0][:, :, :], u[:, 0:256].rearrange("p (a v) -> p a v", a=2), e_(i), Fs[i][0][:, :, :], ALU.mult, ALU.add,
                    [uk, "cvec", Fs[i][1]], [Fs[i][1]])
            for i in range(3):
                tt("pool", Da[i][0][:, :], Da[i][0][:, :], dpa[i][0][:, ck, :], ALU.mult, [Da[i][1], dpa[i][1]], [Da[i][1]])
        cp("dve", Sst[:, :, :], U0[:, 0:256].rearrange("p (a v) -> p a v", a=2), ["U0"], ["Sst"])
        for i in range(3):
            tt("dve", Sst[:, :, :], Sst[:, :, :], Da[i][0][:, 0:2].unsqueeze(2).to_broadcast([128, 2, 128]), ALU.mult, ["Sst", Da[i][1]], ["Sst"])
            tt("dve", Sst[:, :, :], Sst[:, :, :], Fs[i][0][:, :, :], ALU.add, ["Sst", Fs[i][1]], ["Sst"])
        S.barrier()
        A.release(mk2)
        if mix_stop == "prep":
            S.barrier()
            A.release(mk0)
            return
        mk3 = A.mark()
        QT, QTk = A.alloc([128, 8, 512], BF16)
        OT, OTk = A.alloc([128, 4, 512], BF16)
        GT, GTk = A.alloc([128, 4, 512], BF16)
        sgr, sgrk = A.alloc([128, 4, 512], BF16)
        qg, qgk = A.alloc([128, 2, 512], BF16)
        kg, kgk = A.alloc([128, 2, 512], BF16)
        yT, yTk = A.alloc([128, DC, 512], BF16)
        oT, oTk = A.alloc([128, DC, 512], F32)
        memset("pool", QT[:, :, :], 0.0, [QTk])
        c3, c3k = A.alloc([128, 8], F32)
        r3, r3k = A.alloc([128, 8], F32)
        mk4 = A.mark()
        for (ti, c0, n, blks) in tiles:
            prenorm(l, 16, ti, c0, n, hn, hnk, sqb, sqk, rstd, rstdk)
            wt, wk = nextw()
            load_w(wt, wk, win, C_FQ, 512)
            for h in range(8):
                bk = bank()
                for c in range(DC):
                    mm(ps[bk][0:64, 0:n], wt[:, c, h * 64:(h + 1) * 64], hn[:, c, 0:n], c == 0, c == DC - 1, [wk, hnk], [PSK(bk)])
                act(QT[0:64, h, 0:n], ps[bk][0:64, 0:n], AF.Copy, [PSK(bk)], [QTk], scale=0.125)
            for bi, blk in enumerate(blks):
                bc, nb = bcol(blk)
                lo = bc - c0
                cp("dve", split[0:nb, :, 64], lcS[0:nb, blk, :], [("lcS", blk)], ["split"])
                tt("dve", r3[0:nb, :], lcS[0:nb, blk, :], split[0:nb, :, 64], ALU.subtract, [("lcS", blk), "split"], [r3k])
                cp("dve", split[0:nb, :, 65], r3[0:nb, :], [r3k], ["split"])
                tt("dve", c3[0:nb, :], r3[0:nb, :], split[0:nb, :, 65], ALU.subtract, [r3k, "split"], [c3k])
                cp("dve", split[0:nb, :, 66], c3[0:nb, :], [c3k], ["split"])
                for h in range(8):
                    bk = bank()
                    mm(ps[bk][0:67, 0:nb], split[0:nb, h, :], identb[0:nb, 0:nb], True, True, ["split", "identb"], [PSK(bk)])
                    cp("act" if h % 2 == 0 else "dve", QT[64:67, h, lo:lo + nb], ps[bk][64:67, 0:nb], [PSK(bk)], [QTk])
            if mix_stop == "q":
                continue
            gla_common(ti, c0, n, blks)
            wt, wk = nextw()
            load_w(wt, wk, win, C_GQ, 512)
            for pr in range(2):
                bk = bank()
                for c in range(DC):
                    mm(ps[bk][:, 0:n], wt[:, c, pr * 128:(pr + 1) * 128], hn[:, c, 0:n], c == 0, c == DC - 1, [wk, hnk], [PSK(bk)])
                cp("act", qg[:, pr, 0:n], ps[bk][:, 0:n], [PSK(bk)], [qgk])
                bk = bank()
                for c in range(DC):
                    mm(ps[bk][:, 0:n], wt[:, c, 256 + pr * 128:256 + (pr + 1) * 128], hn[:, c, 0:n], c == 0, c == DC - 1, [wk, hnk], [PSK(bk)])
                cp("dve", kg[:, pr, 0:n], ps[bk][:, 0:n], [PSK(bk)], [kgk])
            wt, wk = nextw()
            load_w(wt, wk, win, C_GR, 512)
            for hh in range(4):
                bk = bank()
                for c in range(DC):
                    mm(ps[bk][:, 0:n], wt[:, c, hh * 128:(hh + 1) * 128], hn[:, c, 0:n], c == 0, c == DC - 1, [wk, hnk], [PSK(bk)])
                act(sgr[:, hh, 0:n], ps[bk][:, 0:n], AF.Silu, [PSK(bk)], [sgrk])
            if mix_stop == "glaproj":
                continue
            mkf = A.mark()
            KTb = [A.alloc([128, SEG], BF16) for _ in range(2)]
            Vb = [A.alloc([128, NB, 128], BF16) for _ in range(2)]
            PT = [A.alloc([128, 512], BF16) for _ in range(5)]
            rec, reck = A.alloc([128, 512], F32)
            for (t_, k_) in KTb:
                memset("pool", t_[64:67, :], 1.0, [k_])
            for (t_, k_) in Vb:
                memset("pool", t_[:, :, 64:128], 1.0, [k_])
            LA = 3
            groups = []
            units = []
            for h in range(8):
                bx = 6 + (h % 2)
                hu = []
                if ti == 0:
                    hu.append(dict(kt=kmeta[0:67, h, 0:16], v=vmeta[0:16, h, :], nk=16, bias=bmeta[0:16, 1, h:h + 1], rk=["kmeta", "vmeta"],
                                   mask=maskA[0:16, 0, 0:16], q0=0, grp=None))
                else:
                    hu.append(dict(kt=kmeta[0:67, h, 0:16], v=vmeta[0:16, h, :], nk=16, bias=bmeta[0:16, 0, h:h + 1], rk=["kmeta", "vmeta"],
                                   mask=None, q0=0, grp=None))
                    for src in (3, 0, 1, 2):
                        nblk_src = NB if src < 3 else blks[-1]
                        gi = len(groups)
                        groups.append((h, src, nblk_src))
                        for kb in range(nblk_src):
                            u = dict(nk=128, q0=0, mask=None, grp=gi, kb=kb)
                            if src < 3:
                                u["bias"] = gbT[:, src, kb * 8 + h:kb * 8 + h + 1]
                            else:
                                u["bias"] = bown[:, kb + 1, h:h + 1]
                                r = (kb + 1) - blks[0]
                                if r >= 0:
                                    u["q0"] = r * 128
                                    u["mask"] = maskA[:, r, r * 128:n]
                            hu.append(u)
                for i_, u in enumerate(hu):
                    u["h"], u["bx"], u["first"], u["last"] = h, bx, i_ == 0, i_ == len(hu) - 1
                units += hu
            gbuf = {}

            def load_group(gi):
                if gi >= len(groups) or gi in gbuf:
                    return
                h, src, nblk_src = groups[gi]
                kt_, ktk_ = KTb[gi % 2]
                v_, vk_ = Vb[gi % 2]
                for c_ in range(nblk_src // CH):
                    if src < 3:
                        dma("sp", kt_[0:64, c_ * CH * 128:(c_ + 1) * CH * 128], GK[l][c_][src * 512 + h * 64:src * 512 + (h + 1) * 64, :], [("GK", l, c_)], [ktk_])
                        dma("pool", v_[:, c_ * CH:(c_ + 1) * CH, 0:64],
                            GV[l][c_][src * 1024 + h * 128:src * 1024 + (h + 1) * 128, :].rearrange("s (b d) -> s b d", b=CH), [("GV", l, c_)], [vk_])
                    else:
                        dma("sp", kt_[0:64, c_ * CH * 128:(c_ + 1) * CH * 128], PK[l][c_][h * 64:(h + 1) * 64, :], [("PK", l, c_)], [ktk_])
                        dma("pool", v_[:, c_ * CH:(c_ + 1) * CH, 0:64],
                            PV[l][c_][h * 128:(h + 1) * 128, :].rearrange("s (b d) -> s b d", b=CH), [("PV", l, c_)], [vk_])
                gbuf[gi] = (kt_, ktk_, v_, vk_)

            def stage_a(ui, u):
                if u["grp"] is not None:
                    load_group(u["grp"])
                    kt_, ktk_, v_, vk_ = gbuf[u["grp"]]
                    kb = u["kb"]
                    u["kt"], u["v"], u["rk"] = kt_[0:67, kb * 128:(kb + 1) * 128], v_[:, kb, :], [ktk_, vk_]
                nk, q0, h = u["nk"], u["q0"], u["h"]
                bs = ui % 6
                mm(ps[bs][0:nk, q0:n], u["kt"], QT[0:67, h, q0:n], True, True, u["rk"] + [QTk], [PSK(bs)])
                p_, pk_ = PT[ui % 5]
                act(p_[0:nk, q0:n], ps[bs][0:nk, q0:n], AF.Exp, [PSK(bs), "gbT", "bown", "bmeta"], [pk_], bias=u["bias"])
                if u["mask"] is not None:
                    tt("dve", p_[0:nk, q0:n], p_[0:nk, q0:n], u["mask"], ALU.mult, [pk_, "maskA"], [pk_])

            def stage_b(ui, u):
                nk, q0, h, bx = u["nk"], u["q0"], u["h"], u["bx"]
                p_, pk_ = PT[ui % 5]
                mm(ps[bx][:, q0:n], u["v"], p_[0:nk, q0:n], u["first"], u["last"], u["rk"] + [pk_], [PSK(bx)])
                if u["grp"] is not None and u["kb"] == 0:
                    load_group(u["grp"] + 1)
                if u["last"]:
                    recip(rec[0:64, 0:n], ps[bx][64:128, 0:n], [PSK(bx)], [reck])
                    hb = (h % 2) * 64
                    tt("dve", OT[hb:hb + 64, h // 2, 0:n], ps[bx][0:64, 0:n], rec[0:64, 0:n], ALU.mult, [PSK(bx), reck], [OTk])

            for i_ in range(len(units) + LA):
                if i_ < len(units):
                    stage_a(i_, units[i_])
                if i_ - LA >= 0:
                    stage_b(i_ - LA, units[i_ - LA])
            S.barrier()
            A.release(mkf)
            if mix_stop == "fox":
                continue
            mkg = A.mark()
            qp, qpk = A.alloc([128, 4, 128], BF16)
            kp, kpk = A.alloc([128, 4, 128], BF16)
            Sbh, _ = A.alloc([128, 4, 128], BF16)
            attm, attmk = A.alloc([128, 4, 128], BF16)
            sqg, sqgk = A.alloc([128, 512], F32)
            rsg, rsgk = A.alloc([128, 512], F32)
            t1, t1k = A.alloc([128, 512], F32)
            uc = [A.alloc([128, 258], F32) for _ in range(2)]
            for bi, blk in enumerate(blks):
                bc, nb = bcol(blk)
                lo = bc - c0
                cb_exp(bi, nb, True)
                for hh in range(4):
                    pr, hb = hh // 2, (hh % 2) * 64
                    stt("dve", qp[0:64, hh, 0:nb], qg[hb:hb + 64, pr, lo:lo + nb], 0.125, E1[hb:hb + 64, pr, 0:nb], ALU.mult, ALU.mult, [qgk, (E1k, pr)], [qpk])
                    tt("dve", kp[0:64, hh, 0:nb], kg[hb:hb + 64, pr, lo:lo + nb], E2[hb:hb + 64, pr, 0:nb], ALU.mult, [kgk, (E2k, pr)], [kpk])
                ba = bank()
                for hh in range(4):
                    mm(ps[ba][0:nb, hh * 128:hh * 128 + nb], kp[0:64, hh, 0:nb], qp[0:64, hh, 0:nb], True, True, [kpk, qpk], [PSK(ba)])
                tt("dve", attm[0:nb, :, 0:nb], ps[ba][0:nb, :].rearrange("p (a t) -> p a t", a=4)[:, :, 0:nb],
                   trif[0:nb, 0:nb].unsqueeze(1).to_broadcast([nb, 4, nb]), ALU.mult, [PSK(ba), "constS"], [attmk])
                if blk == 0:
                    memset("dve", Sbh[:, :, :], 0.0, ["Sbh"])
                else:
                    for hh in range(4):
                        pr, hb = hh // 2, (hh % 2) * 64
                        cp("dve", Sbh[0:64, hh, :], Sst[hb:hb + 64, pr, :], ["Sst"], ["Sbh"])
                bo = bank()
                for hh in range(4):
                    mm(ps[bo][:, hh * 128:hh * 128 + nb], Sbh[0:64, hh, :], qp[0:64, hh, 0:nb], True, False, ["Sbh", qpk], [PSK(bo)])
                    mm(ps[bo][:, hh * 128:hh * 128 + nb], vtok[0:nb, bi, hh * 128:(hh + 1) * 128], attm[0:nb, hh, 0:nb], False, True,
                       [(vtokk, bi), attmk], [PSK(bo)])
                if mix_stop == "gla2":
                    continue
                if blk != 0:
                    u, uk = uc[bi % 2]
                    dma("sp", u[:, :], PU[l][(blk - 1) // CH][:, ((blk - 1) % CH) * 258:((blk - 1) % CH + 1) * 258], [("PU", l, (blk - 1) // CH)], [uk])
                    for pr in range(2):
                        stt("dve", Sst[:, pr, :], Sst[:, pr, :], u[:, 256 + pr:257 + pr], u[:, pr * 128:(pr + 1) * 128], ALU.mult, ALU.add,
                            ["Sst", uk], ["Sst"])
                if mix_stop == "gla3":
                    continue
                o4 = ps[bo][:, :].rearrange("p (a t) -> p a t", a=4)[:, :, 0:nb]
                act(sqg[:, 0:4 * nb].rearrange("p (a t) -> p a t", a=4), o4, AF.Square, [PSK(bo)], [sqgk])
                bs = bank()
                mm(ps[bs][:, 0:4 * nb], ones_t[:, :], sqg[:, 0:4 * nb], True, True, [sqgk, "constS2"], [PSK(bs)])
                act(rsg[:, 0:4 * nb], ps[bs][:, 0:4 * nb], AF.Sqrt, [PSK(bs)], [rsgk], bias=epsb[:, 0:1], scale=1.0 / 128)
                recip(rsg[:, 0:4 * nb], rsg[:, 0:4 * nb], [rsgk], [rsgk])
                tt("dve", t1[:, 0:4 * nb].rearrange("p (a t) -> p a t", a=4), o4, rsg[:, 0:4 * nb].rearrange("p (a t) -> p a t", a=4), ALU.mult,
                   [PSK(bo), rsgk], [t1k])
                for hh in range(4):
                    stt("dve", GT[:, hh, lo:lo + nb], t1[:, hh * nb:(hh + 1) * nb], sm(l, 48 + hh, 1), sgr[:, hh, lo:lo + nb], ALU.mult, ALU.mult,
                        [t1k, "smalls", sgrk], [GTk])
            S.barrier()
            A.release(mkg)
            if mix_stop in ("gla", "gla2", "gla3"):
                continue
            mkm = A.mark()
            sga, sgak = A.alloc([128, 512], F32)
            sgbt, sgbk = A.alloc([128, 512], F32)
            t2, t2k = A.alloc([128, 512], F32)
            tm0, tmk0 = A.alloc([128, 512], F32)
            tm1, tmk1 = A.alloc([128, 512], F32)
            for cg in range(2):
                wa, wak = nextw()
                load_w(wa, wak, win, C_MA + cg * 512, 512)
                wbb, wbk = nextw()
                load_w(wbb, wbk, win, C_MB + cg * 512, 512)
                wo, wok = nextw()
                dma("pool", wo[:, 0:4, :], w_fo[l].rearrange("(c p) f -> p c f", p=128)[:, :, cg * 512:(cg + 1) * 512], [], [wok])
                dma("pool", wo[:, 4:8, :], w_go[l].rearrange("(c p) f -> p c f", p=128)[:, :, cg * 512:(cg + 1) * 512], [], [wok])
                for cc in range(4):
                    c = cg * 4 + cc
                    b1, b2, b3, b4 = bank(), bank(), bank(), bank()
                    for k in range(DC):
                        mm(ps[b1][:, 0:n], wa[:, k, cc * 128:(cc + 1) * 128], hn[:, k, 0:n], k == 0, k == DC - 1, [wak, hnk], [PSK(b1)])
                    for k in range(DC):
                        mm(ps[b2][:, 0:n], wbb[:, k, cc * 128:(cc + 1) * 128], hn[:, k, 0:n], k == 0, k == DC - 1, [wbk, hnk], [PSK(b2)])
                    for k in range(4):
                        mm(ps[b3][:, 0:n], wo[:, k, cc * 128:(cc + 1) * 128], OT[:, k, 0:n], k == 0, k == 3, [wok, OTk], [PSK(b3)])
                    for k in range(4):
                        mm(ps[b4][:, 0:n], wo[:, 4 + k, cc * 128:(cc + 1) * 128], GT[:, k, 0:n], k == 0, k == 3, [wok, GTk], [PSK(b4)])
                    act(sga[:, 0:n], ps[b1][:, 0:n], AF.Sigmoid, [PSK(b1)], [sgak])
                    act(sgbt[:, 0:n], ps[b2][:, 0:n], AF.Sigmoid, [PSK(b2)], [sgbk])
                    tt("dve", t2[:, 0:n], sga[:, 0:n], ps[b3][:, 0:n], ALU.mult, [sgak, PSK(b3)], [t2k])
                    tt("dve", sgbt[:, 0:n], sgbt[:, 0:n], ps[b4][:, 0:n], ALU.mult, [sgbk, PSK(b4)], [sgbk])
                    tt("pool", yT[:, c, 0:n], t2[:, 0:n], sgbt[:, 0:n], ALU.add, [t2k, sgbk], [(yTk, c)])
            for c2 in range(DC):
                if c2 % 4 == 0:
                    wo, wok = nextw()
                    load_w(wo, wok, w_out[l], c2 * 128, 512)
                bk = bank()
                for k in range(DC):
                    mm(ps[bk][:, 0:n], wo[:, k, (c2 % 4) * 128:(c2 % 4 + 1) * 128], yT[:, k, 0:n], k == 0, k == DC - 1, [wok, (yTk, k)], [PSK(bk)])
                cp("act", oT[:, c2, 0:n], ps[bk][:, 0:n], [PSK(bk)], [oTk])
            postnorm_residual(l, 24, ti, c0, n, oT, oTk, sqb, sqk, rstd, rstdk, [tm0, tm1], [tmk0, tmk1], 1.0)
            S.barrier()
            A.release(mkm)
        S.barrier()
        A.release(mk0)

    for l in range(L):
        if stages is None or f"mix_{l}" in stages:
            mixer(l)
        elif f"ffn0_{l}" in stages:
            ffn(l, 0)
        if stages is None or f"ffn1_{l}" in stages:
            ffn(l, 1)
    finals = []
    for (ti, c0, n, blks) in tiles[1:]:
        finals.append(dma("sp", outT.rearrange("(c p) t -> p c t", p=128)[:, :, c0 - 16:c0 - 16 + n], hT[:, :, c0:c0 + n], [("hT", ti)], [("outT", ti)]))
    if dumps:
        dump("hT", hT[:, :, :], [("hT", t[0]) for t in tiles], [128, DC, TL])
    S.emit(final_ops=finals + list(dump_outs.values()))
    return nc, S


def make_consts():
    s = np.arange(128)[:, None]
    t = np.arange(128)[None, :]
    le = (s <= t).astype(np.float32)
    c = np.zeros((128, 5 * 128 + 4 * 512), np.float32)
    c[:, 0:128] = np.eye(128, dtype=np.float32)
    c[:, 128:256] = le * (-1.0 / 16.0)
    c[:, 256:384] = (s > t).astype(np.float32) * (-1.0 / 16.0)
    c[:, 384:512] = -le
    c[:, 512:640] = le
    for r in range(4):
        m = np.zeros((128, 512), np.float32)
        for q in range(4):
            if q == r:
                m[:, q * 128:(q + 1) * 128] = le
            elif q > r:
                m[:, q * 128:(q + 1) * 128] = 1.0
        c[:, 640 + r * 512:640 + (r + 1) * 512] = m
    return c


def make_smalls(inp, L):
    sm = np.zeros((128, L * NSM), np.float32)
    names = ["g_pre_ffn1", "g_post_ffn1", "g_pre_mix", "g_post_mix", "g_pre_ffn2", "g_post_ffn2"]
    for l in range(L):
        o = l * NSM
        for i, nm in enumerate(names):
            sm[:, o + i * 8:o + (i + 1) * 8] = np.asarray(inp[nm], np.float32)[l].reshape(8, 128).T
        sm[:, o + 48:o + 52] = np.asarray(inp["g_gla_out"], np.float32)[l].reshape(4, 128).T
        sm[:, o + 52:o + 308] = np.asarray(inp["b_alpha"], np.float32)[l][None, :]
        sm[:, o + 308:o + 316] = np.asarray(inp["b_f"], np.float32)[l][None, :]
    return sm


def make_cvec(j):
    v = np.zeros((128, 8), np.float32)
    for i in range(3):
        v[:, i] = 1.0 if i < j else 0.0
        v[:, 3 + i] = 0.0 if i < j else -30000.0
    return v


_CACHE = {}


def run(inputs, NB, L=2, dumps=None, stages=None, mix_stop=None):
    key = (NB, L, tuple(dumps) if dumps else None)
    x = np.asarray(inputs["x"], np.float32)
    B, SEQ, _ = x.shape
    assert B == 2 and SEQ == 4 * NB * 128
    nc, S = build_program(NB, L, dumps, stages, mix_stop)
    consts = make_consts()
    smalls = make_smalls(inputs, L)
    metaT = np.ascontiguousarray(np.asarray(inputs["meta_tokens"], np.float32).T)
    shared = dict(consts=consts, smalls=smalls, metaT=metaT)
    for nm in ["w_in", "w_alpha_up", "w_fox_o", "w_gla_o", "w_out", "w_ffn1_gu", "w_ffn1_down", "w_ffn2_gu", "w_ffn2_down"]:
        shared[nm] = np.ascontiguousarray(np.asarray(inputs[nm], np.float32)[:L])
    in_maps = []
    SEG = NB * 128
    for core in range(8):
        b, j = core // 4, core % 4
        m = dict(shared)
        m["xT"] = np.ascontiguousarray(x[b, j * SEG:(j + 1) * SEG, :].T)
        m["cvec"] = make_cvec(j)
        in_maps.append(m)
    res = run_bass_kernel_spmd(nc, in_maps, core_ids=list(range(8)))
    out = np.zeros((B, SEQ, D), np.float32)
    for core in range(8):
        b, j = core // 4, core % 4
        out[b, j * SEG:(j + 1) * SEG, :] = np.asarray(res.results[core]["outT"]).T
    return out, res


def kernel(**inputs):
    out, _ = run(inputs, 16, 2)
    return out
```

```python
import numpy as np
import concourse.bass as bass
import concourse.mybir as mybir
from concourse.bass_utils import run_bass_kernel_spmd

F32 = mybir.dt.float32
BF16 = mybir.dt.bfloat16
ALU = mybir.AluOpType
AF = mybir.ActivationFunctionType

EPOCH = 24000
N_DMA_SEMS = 24
D = 1024
DC = 8
FF = 2816
FC = 22
NIN = 5144
C_FQ, C_FK, C_FV, C_FF, C_GQ, C_GK, C_GV, C_GA, C_GR, C_MA, C_MB = 0, 512, 1024, 1536, 1544, 1800, 2056, 2568, 2584, 3096, 4120
EPS = 1e-6
NSM = 316


class Sched:
    ENGS = ("pe", "act", "dve", "pool", "sp")

    def __init__(self, nc):
        self.nc = nc
        self.ops = []
        self.last_write = {}
        self.readers = {}
        self.pending_barrier = None

    def op(self, eng, fn, reads=(), writes=(), dma=False, cc=False):
        deps = set()
        for r in reads:
            lw = self.last_write.get(r)
            if lw is not None:
                deps.add(lw)
        for w in writes:
            lw = self.last_write.get(w)
            if lw is not None:
                deps.add(lw)
            for rd in self.readers.get(w, ()):
                deps.add(rd)
        oid = len(self.ops)
        if self.pending_barrier is not None:
            pb = self.pending_barrier
            if eng not in pb["done"]:
                deps |= pb["deps"]
                pb["done"].add(eng)
        self.ops.append(dict(eng=eng, fn=fn, deps=deps, dma=dma, cc=cc))
        for r in reads:
            lst = self.readers.setdefault(r, [])
            if not dma:
                lst[:] = [x for x in lst if self.ops[x]["dma"] or self.ops[x]["eng"] != eng]
            lst.append(oid)
        for w in writes:
            self.last_write[w] = oid
            self.readers[w] = []
        return oid

    def barrier(self):
        deps = set()
        seen = set()
        nd = 0
        for i in range(len(self.ops) - 1, -1, -1):
            o = self.ops[i]
            if o["dma"]:
                if nd < N_DMA_SEMS:
                    deps.add(i)
                    nd += 1
            elif o["eng"] not in seen:
                seen.add(o["eng"])
                deps.add(i)
            if len(seen) == 5 and nd >= N_DMA_SEMS:
                break
        self.pending_barrier = dict(deps=deps, done=set())

    def emit(self, final_ops=()):
        nc = self.nc
        ops = self.ops
        n = len(ops)
        needs = [False] * n
        for o in ops:
            for d in o["deps"]:
                if ops[d]["eng"] == "pe" and o["eng"] == "pe" and not ops[d]["dma"] and not o["dma"]:
                    continue
                needs[d] = True
        for d in final_ops:
            needs[d] = True
        cnt = {e: 0 for e in self.ENGS}
        for i, o in enumerate(ops):
            if not o["dma"] and needs[i]:
                cnt[o["eng"]] += 1
        sems = {e: [nc.alloc_semaphore(f"s_{e}_{i}") for i in range(cnt[e] // EPOCH + 1)] for e in self.ENGS}
        dsems = [nc.alloc_semaphore(f"s_dma_{i}") for i in range(N_DMA_SEMS)]
        dcount = [0] * N_DMA_SEMS
        dlast = [None] * N_DMA_SEMS
        token = [None] * n
        ecount = {e: 0 for e in self.ENGS}
        dk = 0
        for i, o in enumerate(ops):
            if o["cc"]:
                token[i] = (nc.alloc_semaphore(f"s_cc_{i}"), 1, 1)
            elif o["dma"]:
                k = dk % N_DMA_SEMS
                dk += 1
                if dlast[k] is not None:
                    o["deps"].add(dlast[k])
                dcount[k] += 16
                token[i] = (dsems[k], dcount[k], 16)
                dlast[k] = i
            elif needs[i]:
                c = ecount[o["eng"]]
                ecount[o["eng"]] = c + 1
                token[i] = (sems[o["eng"]][c // EPOCH], c % EPOCH + 1, 1)
        streams = {e: [] for e in self.ENGS}
        for i, o in enumerate(ops):
            streams[o["eng"]].append(i)
        self.n_waits = 0

        def run_stream(e, eng):
            waited = {}
            for i in streams[e]:
                o = ops[i]
                for d in sorted(o["deps"]):
                    od = ops[d]
                    if e == "pe" and od["eng"] == "pe" and not od["dma"] and not o["dma"]:
                        continue
                    sem, val, _ = token[d]
                    key = id(sem)
                    if waited.get(key, 0) >= val:
                        continue
                    eng.wait_ge(sem, val)
                    self.n_waits += 1
                    waited[key] = val
                ins = o["fn"](eng)
                if token[i] is not None:
                    sem, val, step = token[i]
                    ins.then_inc(sem, step)
            if e == "sp":
                for d in final_ops:
                    sem, val, _ = token[d]
                    eng.wait_ge(sem, val)

        with nc.Block() as block:
            @block.tensor
            def _(eng):
                run_stream("pe", eng)

            @block.scalar
            def _(eng):
                run_stream("act", eng)

            @block.vector
            def _(eng):
                run_stream("dve", eng)

            @block.gpsimd
            def _(eng):
                run_stream("pool", eng)

            @block.sync
            def _(eng):
                run_stream("sp", eng)


class Arena:
    def __init__(self, nc, nbytes):
        self.t = nc.alloc_sbuf_tensor("arena", [128, nbytes // 2], BF16)
        self.n = nbytes // 2
        self.top = 0
        self.uid = 0

    def alloc(self, shape, dtype):
        free = 1
        for s in shape[1:]:
            free *= s
        ne = free * (2 if dtype == F32 else 1)
        ne = (ne + 15) // 16 * 16
        assert self.top + ne <= self.n, f"arena overflow {self.top + ne} > {self.n}"
        ap = self.t[:, self.top:self.top + ne]
        self.top += ne
        if dtype == F32:
            ap = ap.bitcast(F32)
        ap = ap[:, 0:free]
        if len(shape) == 3:
            ap = ap.rearrange("p (a b) -> p a b", a=shape[1])
        self.uid += 1
        return ap, ("ar", self.uid)

    def mark(self):
        return self.top

    def release(self, m):
        self.top = m


def build_program(NB, L=2, dumps=None, stages=None, mix_stop=None):
    nc = bass.Bass("TRN2", target_bir_lowering=False)
    SEG = NB * 128
    TL = 16 + SEG
    NPB = NB * 8 + 8
    NPU = NB * 258
    S = Sched(nc)

    def dram_in(name, shape, dt=F32):
        return nc.dram_tensor(name, shape, dt, kind="ExternalInput").ap()

    xT = dram_in("xT", [D, SEG])
    metaT = dram_in("metaT", [D, 16])
    cvec_d = dram_in("cvec", [128, 8])
    consts_d = dram_in("consts", [128, 5 * 128 + 4 * 512])
    smalls_d = dram_in("smalls", [128, L * NSM])
    w_in = dram_in("w_in", [L, D, NIN])
    w_au = dram_in("w_alpha_up", [L, 16, 256])
    w_fo = dram_in("w_fox_o", [L, 512, D])
    w_go = dram_in("w_gla_o", [L, 512, D])
    w_out = dram_in("w_out", [L, D, D])
    w_gu = [dram_in("w_ffn1_gu", [L, D, 2 * FF]), dram_in("w_ffn2_gu", [L, D, 2 * FF])]
    w_dn = [dram_in("w_ffn1_down", [L, FF, D]), dram_in("w_ffn2_down", [L, FF, D])]
    outT = nc.dram_tensor("outT", [D, SEG], F32, kind="ExternalOutput").ap()
    CH = min(4, NB)
    NCH = NB // CH
    PKa = [nc.dram_tensor(f"PKa{l}", [NCH, 512, CH * 128], BF16).ap() for l in range(L)]
    PVa = [nc.dram_tensor(f"PVa{l}", [NCH, 1024, CH * 64], BF16).ap() for l in range(L)]
    PK = [[PKa[l][c] for c in range(NCH)] for l in range(L)]
    PV = [[PVa[l][c] for c in range(NCH)] for l in range(L)]
    PB = [nc.dram_tensor(f"PB{l}", [128, NPB], F32).ap() for l in range(L)]
    PU = [[nc.dram_tensor(f"PU{l}_{c}", [128, CH * 258], F32).ap() for c in range(NCH)] for l in range(L)]
    GKa = [nc.dram_tensor(f"GKa{l}", [NCH, 4 * 512, CH * 128], BF16).ap() for l in range(L)]
    GVa = [nc.dram_tensor(f"GVa{l}", [NCH, 4 * 1024, CH * 64], BF16).ap() for l in range(L)]
    GK = [[GKa[l][c] for c in range(NCH)] for l in range(L)]
    GV = [[GVa[l][c] for c in range(NCH)] for l in range(L)]
    GB = [nc.dram_tensor(f"GB{l}", [4 * 128, NPB], F32).ap() for l in range(L)]
    GU = [[nc.dram_tensor(f"GU{l}_{c}", [4 * 128, CH * 258], F32).ap() for c in range(NCH)] for l in range(L)]
    dump_outs = {}

    def sb(name, shape, dt):
        return nc.alloc_sbuf_tensor("sb_" + name, shape, dt)

    hT = sb("hT", [128, DC, TL], F32)
    constS = sb("constS", [128, 5 * 128], F32)
    maskA = sb("maskA", [128, 4, 512], BF16)
    identb = sb("identb", [128, 128], BF16)
    smalls = sb("smalls", [128, L * NSM], F32)
    cvec = sb("cvec", [128, 8], F32)
    wau = sb("wau", [16, 256], BF16)
    lcS = sb("lcS", [128, NB + 1, 8], F32)
    runS = sb("runS", [128, 8], F32)
    m0S = sb("m0S", [128, 8], F32)
    kmeta = sb("kmeta", [128, 8, 16], BF16)
    vmeta = sb("vmeta", [128, 8, 128], BF16)
    Sst = sb("Sst", [128, 2, 128], F32)
    Sbf = sb("Sbf", [128, 2, 128], BF16)
    U0 = sb("U0", [128, 258], F32)
    gbT = sb("gbT", [128, 3, NPB], F32)
    pbS = sb("pbS", [128, NPB], F32)
    bown = sb("bown", [128, NB + 1, 8], F32)
    bmeta = sb("bmeta", [128, 2, 8], F32)
    Gt = sb("Gt", [128, 4, 8], F32)
    split = sb("split", [128, 8, 67], BF16)
    ps = [nc.alloc_psum_tensor(f"ps{i}", [128, 512], F32) for i in range(8)]
    A = Arena(nc, nc.sbuf_bytes_remaining - 1280)

    ident_f = constS[:, 0:128]
    trin16 = constS[:, 128:256]
    trirev16 = constS[:, 256:384]
    trin1 = constS[:, 384:512]
    trif = constS[:, 512:640]
    PSK = lambda k: ("ps", k)

    def dma(q, out, in_, reads, writes):
        return S.op(q, lambda e: e.dma_start(out=out, in_=in_), reads, writes, dma=True)

    def mm(out, lhsT, rhs, start, stop, reads, writes, **kw):
        return S.op("pe", lambda e: e.matmul(out, lhsT=lhsT, rhs=rhs, start=start, stop=stop, **kw), reads, writes)

    def act(out, in_, func, reads, writes, bias=0.0, scale=1.0):
        return S.op("act", lambda e: e.activation(out=out, in_=in_, func=func, bias=bias, scale=scale), reads, writes)

    def tt(eng, out, in0, in1, op, reads, writes):
        return S.op(eng, lambda e: e.tensor_tensor(out=out, in0=in0, in1=in1, op=op), reads, writes)

    def ts(eng, out, in0, s1, op0, reads, writes, s2=None, op1=None):
        if op1 is None:
            return S.op(eng, lambda e: e.tensor_scalar(out=out, in0=in0, scalar1=s1, scalar2=None, op0=op0), reads, writes)
        return S.op(eng, lambda e: e.tensor_scalar(out=out, in0=in0, scalar1=s1, scalar2=s2, op0=op0, op1=op1), reads, writes)

    def stt(eng, out, in0, scalar, in1, op0, op1, reads, writes):
        return S.op(eng, lambda e: e.scalar_tensor_tensor(out=out, in0=in0, scalar=scalar, in1=in1, op0=op0, op1=op1), reads, writes)

    def cp(eng, out, in_, reads, writes):
        if eng == "act":
            return S.op("act", lambda e: e.copy(out=out, in_=in_), reads, writes)
        return S.op(eng, lambda e: e.tensor_copy(out=out, in_=in_), reads, writes)

    def memset(eng, ap, val, writes):
        return S.op(eng, lambda e: e.memset(ap, val), (), writes)

    def recip(out, in_, reads, writes):
        return S.op("dve", lambda e: e.reciprocal(out=out, in_=in_), reads, writes)

    def dump(name, ap, keys, shape):
        if dumps is None or name not in dumps:
            return
        d = nc.dram_tensor("dbg_" + name, list(shape), F32 if ap.dtype == F32 else BF16, kind="ExternalOutput").ap()
        dump_outs[name] = dma("sp", d, ap, list(keys), [("dbgd", name)])

    bank_rr = [0]

    def bank():
        b = bank_rr[0]
        bank_rr[0] = (b + 1) % 8
        return b

    mk_init = A.mark()
    maskAf, _ = A.alloc([128, 4 * 512], F32)
    dma("sp", constS[:, :], consts_d[:, 0:640], [], ["constS"])
    dma("sp", maskAf[:, :], consts_d[:, 640:640 + 2048], [], ["maskAf"])
    dma("sp", smalls[:, :], smalls_d[:, :], [], ["smalls"])
    dma("sp", cvec[:, :], cvec_d[:, :], [], ["cvec"])
    cp("dve", maskA[:, :, :], maskAf[:, :].rearrange("p (a b) -> p a b", a=4), ["maskAf"], ["maskA"])
    cp("dve", identb[:, :], ident_f, ["constS"], ["identb"])
    S.barrier()
    A.release(mk_init)
    memset("pool", split[:, :, :], 0.0, ["split"])
    memset("pool", lcS[:, :, :], 0.0, [("lcS", b_) for b_ in range(NB + 1)])
    memset("pool", kmeta[:, :, :], 1.0, ["kmeta"])
    memset("pool", vmeta[:, :, :], 1.0, ["vmeta"])
    dma("sp", hT[:, :, 0:16], metaT.rearrange("(c p) t -> p c t", p=128), [], [("hT", 0)])
    tiles = [(0, 0, 16, [0])]
    b = 1
    while b <= NB:
        nbk = min(4, NB - b + 1)
        tiles.append((len(tiles), 16 + (b - 1) * 128, nbk * 128, list(range(b, b + nbk))))
        b += nbk
    for (ti, c0, n, blks) in tiles[1:]:
        dma("sp", hT[:, :, c0:c0 + n], xT.rearrange("(c p) t -> p c t", p=128)[:, :, c0 - 16:c0 - 16 + n], [], [("hT", ti)])

    def bcol(blk):
        return (0, 16) if blk == 0 else (16 + (blk - 1) * 128, 128)

    def sm(l, off, w):
        return smalls[:, l * NSM + off:l * NSM + off + w]

    def norm_rstd(srcs, n, skeys, sqb, sqk, rstd, rstdk, nfeat):
        bk = bank()
        for i, s_ap in enumerate(srcs):
            j = i % 2
            act(sqb[j][:, 0:n], s_ap, AF.Square, list(skeys[i]), [sqk[j]])
            mm(ps[bk][:, 0:n], trif_ones, sqb[j][:, 0:n], i == 0, i == len(srcs) - 1, [sqk[j], "constS2"], [PSK(bk)])
        act(rstd[:, 0:n], ps[bk][:, 0:n], AF.Sqrt, [PSK(bk)], [rstdk], bias=epsb[:, 0:1], scale=1.0 / nfeat)
        recip(rstd[:, 0:n], rstd[:, 0:n], [rstdk], [rstdk])

    ones_t = sb("ones_f", [128, 128], F32)
    memset("dve", ones_t[:, :], 1.0, ["constS2"])
    trif_ones = ones_t[:, :]
    nones_t = sb("nones_f", [128, 128], F32)
    memset("dve", nones_t[:, :], -1.0, ["constS3"])
    epsb = sb("epsb", [128, 1], F32)
    memset("dve", epsb[:, :], EPS, ["epsb"])
    oneb = sb("oneb", [128, 1], F32)
    memset("dve", oneb[:, :], 1.0, ["oneb"])

    def load_w(dst, dstk, w2d, col0, ncols, kc=DC, dcol=0):
        return dma("pool", dst[:, 0:kc, dcol:dcol + ncols], w2d.rearrange("(c p) f -> p c f", p=128)[:, :, col0:col0 + ncols], [], [dstk])

    def prenorm(l, goff, ti, c0, n, hn, hnk, sqb, sqk, rstd, rstdk):
        norm_rstd([hT[:, c, c0:c0 + n] for c in range(DC)], n, [[("hT", ti)]] * DC, sqb, sqk, rstd, rstdk, D)
        for c in range(DC):
            stt("dve", hn[:, c, 0:n], hT[:, c, c0:c0 + n], sm(l, goff + c, 1), rstd[:, 0:n], ALU.mult, ALU.mult,
                [("hT", ti), rstdk, "smalls"], [hnk])

    def postnorm_residual(l, goff, ti, c0, n, oT, oTk, sqb, sqk, rstd, rstdk, tmp, tmpk, scale):
        norm_rstd([oT[:, c, 0:n] for c in range(DC)], n, [[oTk]] * DC, sqb, sqk, rstd, rstdk, D)
        for c in range(DC):
            j = c % 2
            stt("dve", tmp[j][:, 0:n], oT[:, c, 0:n], sm(l, goff + c, 1), rstd[:, 0:n], ALU.mult, ALU.mult, [oTk, rstdk, "smalls"], [tmpk[j]])
            stt("dve", hT[:, c, c0:c0 + n], tmp[j][:, 0:n], scale, hT[:, c, c0:c0 + n], ALU.mult, ALU.add, [tmpk[j], ("hT", ti)], [("hT", ti)])

    def ffn(l, which, shared=None, after_group=None):
        mk = A.mark()
        W = 528
        if shared is None:
            hn, hnk = A.alloc([128, DC, W], BF16)
            sq0, sqk0 = A.alloc([128, 512], F32)
            sq1, sqk1 = A.alloc([128, 512], F32)
            rstd, rstdk = A.alloc([128, W], F32)
            wb = [A.alloc([128, DC, 512], BF16) for _ in range(4)]
        else:
            hn, hnk, (sq0, sq1), (sqk0, sqk1), rstd, rstdk, wb = shared
        actT, actk = A.alloc([128, FC, W], BF16)
        sil0, silk0 = A.alloc([128, 512], F32)
        sil1, silk1 = A.alloc([128, 512], F32)
        oT, oTk = A.alloc([128, DC, W], F32)
        wd = [A.alloc([128, FC, 128], BF16) for _ in range(2)]
        sqb, sqk, sil, silk = [sq0, sq1], [sqk0, sqk1], [sil0, sil1], [silk0, silk1]
        gpre = 0 if which == 0 else 32
        gpost = 8 if which == 0 else 40
        wgu2 = w_gu[which][l]
        wdn2 = w_dn[which][l]
        wi = 0
        di = 0
        tgroups = [[tiles[1] + (0,), tiles[0] + (512,)]] + [[t + (0,)] for t in tiles[2:]]
        for grp in tgroups:
            for (ti, c0, n, blks, bo_) in grp:
                prenorm(l, gpre, ti, c0, n, hn[:, :, bo_:bo_ + n], (hnk, bo_), sqb, sqk, rstd[:, bo_:bo_ + n], (rstdk, bo_))
            for fp in range(FC // 2):
                wt, wk = wb[wi % len(wb)]
                wi += 1
                load_w(wt, wk, wgu2, fp * 256, 256, dcol=0)
                load_w(wt, wk, wgu2, FF + fp * 256, 256, dcol=256)
                for sub in range(2):
                    fi = fp * 2 + sub
                    for (ti, c0, n, blks, bo_) in grp:
                        bg, bu = bank(), bank()
                        for c in range(DC):
                            mm(ps[bg][:, 0:n], wt[:, c, sub * 128:(sub + 1) * 128], hn[:, c, bo_:bo_ + n], c == 0, c == DC - 1, [wk, (hnk, bo_)], [PSK(bg)])
                        for c in range(DC):
                            mm(ps[bu][:, 0:n], wt[:, c, 256 + sub * 128:256 + (sub + 1) * 128], hn[:, c, bo_:bo_ + n], c == 0, c == DC - 1,
                               [wk, (hnk, bo_)], [PSK(bu)])
                        j = fi % 2
                        act(sil[j][:, 0:n], ps[bg][:, 0:n], AF.Silu, [PSK(bg)], [silk[j]])
                        tt("dve", actT[:, fi, bo_:bo_ + n], sil[j][:, 0:n], ps[bu][:, 0:n], ALU.mult, [silk[j], PSK(bu)], [(actk, fi, bo_)])
            for jd in range(DC):
                wt, wk = wd[di % 2]
                di += 1
                dma("pool", wt[:, :, :], wdn2.rearrange("(fc p) d -> p fc d", p=128)[:, :, jd * 128:(jd + 1) * 128], [], [wk])
                for (ti, c0, n, blks, bo_) in grp:
                    bo = bank()
                    for fc in range(FC):
                        mm(ps[bo][:, 0:n], wt[:, fc, :], actT[:, fc, bo_:bo_ + n], fc == 0, fc == FC - 1, [wk, (actk, fc, bo_)], [PSK(bo)])
                    cp("act", oT[:, jd, bo_:bo_ + n], ps[bo][:, 0:n], [PSK(bo)], [(oTk, bo_)])
            for (ti, c0, n, blks, bo_) in grp:
                postnorm_residual(l, gpost, ti, c0, n, oT[:, :, bo_:bo_ + n], (oTk, bo_), sqb, sqk, rstd[:, bo_:bo_ + n], (rstdk, bo_), sil, silk, 0.5)
            if after_group is not None:
                after_group(grp)
        S.barrier()
        A.release(mk)

    def logsig_pos(x_ap, xk, out_ap, outk, n_rows):
        act(out_ap, x_ap, AF.Exp, [xk], [outk], scale=-1.0)
        act(out_ap, out_ap, AF.Ln, [outk], [outk], bias=oneb[0:n_rows, 0:1])

    def mixer(l):
        win = w_in[l]
        mk0 = A.mark()
        hn, hnkb = A.alloc([128, DC, 528], BF16)
        hnk = (hnkb, 0)
        sq0, sqk0 = A.alloc([128, 512], F32)
        sq1, sqk1 = A.alloc([128, 512], F32)
        rstd, rstdkb = A.alloc([128, 528], F32)
        rstdk = (rstdkb, 0)
        sqb, sqk = [sq0, sq1], [sqk0, sqk1]
        wb = [A.alloc([128, DC, 512], BF16) for _ in range(3)]
        wsm, wsmk = A.alloc([128, DC, 32], BF16)
        vtok, vtokk = A.alloc([128, 4, 512], BF16)
        Ltok, Ltokk = A.alloc([128, 4, 256], F32)
        E1, E1k = A.alloc([128, 2, 128], F32)
        E2, E2k = A.alloc([128, 2, 128], F32)
        gaT, gaTk = A.alloc([128, 512], BF16)
        xf, xfk = A.alloc([128, 8], F32)
        wcnt = [0]

        def nextw():
            w = wb[wcnt[0] % 3]
            wcnt[0] += 1
            return w

        load_w(wsm, wsmk, win, C_FF, 8, dcol=0)
        load_w(wsm, wsmk, win, C_GA, 16, dcol=8)
        dma("pool", wau[:, :], w_au[l], [], ["wau"])
        b_alpha = sm(l, 52, 256)
        b_f = sm(l, 308, 8)

        def gla_common(ti, c0, n, blks):
            bk = bank()
            for c in range(DC):
                mm(ps[bk][0:16, 0:n], wsm[:, c, 8:24], hn[:, c, 0:n], c == 0, c == DC - 1, [wsmk, hnk], [PSK(bk)])
            cp("act", gaT[0:16, 0:n], ps[bk][0:16, 0:n], [PSK(bk)], [gaTk])
            wt, wk = nextw()
            load_w(wt, wk, win, C_GV, 512)
            for bi, blk in enumerate(blks):
                bc, nb = bcol(blk)
                lo = bc - c0
                bk = bank()
                for c in range(DC):
                    mm(ps[bk][0:nb, 0:512], hn[:, c, lo:lo + nb], wt[:, c, :], c == 0, c == DC - 1, [wk, hnk], [PSK(bk)])
                cp("act", vtok[0:nb, bi, :], ps[bk][0:nb, 0:512], [PSK(bk)], [(vtokk, bi)])
                bk = bank()
                mm(ps[bk][0:nb, 0:256], gaT[0:16, lo:lo + nb], wau[:, :], True, True, [gaTk, "wau"], [PSK(bk)])
                tt("dve", Ltok[0:nb, bi, :], ps[bk][0:nb, 0:256], b_alpha[0:nb, :], ALU.add, [PSK(bk), "smalls"], [(Ltokk, bi)])
                logsig_pos(Ltok[0:nb, bi, :], (Ltokk, bi), Ltok[0:nb, bi, :], (Ltokk, bi), nb)

        def cb_exp(bi, nb, need_e2):
            for pr in range(2):
                bk = bank()
                mm(ps[bk][:, 0:nb], Ltok[0:nb, bi, pr * 128:(pr + 1) * 128], trin16[0:nb, 0:nb], True, True, [(Ltokk, bi), "constS"], [PSK(bk)])
                act(E1[:, pr, 0:nb], ps[bk][:, 0:nb], AF.Exp, [PSK(bk)], [(E1k, pr)])
                if need_e2:
                    act(E2[:, pr, 0:nb], ps[bk][:, 0:nb], AF.Exp, [PSK(bk)], [(E2k, pr)], scale=-1.0)

        rg = [[0, 1, 2, 3], [4, 5, 6, 7]]
        gathered = set()

        def gather_chunk(c_):
            if c_ in gathered or c_ < 0 or c_ >= NCH:
                return
            gathered.add(c_)
            for (P_, G_, nm) in ((PK[l][c_], GK[l][c_], "K"), (PV[l][c_], GV[l][c_], "V"), (PU[l][c_], GU[l][c_], "U")):
                S.op("pool", lambda e, P_=P_, G_=G_: e.collective_compute("AllGather", ALU.bypass, replica_groups=rg, ins=[P_.opt()], outs=[G_.opt()]),
                     [("P" + nm, l, c_)], [("G" + nm, l, c_)], dma=True, cc=True)

        mk1 = A.mark()
        ktile, ktilek = A.alloc([128, 8, 512], BF16)
        vt, vtk = A.alloc([128, 512], BF16)
        Lf, Lfk = A.alloc([128, 8], F32)
        ktk, ktkk = A.alloc([128, 256], BF16)
        Er, Erk = A.alloc([128, 256], F32)
        usb, usbk = A.alloc([128, 258], F32)
        memset("dve", runS[:, :], 0.0, ["runS"])

        def p1_tile(ti, c0, n, blks):
            prenorm(l, 16, ti, c0, n, hn, hnk, sqb, sqk, rstd, rstdk)
            wt, wk = nextw()
            load_w(wt, wk, win, C_FK, 512)
            for h in range(8):
                bk = bank()
                for c in range(DC):
                    mm(ps[bk][0:64, 0:n], wt[:, c, h * 64:(h + 1) * 64], hn[:, c, 0:n], c == 0, c == DC - 1, [wk, hnk], [PSK(bk)])
                cp("act" if h % 2 == 0 else "dve", ktile[0:64, h, 0:n], ps[bk][0:64, 0:n], [PSK(bk)], [ktilek])
            if ti == 0:
                cp("dve", kmeta[0:64, :, 0:16], ktile[0:64, :, 0:16], [ktilek], ["kmeta"])
            else:
                dma("sp", PK[l][ti - 1].rearrange("(h d) t -> d h t", h=8)[:, :, 0:n], ktile[0:64, :, 0:n], [ktilek], [("PK", l, ti - 1)])
            wt, wk = nextw()
            load_w(wt, wk, win, C_FV, 512)
            for bi, blk in enumerate(blks):
                bc, nb = bcol(blk)
                lo = bc - c0
                bk = bank()
                for c in range(DC):
                    mm(ps[bk][0:nb, 0:512], hn[:, c, lo:lo + nb], wt[:, c, :], c == 0, c == DC - 1, [wk, hnk], [PSK(bk)])
                if blk == 0:
                    cp("act", vmeta[0:16, :, 0:64], ps[bk][0:16, 0:512].rearrange("p (h d) -> p h d", h=8), [PSK(bk)], ["vmeta"])
                else:
                    cp("act", vt[0:nb, :], ps[bk][0:nb, 0:512], [PSK(bk)], [vtk])
                    dma("sp", PV[l][(blk - 1) // CH].rearrange("(h s) (b d) -> s h b d", h=8, b=CH)[:, :, (blk - 1) % CH, :],
                        vt[0:nb, :].rearrange("p (h d) -> p h d", h=8), [vtk], [("PV", l, (blk - 1) // CH)])
                bk = bank()
                for c in range(DC):
                    mm(ps[bk][0:nb, 0:8], hn[:, c, lo:lo + nb], wsm[:, c, 0:8], c == 0, c == DC - 1, [wsmk, hnk], [PSK(bk)])
                tt("dve", xf[0:nb, :], ps[bk][0:nb, 0:8], b_f[0:nb, :], ALU.add, [PSK(bk), "smalls"], [xfk])
                logsig_pos(xf[0:nb, :], xfk, Lf[0:nb, :], Lfk, nb)
                bk = bank()
                mm(ps[bk][0:nb, 0:8], trin1[0:nb, 0:nb], Lf[0:nb, :], True, True, [Lfk, "constS"], [PSK(bk)])
                mm(ps[bk][:, 8:16], nones_t[0:nb, :], Lf[0:nb, :], True, True, [Lfk, "constS3"], [PSK(bk)])
                if blk == 0:
                    cp("dve", lcS[0:nb, 0, :], ps[bk][0:nb, 0:8], [PSK(bk)], [("lcS", 0)])
                    cp("dve", m0S[:, :], ps[bk][:, 8:16], [PSK(bk)], ["m0S"])
                else:
                    tt("dve", lcS[0:nb, blk, :], ps[bk][0:nb, 0:8], runS[0:nb, :], ALU.add, [PSK(bk), "runS"], [("lcS", blk)])
                    tt("dve", runS[:, :], runS[:, :], ps[bk][:, 8:16], ALU.add, [PSK(bk), "runS"], ["runS"])
            gla_common(ti, c0, n, blks)
            wt, wk = nextw()
            load_w(wt, wk, win, C_GK, 256)
            for bi, blk in enumerate(blks):
                bc, nb = bcol(blk)
                lo = bc - c0
                cb_exp(bi, nb, False)
                bk = bank()
                for c in range(DC):
                    mm(ps[bk][0:nb, 0:256], hn[:, c, lo:lo + nb], wt[:, c, 0:256], c == 0, c == DC - 1, [wk, hnk], [PSK(bk)])
                bk2 = bank()
                mm(ps[bk2][0:nb, 0:256], trirev16[0:nb, 0:nb], Ltok[0:nb, bi, :], True, True, [(Ltokk, bi), "constS"], [PSK(bk2)])
                act(Er[0:nb, :], ps[bk2][0:nb, 0:256], AF.Exp, [PSK(bk2)], [Erk])
                tt("dve", ktk[0:nb, :], ps[bk][0:nb, 0:256], Er[0:nb, :], ALU.mult, [PSK(bk), Erk], [ktkk])
                bk = bank()
                for hh in range(4):
                    pr, hb = hh // 2, (hh % 2) * 64
                    mm(ps[bk][hb:hb + 64, pr * 128:(pr + 1) * 128], ktk[0:nb, hh * 64:(hh + 1) * 64], vtok[0:nb, bi, hh * 128:(hh + 1) * 128],
                       True, True, [ktkk, (vtokk, bi)], [PSK(bk)], tile_position=(0, hb))
                dst, dstk = (U0, "U0") if blk == 0 else (usb, usbk)
                cp("act", dst[:, 0:256], ps[bk][:, 0:256], [PSK(bk)], [dstk])
                cp("dve", dst[:, 256:258], E1[:, :, nb - 1], [(E1k, 0), (E1k, 1)], [dstk])
                if blk != 0:
                    dma("sp", PU[l][(blk - 1) // CH][:, ((blk - 1) % CH) * 258:((blk - 1) % CH + 1) * 258], usb[:, :], [usbk], [("PU", l, (blk - 1) // CH)])

        def after_group(grp):
            for t_ in sorted(grp, key=lambda t: t[0]):
                p1_tile(t_[0], t_[1], t_[2], t_[3])
                if t_[0] >= 1:
                    gather_chunk(t_[0] - 1)

        if stages is None or f"ffn0_{l}" in stages:
            ffn(l, 0, shared=(hn, hnkb, (sq0, sq1), (sqk0, sqk1), rstd, rstdkb, wb), after_group=after_group)
        else:
            for t_ in tiles:
                after_group([t_])
        for blk in range(1, NB + 1):
            tt("dve", pbS[:, (blk - 1) * 8:blk * 8], runS[:, :], lcS[:, blk, :], ALU.subtract, ["runS", ("lcS", blk)], ["pbS"])
        cp("dve", pbS[:, NB * 8:NB * 8 + 8], runS[:, :], ["runS"], ["pbS"])
        dma("sp", PB[l][:, :], pbS[:, :], ["pbS"], [("PB", l)])
        if mix_stop == "p1":
            S.barrier()
            A.release(mk0)
            return
        for c_ in range(NCH):
            gather_chunk(c_)
        S.op("pool", lambda e: e.collective_compute("AllGather", ALU.bypass, replica_groups=rg, ins=[PB[l].opt()], outs=[GB[l].opt()]),
             [("PB", l)], [("GB", l)], dma=True, cc=True)
        S.barrier()
        A.release(mk1)
        if mix_stop == "p2":
            S.barrier()
            A.release(mk0)
            return
        for i in range(3):
            dma("sp", gbT[:, i, :], GB[l][i * 128:(i + 1) * 128, :], [("GB", l)], ["gbT"])
        Tb = lambda i: gbT[:, i, NB * 8:NB * 8 + 8]
        e_ = lambda i: cvec[:, i:i + 1]
        vis = lambda i: cvec[:, 3 + i:4 + i]
        ts("dve", Gt[:, 2, :], Tb(2), e_(2), ALU.mult, ["gbT", "cvec"], ["Gt"])
        stt("dve", Gt[:, 0, :], Tb(1), e_(1), Gt[:, 2, :], ALU.mult, ALU.add, ["gbT", "cvec", "Gt"], ["Gt"])
        stt("dve", Gt[:, 3, :], Tb(0), e_(0), Gt[:, 0, :], ALU.mult, ALU.add, ["gbT", "cvec", "Gt"], ["Gt"])
        ts("dve", Gt[:, 1, :], Gt[:, 2, :], vis(1), ALU.add, ["Gt", "cvec"], ["Gt"])
        ts("dve", Gt[:, 0, :], Gt[:, 0, :], vis(0), ALU.add, ["Gt", "cvec"], ["Gt"])
        ts("dve", Gt[:, 2, :], Gt[:, 2, :], 0.0, ALU.mult, ["Gt"], ["Gt"], s2=vis(2), op1=ALU.add)
        for i in range(3):
            tt("dve", gbT[:, i, 0:NB * 8].rearrange("p (b h) -> p b h", h=8), gbT[:, i, 0:NB * 8].rearrange("p (b h) -> p b h", h=8),
               Gt[:, i, :].unsqueeze(1).to_broadcast([128, NB, 8]), ALU.add, ["gbT", "Gt"], ["gbT"])
        ts("dve", bown[:, :, :], lcS[:, :, :], -1.0, ALU.mult, [("lcS", b) for b in range(NB + 1)], ["bown"])
        tt("dve", bmeta[:, 0, :], m0S[:, :], lcS[:, 0, :], ALU.subtract, ["m0S", ("lcS", 0)], ["bmeta"])
        tt("dve", bmeta[:, 0, :], bmeta[:, 0, :], Gt[:, 3, :], ALU.add, ["bmeta", "Gt"], ["bmeta"])
        ts("dve", bmeta[:, 1, :], lcS[:, 0, :], -1.0, ALU.mult, [("lcS", 0)], ["bmeta"])
        mk2 = A.mark()
        ubig = [[A.alloc([128, CH * 258], F32) for _ in range(NCH)] for _ in range(3)]
        for i in range(3):
            for c_ in range(NCH):
                dma("sp", ubig[i][c_][0][:, :], GU[l][c_][i * 128:(i + 1) * 128, :], [("GU", l, c_)], [ubig[i][c_][1]])
        ome, omek = A.alloc([128, 3], F32)
        ts("dve", ome[:, :], cvec[:, 0:3], -1.0, ALU.mult, ["cvec"], [omek], s2=1.0, op1=ALU.add)
        dpa = [A.alloc([128, NB, 2], F32) for _ in range(3)]
        Fs = [A.alloc([128, 2, 128], F32) for _ in range(3)]
        Da = [A.alloc([128, 2], F32) for _ in range(3)]
        for i in range(3):
            for c_ in range(NCH):
                ubt, uk = ubig[i][c_]
                ts("dve", dpa[i][0][:, c_ * CH:(c_ + 1) * CH, :], ubt[:, :].rearrange("p (c w) -> p c w", c=CH)[:, :, 256:258], e_(i), ALU.mult,
                   [uk, "cvec"], [dpa[i][1]], s2=ome[:, i:i + 1], op1=ALU.add)
            memset("pool", Fs[i][0][:, :, :], 0.0, [Fs[i][1]])
            memset("pool", Da[i][0][:, :], 1.0, [Da[i][1]])
        for ck in range(NB):
            for i in range(3):
                tt("dve", Fs[i][0][:, :, :], Fs[i][0][:, :, :], dpa[i][0][:, ck, :].unsqueeze(2).to_broadcast([128, 2, 128]), ALU.mult,
                   [Fs[i][1], dpa[i][1]], [Fs[i][1]])
            for i in range(3):
                ubt, uk = ubig[i][ck // CH]
                u = ubt[:, (ck % CH) * 258:(ck % CH + 1) * 258]
                stt("dve", Fs[i][0][:, :, :], u[:, 0:256].rearrange("p (a v) -> p a v", a=2), e_(i), Fs[i][0][:, :, :], ALU.mult, ALU.add,
                    [uk, "cvec", Fs[i][1]], [Fs[i][1]])
            for i in range(3):
                tt("pool", Da[i][0][:, :], Da[i][0][:, :], dpa[i][0][:, ck, :], ALU.mult, [Da[i][1], dpa[i][1]], [Da[i][1]])
        cp("dve", Sst[:, :, :], U0[:, 0:256].rearrange("p (a v) -> p a v", a=2), ["U0"], ["Sst"])
        for i in range(3):
            tt("dve", Sst[:, :, :], Sst[:, :, :], Da[i][0][:, 0:2].unsqueeze(2).to_broadcast([128, 2, 128]), ALU.mult, ["Sst", Da[i][1]], ["Sst"])
            tt("dve", Sst[:, :, :], Sst[:, :, :], Fs[i][0][:, :, :], ALU.add, ["Sst", Fs[i][1]], ["Sst"])
        S.barrier()
        A.release(mk2)
        if mix_stop == "prep":
            S.barrier()
            A.release(mk0)
            return
        mk3 = A.mark()
        QT, QTk = A.alloc([128, 8, 512], BF16)
        OT, OTk = A.alloc([128, 4, 512], BF16)
        GT, GTk = A.alloc([128, 4, 512], BF16)
        sgr, sgrk = A.alloc([128, 4, 512], BF16)
        qg, qgk = A.alloc([128, 2, 512], BF16)
        kg, kgk = A.alloc([128, 2, 512], BF16)
        yT, yTk = A.alloc([128, DC, 512], BF16)
        oT, oTk = A.alloc([128, DC, 512], F32)
        memset("pool", QT[:, :, :], 0.0, [QTk])
        c3, c3k = A.alloc([128, 8], F32)
        r3, r3k = A.alloc([128, 8], F32)
        mk4 = A.mark()
        for (ti, c0, n, blks) in tiles:
            prenorm(l, 16, ti, c0, n, hn, hnk, sqb, sqk, rstd, rstdk)
            wt, wk = nextw()
            load_w(wt, wk, win, C_FQ, 512)
            for h in range(8):
                bk = bank()
                for c in range(DC):
                    mm(ps[bk][0:64, 0:n], wt[:, c, h * 64:(h + 1) * 64], hn[:, c, 0:n], c == 0, c == DC - 1, [wk, hnk], [PSK(bk)])
                act(QT[0:64, h, 0:n], ps[bk][0:64, 0:n], AF.Copy, [PSK(bk)], [QTk], scale=0.125)
            for bi, blk in enumerate(blks):
                bc, nb = bcol(blk)
                lo = bc - c0
                cp("dve", split[0:nb, :, 64], lcS[0:nb, blk, :], [("lcS", blk)], ["split"])
                tt("dve", r3[0:nb, :], lcS[0:nb, blk, :], split[0:nb, :, 64], ALU.subtract, [("lcS", blk), "split"], [r3k])
                cp("dve", split[0:nb, :, 65], r3[0:nb, :], [r3k], ["split"])
                tt("dve", c3[0:nb, :], r3[0:nb, :], split[0:nb, :, 65], ALU.subtract, [r3k, "split"], [c3k])
                cp("dve", split[0:nb, :, 66], c3[0:nb, :], [c3k], ["split"])
                for h in range(8):
                    bk = bank()
                    mm(ps[bk][0:67, 0:nb], split[0:nb, h, :], identb[0:nb, 0:nb], True, True, ["split", "identb"], [PSK(bk)])
                    cp("act" if h % 2 == 0 else "dve", QT[64:67, h, lo:lo + nb], ps[bk][64:67, 0:nb], [PSK(bk)], [QTk])
            if mix_stop == "q":
                continue
            gla_common(ti, c0, n, blks)
            wt, wk = nextw()
            load_w(wt, wk, win, C_GQ, 512)
            for pr in range(2):
                bk = bank()
                for c in range(DC):
                    mm(ps[bk][:, 0:n], wt[:, c, pr * 128:(pr + 1) * 128], hn[:, c, 0:n], c == 0, c == DC - 1, [wk, hnk], [PSK(bk)])
                cp("act", qg[:, pr, 0:n], ps[bk][:, 0:n], [PSK(bk)], [qgk])
                bk = bank()
                for c in range(DC):
                    mm(ps[bk][:, 0:n], wt[:, c, 256 + pr * 128:256 + (pr + 1) * 128], hn[:, c, 0:n], c == 0, c == DC - 1, [wk, hnk], [PSK(bk)])
                cp("dve", kg[:, pr, 0:n], ps[bk][:, 0:n], [PSK(bk)], [kgk])
            wt, wk = nextw()
            load_w(wt, wk, win, C_GR, 512)
            for hh in range(4):
                bk = bank()
                for c in range(DC):
                    mm(ps[bk][:, 0:n], wt[:, c, hh * 128:(hh + 1) * 128], hn[:, c, 0:n], c == 0, c == DC - 1, [wk, hnk], [PSK(bk)])
                act(sgr[:, hh, 0:n], ps[bk][:, 0:n], AF.Silu, [PSK(bk)], [sgrk])
            if mix_stop == "glaproj":
                continue
            mkf = A.mark()
            KTb = [A.alloc([128, SEG], BF16) for _ in range(2)]
            Vb = [A.alloc([128, NB, 128], BF16) for _ in range(2)]
            PT = [A.alloc([128, 512], BF16) for _ in range(5)]
            rec, reck = A.alloc([128, 512], F32)
            for (t_, k_) in KTb:
                memset("pool", t_[64:67, :], 1.0, [k_])
            for (t_, k_) in Vb:
                memset("pool", t_[:, :, 64:128], 1.0, [k_])
            LA = 3
            groups = []
            units = []
            for h in range(8):
                bx = 6 + (h % 2)
                hu = []
                if ti == 0:
                    hu.append(dict(kt=kmeta[0:67, h, 0:16], v=vmeta[0:16, h, :], nk=16, bias=bmeta[0:16, 1, h:h + 1], rk=["kmeta", "vmeta"],
                                   mask=maskA[0:16, 0, 0:16], q0=0, grp=None))
                else:
                    hu.append(dict(kt=kmeta[0:67, h, 0:16], v=vmeta[0:16, h, :], nk=16, bias=bmeta[0:16, 0, h:h + 1], rk=["kmeta", "vmeta"],
                                   mask=None, q0=0, grp=None))
                    for src in (3, 0, 1, 2):
                        nblk_src = NB if src < 3 else blks[-1]
                        gi = len(groups)
                        groups.append((h, src, nblk_src))
                        for kb in range(nblk_src):
                            u = dict(nk=128, q0=0, mask=None, grp=gi, kb=kb)
                            if src < 3:
                                u["bias"] = gbT[:, src, kb * 8 + h:kb * 8 + h + 1]
                            else:
                                u["bias"] = bown[:, kb + 1, h:h + 1]
                                r = (kb + 1) - blks[0]
                                if r >= 0:
                                    u["q0"] = r * 128
                                    u["mask"] = maskA[:, r, r * 128:n]
                            hu.append(u)
                for i_, u in enumerate(hu):
                    u["h"], u["bx"], u["first"], u["last"] = h, bx, i_ == 0, i_ == len(hu) - 1
                units += hu
            gbuf = {}

            def load_group(gi):
                if gi >= len(groups) or gi in gbuf:
                    return
                h, src, nblk_src = groups[gi]
                kt_, ktk_ = KTb[gi % 2]
                v_, vk_ = Vb[gi % 2]
                ncs = nblk_src // CH
                if src < 3:
                    ksrc = GKa[l][:, src * 512 + h * 64:src * 512 + (h + 1) * 64, :]
                    vsrc = GVa[l][:, src * 1024 + h * 128:src * 1024 + (h + 1) * 128, :]
                    kkeys = [("GK", l, c_) for c_ in range(ncs)]
                    vkeys = [("GV", l, c_) for c_ in range(ncs)]
                else:
                    ksrc = PKa[l][0:ncs, h * 64:(h + 1) * 64, :]
                    vsrc = PVa[l][0:ncs, h * 128:(h + 1) * 128, :]
                    kkeys = [("PK", l, c_) for c_ in range(ncs)]
                    vkeys = [("PV", l, c_) for c_ in range(ncs)]
                dma("sp", kt_[0:64, 0:ncs * CH * 128].rearrange("d (c t) -> d c t", c=ncs), ksrc.rearrange("c d t -> d c t"), kkeys, [ktk_])
                for c_ in range(ncs):
                    dma("sp", v_[:, c_ * CH:(c_ + 1) * CH, 0:64], vsrc[c_].rearrange("s (b d) -> s b d", b=CH), [vkeys[c_]], [vk_])
                gbuf[gi] = (kt_, ktk_, v_, vk_)

            def stage_a(ui, u):
                if u["grp"] is not None:
                    load_group(u["grp"])
                    kt_, ktk_, v_, vk_ = gbuf[u["grp"]]
                    kb = u["kb"]
                    u["kt"], u["v"], u["rk"] = kt_[0:67, kb * 128:(kb + 1) * 128], v_[:, kb, :], [ktk_, vk_]
                nk, q0, h = u["nk"], u["q0"], u["h"]
                bs = ui % 6
                mm(ps[bs][0:nk, q0:n], u["kt"], QT[0:67, h, q0:n], True, True, u["rk"] + [QTk], [PSK(bs)])
                p_, pk_ = PT[ui % 5]
                act(p_[0:nk, q0:n], ps[bs][0:nk, q0:n], AF.Exp, [PSK(bs), "gbT", "bown", "bmeta"], [pk_], bias=u["bias"])
                if u["mask"] is not None:
                    tt("dve", p_[0:nk, q0:n], p_[0:nk, q0:n], u["mask"], ALU.mult, [pk_, "maskA"], [pk_])

            def stage_b(ui, u):
                nk, q0, h, bx = u["nk"], u["q0"], u["h"], u["bx"]
                p_, pk_ = PT[ui % 5]
                mm(ps[bx][:, q0:n], u["v"], p_[0:nk, q0:n], u["first"], u["last"], u["rk"] + [pk_], [PSK(bx)])
                if u["grp"] is not None and u["kb"] == 0:
                    load_group(u["grp"] + 1)
                if u["last"]:
                    recip(rec[0:64, 0:n], ps[bx][64:128, 0:n], [PSK(bx)], [reck])
                    hb = (h % 2) * 64
                    tt("dve", OT[hb:hb + 64, h // 2, 0:n], ps[bx][0:64, 0:n], rec[0:64, 0:n], ALU.mult, [PSK(bx), reck], [OTk])

            for i_ in range(len(units) + LA):
                if i_ < len(units):
                    stage_a(i_, units[i_])
                if i_ - LA >= 0:
                    stage_b(i_ - LA, units[i_ - LA])
            S.barrier()
            A.release(mkf)
            if mix_stop == "fox":
                continue
            mkg = A.mark()
            qp, qpk = A.alloc([128, 4, 128], BF16)
            kp, kpk = A.alloc([128, 4, 128], BF16)
            Sbh, _ = A.alloc([128, 4, 128], BF16)
            attm, attmk = A.alloc([128, 4, 128], BF16)
            sqg, sqgk = A.alloc([128, 512], F32)
            rsg, rsgk = A.alloc([128, 512], F32)
            t1, t1k = A.alloc([128, 512], F32)
            uc = [A.alloc([128, 258], F32) for _ in range(2)]
            for bi, blk in enumerate(blks):
                bc, nb = bcol(blk)
                lo = bc - c0
                cb_exp(bi, nb, True)
                for hh in range(4):
                    pr, hb = hh // 2, (hh % 2) * 64
                    stt("dve", qp[0:64, hh, 0:nb], qg[hb:hb + 64, pr, lo:lo + nb], 0.125, E1[hb:hb + 64, pr, 0:nb], ALU.mult, ALU.mult, [qgk, (E1k, pr)], [qpk])
                    tt("dve", kp[0:64, hh, 0:nb], kg[hb:hb + 64, pr, lo:lo + nb], E2[hb:hb + 64, pr, 0:nb], ALU.mult, [kgk, (E2k, pr)], [kpk])
                ba = bank()
                for hh in range(4):
                    mm(ps[ba][0:nb, hh * 128:hh * 128 + nb], kp[0:64, hh, 0:nb], qp[0:64, hh, 0:nb], True, True, [kpk, qpk], [PSK(ba)])
                tt("dve", attm[0:nb, :, 0:nb], ps[ba][0:nb, :].rearrange("p (a t) -> p a t", a=4)[:, :, 0:nb],
                   trif[0:nb, 0:nb].unsqueeze(1).to_broadcast([nb, 4, nb]), ALU.mult, [PSK(ba), "constS"], [attmk])
                if blk == 0:
                    memset("dve", Sbh[:, :, :], 0.0, ["Sbh"])
                else:
                    for hh in range(4):
                        pr, hb = hh // 2, (hh % 2) * 64
                        cp("dve", Sbh[0:64, hh, :], Sst[hb:hb + 64, pr, :], ["Sst"], ["Sbh"])
                bo = bank()
                for hh in range(4):
                    mm(ps[bo][:, hh * 128:hh * 128 + nb], Sbh[0:64, hh, :], qp[0:64, hh, 0:nb], True, False, ["Sbh", qpk], [PSK(bo)])
                    mm(ps[bo][:, hh * 128:hh * 128 + nb], vtok[0:nb, bi, hh * 128:(hh + 1) * 128], attm[0:nb, hh, 0:nb], False, True,
                       [(vtokk, bi), attmk], [PSK(bo)])
                if mix_stop == "gla2":
                    continue
                if blk != 0:
                    u, uk = uc[bi % 2]
                    dma("sp", u[:, :], PU[l][(blk - 1) // CH][:, ((blk - 1) % CH) * 258:((blk - 1) % CH + 1) * 258], [("PU", l, (blk - 1) // CH)], [uk])
                    for pr in range(2):
                        stt("dve", Sst[:, pr, :], Sst[:, pr, :], u[:, 256 + pr:257 + pr], u[:, pr * 128:(pr + 1) * 128], ALU.mult, ALU.add,
                            ["Sst", uk], ["Sst"])
                if mix_stop == "gla3":
                    continue
                o4 = ps[bo][:, :].rearrange("p (a t) -> p a t", a=4)[:, :, 0:nb]
                act(sqg[:, 0:4 * nb].rearrange("p (a t) -> p a t", a=4), o4, AF.Square, [PSK(bo)], [sqgk])
                bs = bank()
                mm(ps[bs][:, 0:4 * nb], ones_t[:, :], sqg[:, 0:4 * nb], True, True, [sqgk, "constS2"], [PSK(bs)])
                act(rsg[:, 0:4 * nb], ps[bs][:, 0:4 * nb], AF.Sqrt, [PSK(bs)], [rsgk], bias=epsb[:, 0:1], scale=1.0 / 128)
                recip(rsg[:, 0:4 * nb], rsg[:, 0:4 * nb], [rsgk], [rsgk])
                tt("dve", t1[:, 0:4 * nb].rearrange("p (a t) -> p a t", a=4), o4, rsg[:, 0:4 * nb].rearrange("p (a t) -> p a t", a=4), ALU.mult,
                   [PSK(bo), rsgk], [t1k])
                for hh in range(4):
                    stt("dve", GT[:, hh, lo:lo + nb], t1[:, hh * nb:(hh + 1) * nb], sm(l, 48 + hh, 1), sgr[:, hh, lo:lo + nb], ALU.mult, ALU.mult,
                        [t1k, "smalls", sgrk], [GTk])
            S.barrier()
            A.release(mkg)
            if mix_stop in ("gla", "gla2", "gla3"):
                continue
            mkm = A.mark()
            sga, sgak = A.alloc([128, 512], F32)
            sgbt, sgbk = A.alloc([128, 512], F32)
            t2, t2k = A.alloc([128, 512], F32)
            tm0, tmk0 = A.alloc([128, 512], F32)
            tm1, tmk1 = A.alloc([128, 512], F32)
            for cg in range(2):
                wa, wak = nextw()
                load_w(wa, wak, win, C_MA + cg * 512, 512)
                wbb, wbk = nextw()
                load_w(wbb, wbk, win, C_MB + cg * 512, 512)
                wo, wok = nextw()
                dma("pool", wo[:, 0:4, :], w_fo[l].rearrange("(c p) f -> p c f", p=128)[:, :, cg * 512:(cg + 1) * 512], [], [wok])
                dma("pool", wo[:, 4:8, :], w_go[l].rearrange("(c p) f -> p c f", p=128)[:, :, cg * 512:(cg + 1) * 512], [], [wok])
                for cc in range(4):
                    c = cg * 4 + cc
                    b1, b2, b3, b4 = bank(), bank(), bank(), bank()
                    for k in range(DC):
                        mm(ps[b1][:, 0:n], wa[:, k, cc * 128:(cc + 1) * 128], hn[:, k, 0:n], k == 0, k == DC - 1, [wak, hnk], [PSK(b1)])
                    for k in range(DC):
                        mm(ps[b2][:, 0:n], wbb[:, k, cc * 128:(cc + 1) * 128], hn[:, k, 0:n], k == 0, k == DC - 1, [wbk, hnk], [PSK(b2)])
                    for k in range(4):
                        mm(ps[b3][:, 0:n], wo[:, k, cc * 128:(cc + 1) * 128], OT[:, k, 0:n], k == 0, k == 3, [wok, OTk], [PSK(b3)])
                    for k in range(4):
                        mm(ps[b4][:, 0:n], wo[:, 4 + k, cc * 128:(cc + 1) * 128], GT[:, k, 0:n], k == 0, k == 3, [wok, GTk], [PSK(b4)])
                    act(sga[:, 0:n], ps[b1][:, 0:n], AF.Sigmoid, [PSK(b1)], [sgak])
                    act(sgbt[:, 0:n], ps[b2][:, 0:n], AF.Sigmoid, [PSK(b2)], [sgbk])
                    tt("dve", t2[:, 0:n], sga[:, 0:n], ps[b3][:, 0:n], ALU.mult, [sgak, PSK(b3)], [t2k])
                    tt("dve", sgbt[:, 0:n], sgbt[:, 0:n], ps[b4][:, 0:n], ALU.mult, [sgbk, PSK(b4)], [sgbk])
                    tt("pool", yT[:, c, 0:n], t2[:, 0:n], sgbt[:, 0:n], ALU.add, [t2k, sgbk], [(yTk, c)])
            for c2 in range(DC):
                if c2 % 4 == 0:
                    wo, wok = nextw()
                    load_w(wo, wok, w_out[l], c2 * 128, 512)
                bk = bank()
                for k in range(DC):
                    mm(ps[bk][:, 0:n], wo[:, k, (c2 % 4) * 128:(c2 % 4 + 1) * 128], yT[:, k, 0:n], k == 0, k == DC - 1, [wok, (yTk, k)], [PSK(bk)])
                cp("act", oT[:, c2, 0:n], ps[bk][:, 0:n], [PSK(bk)], [oTk])
            postnorm_residual(l, 24, ti, c0, n, oT, oTk, sqb, sqk, rstd, rstdk, [tm0, tm1], [tmk0, tmk1], 1.0)
            S.barrier()
            A.release(mkm)
        S.barrier()
        A.release(mk0)

    for l in range(L):
        if stages is None or f"mix_{l}" in stages:
            mixer(l)
        elif f"ffn0_{l}" in stages:
            ffn(l, 0)
        if stages is None or f"ffn1_{l}" in stages:
            ffn(l, 1)
    finals = []
    for (ti, c0, n, blks) in tiles[1:]:
        finals.append(dma("sp", outT.rearrange("(c p) t -> p c t", p=128)[:, :, c0 - 16:c0 - 16 + n], hT[:, :, c0:c0 + n], [("hT", ti)], [("outT", ti)]))
    if dumps:
        dump("hT", hT[:, :, :], [("hT", t[0]) for t in tiles], [128, DC, TL])
    S.emit(final_ops=finals + list(dump_outs.values()))
    return nc, S


def make_consts():
    s = np.arange(128)[:, None]
    t = np.arange(128)[None, :]
    le = (s <= t).astype(np.float32)
    c = np.zeros((128, 5 * 128 + 4 * 512), np.float32)
    c[:, 0:128] = np.eye(128, dtype=np.float32)
    c[:, 128:256] = le * (-1.0 / 16.0)
    c[:, 256:384] = (s > t).astype(np.float32) * (-1.0 / 16.0)
    c[:, 384:512] = -le
    c[:, 512:640] = le
    for r in range(4):
        m = np.zeros((128, 512), np.float32)
        for q in range(4):
            if q == r:
                m[:, q * 128:(q + 1) * 128] = le
            elif q > r:
                m[:, q * 128:(q + 1) * 128] = 1.0
        c[:, 640 + r * 512:640 + (r + 1) * 512] = m
    return c


def make_smalls(inp, L):
    sm = np.zeros((128, L * NSM), np.float32)
    names = ["g_pre_ffn1", "g_post_ffn1", "g_pre_mix", "g_post_mix", "g_pre_ffn2", "g_post_ffn2"]
    for l in range(L):
        o = l * NSM
        for i, nm in enumerate(names):
            sm[:, o + i * 8:o + (i + 1) * 8] = np.asarray(inp[nm], np.float32)[l].reshape(8, 128).T
        sm[:, o + 48:o + 52] = np.asarray(inp["g_gla_out"], np.float32)[l].reshape(4, 128).T
        sm[:, o + 52:o + 308] = np.asarray(inp["b_alpha"], np.float32)[l][None, :]
        sm[:, o + 308:o + 316] = np.asarray(inp["b_f"], np.float32)[l][None, :]
    return sm


def make_cvec(j):
    v = np.zeros((128, 8), np.float32)
    for i in range(3):
        v[:, i] = 1.0 if i < j else 0.0
        v[:, 3 + i] = 0.0 if i < j else -30000.0
    return v


_CACHE = {}


def run(inputs, NB, L=2, dumps=None, stages=None, mix_stop=None):
    key = (NB, L, tuple(dumps) if dumps else None)
    x = np.asarray(inputs["x"], np.float32)
    B, SEQ, _ = x.shape
    assert B == 2 and SEQ == 4 * NB * 128
    nc, S = build_program(NB, L, dumps, stages, mix_stop)
    consts = make_consts()
    smalls = make_smalls(inputs, L)
    metaT = np.ascontiguousarray(np.asarray(inputs["meta_tokens"], np.float32).T)
    shared = dict(consts=consts, smalls=smalls, metaT=metaT)
    for nm in ["w_in", "w_alpha_up", "w_fox_o", "w_gla_o", "w_out", "w_ffn1_gu", "w_ffn1_down", "w_ffn2_gu", "w_ffn2_down"]:
        shared[nm] = np.ascontiguousarray(np.asarray(inputs[nm], np.float32)[:L])
    in_maps = []
    SEG = NB * 128
    for core in range(8):
        b, j = core // 4, core % 4
        m = dict(shared)
        m["xT"] = np.ascontiguousarray(x[b, j * SEG:(j + 1) * SEG, :].T)
        m["cvec"] = make_cvec(j)
        in_maps.append(m)
    res = run_bass_kernel_spmd(nc, in_maps, core_ids=list(range(8)))
    out = np.zeros((B, SEQ, D), np.float32)
    for core in range(8):
        b, j = core // 4, core % 4
        out[b, j * SEG:(j + 1) * SEG, :] = np.asarray(res.results[core]["outT"]).T
    return out, res


def kernel(**inputs):
    out, _ = run(inputs, 16, 2)
    return out
```

```python
import numpy as np
import concourse.bass as bass
import concourse.mybir as mybir
from concourse.bass_utils import run_bass_kernel_spmd

F32 = mybir.dt.float32
BF16 = mybir.dt.bfloat16
ALU = mybir.AluOpType
AF = mybir.ActivationFunctionType

EPOCH = 24000
N_DMA_SEMS = 24
D = 1024
DC = 8
FF = 2816
FC = 22
NIN = 5144
C_FQ, C_FK, C_FV, C_FF, C_GQ, C_GK, C_GV, C_GA, C_GR, C_MA, C_MB = 0, 512, 1024, 1536, 1544, 1800, 2056, 2568, 2584, 3096, 4120
EPS = 1e-6
NSM = 316


class Sched:
    ENGS = ("pe", "act", "dve", "pool", "sp")

    def __init__(self, nc):
        self.nc = nc
        self.ops = []
        self.last_write = {}
        self.readers = {}
        self.pending_barrier = None

    def op(self, eng, fn, reads=(), writes=(), dma=False, cc=False):
        deps = set()
        for r in reads:
            lw = self.last_write.get(r)
            if lw is not None:
                deps.add(lw)
        for w in writes:
            lw = self.last_write.get(w)
            if lw is not None:
                deps.add(lw)
            for rd in self.readers.get(w, ()):
                deps.add(rd)
        oid = len(self.ops)
        if self.pending_barrier is not None:
            pb = self.pending_barrier
            if eng not in pb["done"]:
                deps |= pb["deps"]
                pb["done"].add(eng)
        self.ops.append(dict(eng=eng, fn=fn, deps=deps, dma=dma, cc=cc))
        for r in reads:
            lst = self.readers.setdefault(r, [])
            if not dma:
                lst[:] = [x for x in lst if self.ops[x]["dma"] or self.ops[x]["eng"] != eng]
            lst.append(oid)
        for w in writes:
            self.last_write[w] = oid
            self.readers[w] = []
        return oid

    def barrier(self):
        deps = set()
        seen = set()
        nd = 0
        for i in range(len(self.ops) - 1, -1, -1):
            o = self.ops[i]
            if o["dma"]:
                if nd < N_DMA_SEMS:
                    deps.add(i)
                    nd += 1
            elif o["eng"] not in seen:
                seen.add(o["eng"])
                deps.add(i)
            if len(seen) == 5 and nd >= N_DMA_SEMS:
                break
        self.pending_barrier = dict(deps=deps, done=set())

    def emit(self, final_ops=()):
        nc = self.nc
        ops = self.ops
        n = len(ops)
        needs = [False] * n
        for o in ops:
            for d in o["deps"]:
                if ops[d]["eng"] == "pe" and o["eng"] == "pe" and not ops[d]["dma"] and not o["dma"]:
                    continue
                needs[d] = True
        for d in final_ops:
            needs[d] = True
        cnt = {e: 0 for e in self.ENGS}
        for i, o in enumerate(ops):
            if not o["dma"] and needs[i]:
                cnt[o["eng"]] += 1
        sems = {e: [nc.alloc_semaphore(f"s_{e}_{i}") for i in range(cnt[e] // EPOCH + 1)] for e in self.ENGS}
        dsems = [nc.alloc_semaphore(f"s_dma_{i}") for i in range(N_DMA_SEMS)]
        dcount = [0] * N_DMA_SEMS
        dlast = [None] * N_DMA_SEMS
        token = [None] * n
        ecount = {e: 0 for e in self.ENGS}
        dk = 0
        for i, o in enumerate(ops):
            if o["cc"]:
                token[i] = (nc.alloc_semaphore(f"s_cc_{i}"), 1, 1)
            elif o["dma"]:
                k = dk % N_DMA_SEMS
                dk += 1
                if dlast[k] is not None:
                    o["deps"].add(dlast[k])
                dcount[k] += 16
                token[i] = (dsems[k], dcount[k], 16)
                dlast[k] = i
            elif needs[i]:
                c = ecount[o["eng"]]
                ecount[o["eng"]] = c + 1
                token[i] = (sems[o["eng"]][c // EPOCH], c % EPOCH + 1, 1)
        streams = {e: [] for e in self.ENGS}
        for i, o in enumerate(ops):
            streams[o["eng"]].append(i)
        self.n_waits = 0

        def run_stream(e, eng):
            waited = {}
            for i in streams[e]:
                o = ops[i]
                for d in sorted(o["deps"]):
                    od = ops[d]
                    if e == "pe" and od["eng"] == "pe" and not od["dma"] and not o["dma"]:
                        continue
                    sem, val, _ = token[d]
                    key = id(sem)
                    if waited.get(key, 0) >= val:
                        continue
                    eng.wait_ge(sem, val)
                    self.n_waits += 1
                    waited[key] = val
                ins = o["fn"](eng)
                if token[i] is not None:
                    sem, val, step = token[i]
                    ins.then_inc(sem, step)
            if e == "sp":
                for d in final_ops:
                    sem, val, _ = token[d]
                    eng.wait_ge(sem, val)

        with nc.Block() as block:
            @block.tensor
            def _(eng):
                run_stream("pe", eng)

            @block.scalar
            def _(eng):
                run_stream("act", eng)

            @block.vector
            def _(eng):
                run_stream("dve", eng)

            @block.gpsimd
            def _(eng):
                run_stream("pool", eng)

            @block.sync
            def _(eng):
                run_stream("sp", eng)


class Arena:
    def __init__(self, nc, nbytes):
        self.t = nc.alloc_sbuf_tensor("arena", [128, nbytes // 2], BF16)
        self.n = nbytes // 2
        self.top = 0
        self.uid = 0

    def alloc(self, shape, dtype):
        free = 1
        for s in shape[1:]:
            free *= s
        ne = free * (2 if dtype == F32 else 1)
        ne = (ne + 15) // 16 * 16
        assert self.top + ne <= self.n, f"arena overflow {self.top + ne} > {self.n}"
        ap = self.t[:, self.top:self.top + ne]
        self.top += ne
        if dtype == F32:
            ap = ap.bitcast(F32)
        ap = ap[:, 0:free]
        if len(shape) == 3:
            ap = ap.rearrange("p (a b) -> p a b", a=shape[1])
        self.uid += 1
        return ap, ("ar", self.uid)

    def mark(self):
        return self.top

    def release(self, m):
        self.top = m


def build_program(NB, L=2, dumps=None, stages=None, mix_stop=None):
    nc = bass.Bass("TRN2", target_bir_lowering=False)
    SEG = NB * 128
    TL = 16 + SEG
    NPB = NB * 8 + 8
    NPU = NB * 258
    S = Sched(nc)

    def dram_in(name, shape, dt=F32):
        return nc.dram_tensor(name, shape, dt, kind="ExternalInput").ap()

    xT = dram_in("xT", [D, SEG])
    metaT = dram_in("metaT", [D, 16])
    cvec_d = dram_in("cvec", [128, 8])
    consts_d = dram_in("consts", [128, 5 * 128 + 4 * 512])
    smalls_d = dram_in("smalls", [128, L * NSM])
    w_in = dram_in("w_in", [L, D, NIN])
    w_au = dram_in("w_alpha_up", [L, 16, 256])
    w_fo = dram_in("w_fox_o", [L, 512, D])
    w_go = dram_in("w_gla_o", [L, 512, D])
    w_out = dram_in("w_out", [L, D, D])
    w_gu = [dram_in("w_ffn1_gu", [L, D, 2 * FF]), dram_in("w_ffn2_gu", [L, D, 2 * FF])]
    w_dn = [dram_in("w_ffn1_down", [L, FF, D]), dram_in("w_ffn2_down", [L, FF, D])]
    outT = nc.dram_tensor("outT", [D, SEG], F32, kind="ExternalOutput").ap()
    CH = min(4, NB)
    NCH = NB // CH
    PKa = [nc.dram_tensor(f"PKa{l}", [NCH, 512, CH * 128], BF16).ap() for l in range(L)]
    PVa = [nc.dram_tensor(f"PVa{l}", [NCH, 1024, CH * 64], BF16).ap() for l in range(L)]
    PK = [[PKa[l][c] for c in range(NCH)] for l in range(L)]
    PV = [[PVa[l][c] for c in range(NCH)] for l in range(L)]
    PB = [nc.dram_tensor(f"PB{l}", [128, NPB], F32).ap() for l in range(L)]
    PU = [[nc.dram_tensor(f"PU{l}_{c}", [128, CH * 258], F32).ap() for c in range(NCH)] for l in range(L)]
    GKa = [nc.dram_tensor(f"GKa{l}", [NCH, 4 * 512, CH * 128], BF16).ap() for l in range(L)]
    GVa = [nc.dram_tensor(f"GVa{l}", [NCH, 4 * 1024, CH * 64], BF16).ap() for l in range(L)]
    GK = [[GKa[l][c] for c in range(NCH)] for l in range(L)]
    GV = [[GVa[l][c] for c in range(NCH)] for l in range(L)]
    GB = [nc.dram_tensor(f"GB{l}", [4 * 128, NPB], F32).ap() for l in range(L)]
    GU = [[nc.dram_tensor(f"GU{l}_{c}", [4 * 128, CH * 258], F32).ap() for c in range(NCH)] for l in range(L)]
    dump_outs = {}

    def sb(name, shape, dt):
        return nc.alloc_sbuf_tensor("sb_" + name, shape, dt)

    hT = sb("hT", [128, DC, TL], F32)
    constS = sb("constS", [128, 5 * 128], F32)
    maskA = sb("maskA", [128, 4, 512], BF16)
    identb = sb("identb", [128, 128], BF16)
    smalls = sb("smalls", [128, L * NSM], F32)
    cvec = sb("cvec", [128, 8], F32)
    wau = sb("wau", [16, 256], BF16)
    lcS = sb("lcS", [128, NB + 1, 8], F32)
    runS = sb("runS", [128, 8], F32)
    m0S = sb("m0S", [128, 8], F32)
    kmeta = sb("kmeta", [128, 8, 16], BF16)
    vmeta = sb("vmeta", [128, 8, 128], BF16)
    Sst = sb("Sst", [128, 2, 128], F32)
    Sbf = sb("Sbf", [128, 2, 128], BF16)
    U0 = sb("U0", [128, 258], F32)
    gbT = sb("gbT", [128, 3, NPB], F32)
    pbS = sb("pbS", [128, NPB], F32)
    bown = sb("bown", [128, NB + 1, 8], F32)
    bmeta = sb("bmeta", [128, 2, 8], F32)
    Gt = sb("Gt", [128, 4, 8], F32)
    split = sb("split", [128, 8, 67], BF16)
    ps = [nc.alloc_psum_tensor(f"ps{i}", [128, 512], F32) for i in range(8)]
    A = Arena(nc, nc.sbuf_bytes_remaining - 1280)

    ident_f = constS[:, 0:128]
    trin16 = constS[:, 128:256]
    trirev16 = constS[:, 256:384]
    trin1 = constS[:, 384:512]
    trif = constS[:, 512:640]
    PSK = lambda k: ("ps", k)

    def dma(q, out, in_, reads, writes):
        return S.op(q, lambda e: e.dma_start(out=out, in_=in_), reads, writes, dma=True)

    def mm(out, lhsT, rhs, start, stop, reads, writes, **kw):
        return S.op("pe", lambda e: e.matmul(out, lhsT=lhsT, rhs=rhs, start=start, stop=stop, **kw), reads, writes)

    def act(out, in_, func, reads, writes, bias=0.0, scale=1.0):
        return S.op("act", lambda e: e.activation(out=out, in_=in_, func=func, bias=bias, scale=scale), reads, writes)

    def tt(eng, out, in0, in1, op, reads, writes):
        return S.op(eng, lambda e: e.tensor_tensor(out=out, in0=in0, in1=in1, op=op), reads, writes)

    def ts(eng, out, in0, s1, op0, reads, writes, s2=None, op1=None):
        if op1 is None:
            return S.op(eng, lambda e: e.tensor_scalar(out=out, in0=in0, scalar1=s1, scalar2=None, op0=op0), reads, writes)
        return S.op(eng, lambda e: e.tensor_scalar(out=out, in0=in0, scalar1=s1, scalar2=s2, op0=op0, op1=op1), reads, writes)

    def stt(eng, out, in0, scalar, in1, op0, op1, reads, writes):
        return S.op(eng, lambda e: e.scalar_tensor_tensor(out=out, in0=in0, scalar=scalar, in1=in1, op0=op0, op1=op1), reads, writes)

    def cp(eng, out, in_, reads, writes):
        if eng == "act":
            return S.op("act", lambda e: e.copy(out=out, in_=in_), reads, writes)
        return S.op(eng, lambda e: e.tensor_copy(out=out, in_=in_), reads, writes)

    def memset(eng, ap, val, writes):
        return S.op(eng, lambda e: e.memset(ap, val), (), writes)

    def recip(out, in_, reads, writes):
        return S.op("dve", lambda e: e.reciprocal(out=out, in_=in_), reads, writes)

    def dump(name, ap, keys, shape):
        if dumps is None or name not in dumps:
            return
        d = nc.dram_tensor("dbg_" + name, list(shape), F32 if ap.dtype == F32 else BF16, kind="ExternalOutput").ap()
        dump_outs[name] = dma("sp", d, ap, list(keys), [("dbgd", name)])

    bank_rr = [0]

    def bank():
        b = bank_rr[0]
        bank_rr[0] = (b + 1) % 8
        return b

    mk_init = A.mark()
    maskAf, _ = A.alloc([128, 4 * 512], F32)
    dma("sp", constS[:, :], consts_d[:, 0:640], [], ["constS"])
    dma("sp", maskAf[:, :], consts_d[:, 640:640 + 2048], [], ["maskAf"])
    dma("sp", smalls[:, :], smalls_d[:, :], [], ["smalls"])
    dma("sp", cvec[:, :], cvec_d[:, :], [], ["cvec"])
    cp("dve", maskA[:, :, :], maskAf[:, :].rearrange("p (a b) -> p a b", a=4), ["maskAf"], ["maskA"])
    cp("dve", identb[:, :], ident_f, ["constS"], ["identb"])
    S.barrier()
    A.release(mk_init)
    memset("pool", split[:, :, :], 0.0, ["split"])
    memset("pool", lcS[:, :, :], 0.0, [("lcS", b_) for b_ in range(NB + 1)])
    memset("pool", kmeta[:, :, :], 1.0, ["kmeta"])
    memset("pool", vmeta[:, :, :], 1.0, ["vmeta"])
    dma("sp", hT[:, :, 0:16], metaT.rearrange("(c p) t -> p c t", p=128), [], [("hT", 0)])
    tiles = [(0, 0, 16, [0])]
    b = 1
    while b <= NB:
        nbk = min(4, NB - b + 1)
        tiles.append((len(tiles), 16 + (b - 1) * 128, nbk * 128, list(range(b, b + nbk))))
        b += nbk
    for (ti, c0, n, blks) in tiles[1:]:
        dma("sp", hT[:, :, c0:c0 + n], xT.rearrange("(c p) t -> p c t", p=128)[:, :, c0 - 16:c0 - 16 + n], [], [("hT", ti)])

    def bcol(blk):
        return (0, 16) if blk == 0 else (16 + (blk - 1) * 128, 128)

    def sm(l, off, w):
        return smalls[:, l * NSM + off:l * NSM + off + w]

    def norm_rstd(srcs, n, skeys, sqb, sqk, rstd, rstdk, nfeat):
        bk = bank()
        for i, s_ap in enumerate(srcs):
            j = i % 2
            act(sqb[j][:, 0:n], s_ap, AF.Square, list(skeys[i]), [sqk[j]])
            mm(ps[bk][:, 0:n], trif_ones, sqb[j][:, 0:n], i == 0, i == len(srcs) - 1, [sqk[j], "constS2"], [PSK(bk)])
        act(rstd[:, 0:n], ps[bk][:, 0:n], AF.Sqrt, [PSK(bk)], [rstdk], bias=epsb[:, 0:1], scale=1.0 / nfeat)
        recip(rstd[:, 0:n], rstd[:, 0:n], [rstdk], [rstdk])

    ones_t = sb("ones_f", [128, 128], F32)
    memset("dve", ones_t[:, :], 1.0, ["constS2"])
    trif_ones = ones_t[:, :]
    nones_t = sb("nones_f", [128, 128], F32)
    memset("dve", nones_t[:, :], -1.0, ["constS3"])
    epsb = sb("epsb", [128, 1], F32)
    memset("dve", epsb[:, :], EPS, ["epsb"])
    oneb = sb("oneb", [128, 1], F32)
    memset("dve", oneb[:, :], 1.0, ["oneb"])

    def load_w(dst, dstk, w2d, col0, ncols, kc=DC, dcol=0):
        return dma("pool", dst[:, 0:kc, dcol:dcol + ncols], w2d.rearrange("(c p) f -> p c f", p=128)[:, :, col0:col0 + ncols], [], [dstk])

    def prenorm(l, goff, ti, c0, n, hn, hnk, sqb, sqk, rstd, rstdk):
        norm_rstd([hT[:, c, c0:c0 + n] for c in range(DC)], n, [[("hT", ti)]] * DC, sqb, sqk, rstd, rstdk, D)
        for c in range(DC):
            stt("dve", hn[:, c, 0:n], hT[:, c, c0:c0 + n], sm(l, goff + c, 1), rstd[:, 0:n], ALU.mult, ALU.mult,
                [("hT", ti), rstdk, "smalls"], [hnk])

    def postnorm_residual(l, goff, ti, c0, n, oT, oTk, sqb, sqk, rstd, rstdk, tmp, tmpk, scale):
        norm_rstd([oT[:, c, 0:n] for c in range(DC)], n, [[oTk]] * DC, sqb, sqk, rstd, rstdk, D)
        for c in range(DC):
            j = c % 2
            stt("dve", tmp[j][:, 0:n], oT[:, c, 0:n], sm(l, goff + c, 1), rstd[:, 0:n], ALU.mult, ALU.mult, [oTk, rstdk, "smalls"], [tmpk[j]])
            stt("dve", hT[:, c, c0:c0 + n], tmp[j][:, 0:n], scale, hT[:, c, c0:c0 + n], ALU.mult, ALU.add, [tmpk[j], ("hT", ti)], [("hT", ti)])

    def ffn(l, which, shared=None, after_group=None):
        mk = A.mark()
        W = 528
        if shared is None:
            hn, hnk = A.alloc([128, DC, W], BF16)
            sq0, sqk0 = A.alloc([128, 512], F32)
            sq1, sqk1 = A.alloc([128, 512], F32)
            rstd, rstdk = A.alloc([128, W], F32)
            wb = [A.alloc([128, DC, 512], BF16) for _ in range(4)]
        else:
            hn, hnk, (sq0, sq1), (sqk0, sqk1), rstd, rstdk, wb = shared
        actT, actk = A.alloc([128, FC, W], BF16)
        sil0, silk0 = A.alloc([128, 512], F32)
        sil1, silk1 = A.alloc([128, 512], F32)
        oT, oTk = A.alloc([128, DC, W], F32)
        wd = [A.alloc([128, FC, 128], BF16) for _ in range(2)]
        sqb, sqk, sil, silk = [sq0, sq1], [sqk0, sqk1], [sil0, sil1], [silk0, silk1]
        gpre = 0 if which == 0 else 32
        gpost = 8 if which == 0 else 40
        wgu2 = w_gu[which][l]
        wdn2 = w_dn[which][l]
        wi = 0
        di = 0
        tgroups = [[tiles[1] + (0,), tiles[0] + (512,)]] + [[t + (0,)] for t in tiles[2:]]
        for grp in tgroups:
            for (ti, c0, n, blks, bo_) in grp:
                prenorm(l, gpre, ti, c0, n, hn[:, :, bo_:bo_ + n], (hnk, bo_), sqb, sqk, rstd[:, bo_:bo_ + n], (rstdk, bo_))
            for fp in range(FC // 2):
                wt, wk = wb[wi % len(wb)]
                wi += 1
                load_w(wt, wk, wgu2, fp * 256, 256, dcol=0)
                load_w(wt, wk, wgu2, FF + fp * 256, 256, dcol=256)
                for sub in range(2):
                    fi = fp * 2 + sub
                    for (ti, c0, n, blks, bo_) in grp:
                        bg, bu = bank(), bank()
                        for c in range(DC):
                            mm(ps[bg][:, 0:n], wt[:, c, sub * 128:(sub + 1) * 128], hn[:, c, bo_:bo_ + n], c == 0, c == DC - 1, [wk, (hnk, bo_)], [PSK(bg)])
                        for c in range(DC):
                            mm(ps[bu][:, 0:n], wt[:, c, 256 + sub * 128:256 + (sub + 1) * 128], hn[:, c, bo_:bo_ + n], c == 0, c == DC - 1,
                               [wk, (hnk, bo_)], [PSK(bu)])
                        j = fi % 2
                        act(sil[j][:, 0:n], ps[bg][:, 0:n], AF.Silu, [PSK(bg)], [silk[j]])
                        tt("dve", actT[:, fi, bo_:bo_ + n], sil[j][:, 0:n], ps[bu][:, 0:n], ALU.mult, [silk[j], PSK(bu)], [(actk, fi, bo_)])
            for jd in range(DC):
                wt, wk = wd[di % 2]
                di += 1
                dma("pool", wt[:, :, :], wdn2.rearrange("(fc p) d -> p fc d", p=128)[:, :, jd * 128:(jd + 1) * 128], [], [wk])
                for (ti, c0, n, blks, bo_) in grp:
                    bo = bank()
                    for fc in range(FC):
                        mm(ps[bo][:, 0:n], wt[:, fc, :], actT[:, fc, bo_:bo_ + n], fc == 0, fc == FC - 1, [wk, (actk, fc, bo_)], [PSK(bo)])
                    cp("act", oT[:, jd, bo_:bo_ + n], ps[bo][:, 0:n], [PSK(bo)], [(oTk, bo_)])
            for (ti, c0, n, blks, bo_) in grp:
                postnorm_residual(l, gpost, ti, c0, n, oT[:, :, bo_:bo_ + n], (oTk, bo_), sqb, sqk, rstd[:, bo_:bo_ + n], (rstdk, bo_), sil, silk, 0.5)
            if after_group is not None:
                after_group(grp)
        S.barrier()
        A.release(mk)

    def logsig_pos(x_ap, xk, out_ap, outk, n_rows):
        act(out_ap, x_ap, AF.Exp, [xk], [outk], scale=-1.0)
        act(out_ap, out_ap, AF.Ln, [outk], [outk], bias=oneb[0:n_rows, 0:1])

    def mixer(l):
        win = w_in[l]
        mk0 = A.mark()
        hn, hnkb = A.alloc([128, DC, 528], BF16)
        hnk = (hnkb, 0)
        sq0, sqk0 = A.alloc([128, 512], F32)
        sq1, sqk1 = A.alloc([128, 512], F32)
        rstd, rstdkb = A.alloc([128, 528], F32)
        rstdk = (rstdkb, 0)
        sqb, sqk = [sq0, sq1], [sqk0, sqk1]
        wb = [A.alloc([128, DC, 512], BF16) for _ in range(3)]
        wsm, wsmk = A.alloc([128, DC, 32], BF16)
        vtok, vtokk = A.alloc([128, 4, 512], BF16)
        Ltok, Ltokk = A.alloc([128, 4, 256], F32)
        E1, E1k = A.alloc([128, 2, 128], F32)
        E2, E2k = A.alloc([128, 2, 128], F32)
        gaT, gaTk = A.alloc([128, 512], BF16)
        xf, xfk = A.alloc([128, 8], F32)
        wcnt = [0]

        def nextw():
            w = wb[wcnt[0] % 3]
            wcnt[0] += 1
            return w

        load_w(wsm, wsmk, win, C_FF, 8, dcol=0)
        load_w(wsm, wsmk, win, C_GA, 16, dcol=8)
        dma("pool", wau[:, :], w_au[l], [], ["wau"])
        b_alpha = sm(l, 52, 256)
        b_f = sm(l, 308, 8)

        def gla_common(ti, c0, n, blks):
            bk = bank()
            for c in range(DC):
                mm(ps[bk][0:16, 0:n], wsm[:, c, 8:24], hn[:, c, 0:n], c == 0, c == DC - 1, [wsmk, hnk], [PSK(bk)])
            cp("act", gaT[0:16, 0:n], ps[bk][0:16, 0:n], [PSK(bk)], [gaTk])
            wt, wk = nextw()
            load_w(wt, wk, win, C_GV, 512)
            for bi, blk in enumerate(blks):
                bc, nb = bcol(blk)
                lo = bc - c0
                bk = bank()
                for c in range(DC):
                    mm(ps[bk][0:nb, 0:512], hn[:, c, lo:lo + nb], wt[:, c, :], c == 0, c == DC - 1, [wk, hnk], [PSK(bk)])
                cp("act", vtok[0:nb, bi, :], ps[bk][0:nb, 0:512], [PSK(bk)], [(vtokk, bi)])
                bk = bank()
                mm(ps[bk][0:nb, 0:256], gaT[0:16, lo:lo + nb], wau[:, :], True, True, [gaTk, "wau"], [PSK(bk)])
                tt("dve", Ltok[0:nb, bi, :], ps[bk][0:nb, 0:256], b_alpha[0:nb, :], ALU.add, [PSK(bk), "smalls"], [(Ltokk, bi)])
                logsig_pos(Ltok[0:nb, bi, :], (Ltokk, bi), Ltok[0:nb, bi, :], (Ltokk, bi), nb)

        def cb_exp(bi, nb, need_e2):
            for pr in range(2):
                bk = bank()
                mm(ps[bk][:, 0:nb], Ltok[0:nb, bi, pr * 128:(pr + 1) * 128], trin16[0:nb, 0:nb], True, True, [(Ltokk, bi), "constS"], [PSK(bk)])
                act(E1[:, pr, 0:nb], ps[bk][:, 0:nb], AF.Exp, [PSK(bk)], [(E1k, pr)])
                if need_e2:
                    act(E2[:, pr, 0:nb], ps[bk][:, 0:nb], AF.Exp, [PSK(bk)], [(E2k, pr)], scale=-1.0)

        rg = [[0, 1, 2, 3], [4, 5, 6, 7]]
        gathered = set()

        def gather_chunk(c_):
            if c_ in gathered or c_ < 0 or c_ >= NCH:
                return
            gathered.add(c_)
            for (P_, G_, nm) in ((PK[l][c_], GK[l][c_], "K"), (PV[l][c_], GV[l][c_], "V"), (PU[l][c_], GU[l][c_], "U")):
                S.op("pool", lambda e, P_=P_, G_=G_: e.collective_compute("AllGather", ALU.bypass, replica_groups=rg, ins=[P_.opt()], outs=[G_.opt()]),
                     [("P" + nm, l, c_)], [("G" + nm, l, c_)], dma=True, cc=True)

        mk1 = A.mark()
        ktile, ktilek = A.alloc([128, 8, 512], BF16)
        vt, vtk = A.alloc([128, 512], BF16)
        Lf, Lfk = A.alloc([128, 8], F32)
        ktk, ktkk = A.alloc([128, 256], BF16)
        Er, Erk = A.alloc([128, 256], F32)
        usb, usbk = A.alloc([128, 258], F32)
        memset("dve", runS[:, :], 0.0, ["runS"])

        def p1_tile(ti, c0, n, blks):
            prenorm(l, 16, ti, c0, n, hn, hnk, sqb, sqk, rstd, rstdk)
            wt, wk = nextw()
            load_w(wt, wk, win, C_FK, 512)
            for h in range(8):
                bk = bank()
                for c in range(DC):
                    mm(ps[bk][0:64, 0:n], wt[:, c, h * 64:(h + 1) * 64], hn[:, c, 0:n], c == 0, c == DC - 1, [wk, hnk], [PSK(bk)])
                cp("act" if h % 2 == 0 else "dve", ktile[0:64, h, 0:n], ps[bk][0:64, 0:n], [PSK(bk)], [ktilek])
            if ti == 0:
                cp("dve", kmeta[0:64, :, 0:16], ktile[0:64, :, 0:16], [ktilek], ["kmeta"])
            else:
                dma("sp", PK[l][ti - 1].rearrange("(h d) t -> d h t", h=8)[:, :, 0:n], ktile[0:64, :, 0:n], [ktilek], [("PK", l, ti - 1)])
            wt, wk = nextw()
            load_w(wt, wk, win, C_FV, 512)
            for bi, blk in enumerate(blks):
                bc, nb = bcol(blk)
                lo = bc - c0
                bk = bank()
                for c in range(DC):
                    mm(ps[bk][0:nb, 0:512], hn[:, c, lo:lo + nb], wt[:, c, :], c == 0, c == DC - 1, [wk, hnk], [PSK(bk)])
                if blk == 0:
                    cp("act", vmeta[0:16, :, 0:64], ps[bk][0:16, 0:512].rearrange("p (h d) -> p h d", h=8), [PSK(bk)], ["vmeta"])
                else:
                    cp("act", vt[0:nb, :], ps[bk][0:nb, 0:512], [PSK(bk)], [vtk])
                    dma("sp", PV[l][(blk - 1) // CH].rearrange("(h s) (b d) -> s h b d", h=8, b=CH)[:, :, (blk - 1) % CH, :],
                        vt[0:nb, :].rearrange("p (h d) -> p h d", h=8), [vtk], [("PV", l, (blk - 1) // CH)])
                bk = bank()
                for c in range(DC):
                    mm(ps[bk][0:nb, 0:8], hn[:, c, lo:lo + nb], wsm[:, c, 0:8], c == 0, c == DC - 1, [wsmk, hnk], [PSK(bk)])
                tt("dve", xf[0:nb, :], ps[bk][0:nb, 0:8], b_f[0:nb, :], ALU.add, [PSK(bk), "smalls"], [xfk])
                logsig_pos(xf[0:nb, :], xfk, Lf[0:nb, :], Lfk, nb)
                bk = bank()
                mm(ps[bk][0:nb, 0:8], trin1[0:nb, 0:nb], Lf[0:nb, :], True, True, [Lfk, "constS"], [PSK(bk)])
                mm(ps[bk][:, 8:16], nones_t[0:nb, :], Lf[0:nb, :], True, True, [Lfk, "constS3"], [PSK(bk)])
                if blk == 0:
                    cp("dve", lcS[0:nb, 0, :], ps[bk][0:nb, 0:8], [PSK(bk)], [("lcS", 0)])
                    cp("dve", m0S[:, :], ps[bk][:, 8:16], [PSK(bk)], ["m0S"])
                else:
                    tt("dve", lcS[0:nb, blk, :], ps[bk][0:nb, 0:8], runS[0:nb, :], ALU.add, [PSK(bk), "runS"], [("lcS", blk)])
                    tt("dve", runS[:, :], runS[:, :], ps[bk][:, 8:16], ALU.add, [PSK(bk), "runS"], ["runS"])
            gla_common(ti, c0, n, blks)
            wt, wk = nextw()
            load_w(wt, wk, win, C_GK, 256)
            for bi, blk in enumerate(blks):
                bc, nb = bcol(blk)
                lo = bc - c0
                cb_exp(bi, nb, False)
                bk = bank()
                for c in range(DC):
                    mm(ps[bk][0:nb, 0:256], hn[:, c, lo:lo + nb], wt[:, c, 0:256], c == 0, c == DC - 1, [wk, hnk], [PSK(bk)])
                bk2 = bank()
                mm(ps[bk2][0:nb, 0:256], trirev16[0:nb, 0:nb], Ltok[0:nb, bi, :], True, True, [(Ltokk, bi), "constS"], [PSK(bk2)])
                act(Er[0:nb, :], ps[bk2][0:nb, 0:256], AF.Exp, [PSK(bk2)], [Erk])
                tt("dve", ktk[0:nb, :], ps[bk][0:nb, 0:256], Er[0:nb, :], ALU.mult, [PSK(bk), Erk], [ktkk])
                bk = bank()
                for hh in range(4):
                    pr, hb = hh // 2, (hh % 2) * 64
                    mm(ps[bk][hb:hb + 64, pr * 128:(pr + 1) * 128], ktk[0:nb, hh * 64:(hh + 1) * 64], vtok[0:nb, bi, hh * 128:(hh + 1) * 128],
                       True, True, [ktkk, (vtokk, bi)], [PSK(bk)], tile_position=(0, hb))
                dst, dstk = (U0, "U0") if blk == 0 else (usb, usbk)
                cp("act", dst[:, 0:256], ps[bk][:, 0:256], [PSK(bk)], [dstk])
                cp("dve", dst[:, 256:258], E1[:, :, nb - 1], [(E1k, 0), (E1k, 1)], [dstk])
                if blk != 0:
                    dma("sp", PU[l][(blk - 1) // CH][:, ((blk - 1) % CH) * 258:((blk - 1) % CH + 1) * 258], usb[:, :], [usbk], [("PU", l, (blk - 1) // CH)])

        def after_group(grp):
            for t_ in sorted(grp, key=lambda t: t[0]):
                p1_tile(t_[0], t_[1], t_[2], t_[3])
                if t_[0] >= 1:
                    gather_chunk(t_[0] - 1)

        if stages is None or f"ffn0_{l}" in stages:
            ffn(l, 0, shared=(hn, hnkb, (sq0, sq1), (sqk0, sqk1), rstd, rstdkb, wb), after_group=after_group)
        else:
            for t_ in tiles:
                after_group([t_])
        for blk in range(1, NB + 1):
            tt("dve", pbS[:, (blk - 1) * 8:blk * 8], runS[:, :], lcS[:, blk, :], ALU.subtract, ["runS", ("lcS", blk)], ["pbS"])
        cp("dve", pbS[:, NB * 8:NB * 8 + 8], runS[:, :], ["runS"], ["pbS"])
        dma("sp", PB[l][:, :], pbS[:, :], ["pbS"], [("PB", l)])
        if mix_stop == "p1":
            S.barrier()
            A.release(mk0)
            return
        for c_ in range(NCH):
            gather_chunk(c_)
        S.op("pool", lambda e: e.collective_compute("AllGather", ALU.bypass, replica_groups=rg, ins=[PB[l].opt()], outs=[GB[l].opt()]),
             [("PB", l)], [("GB", l)], dma=True, cc=True)
        S.barrier()
        A.release(mk1)
        if mix_stop == "p2":
            S.barrier()
            A.release(mk0)
            return
        for i in range(3):
            dma("sp", gbT[:, i, :], GB[l][i * 128:(i + 1) * 128, :], [("GB", l)], ["gbT"])
        Tb = lambda i: gbT[:, i, NB * 8:NB * 8 + 8]
        e_ = lambda i: cvec[:, i:i + 1]
        vis = lambda i: cvec[:, 3 + i:4 + i]
        ts("dve", Gt[:, 2, :], Tb(2), e_(2), ALU.mult, ["gbT", "cvec"], ["Gt"])
        stt("dve", Gt[:, 0, :], Tb(1), e_(1), Gt[:, 2, :], ALU.mult, ALU.add, ["gbT", "cvec", "Gt"], ["Gt"])
        stt("dve", Gt[:, 3, :], Tb(0), e_(0), Gt[:, 0, :], ALU.mult, ALU.add, ["gbT", "cvec", "Gt"], ["Gt"])
        ts("dve", Gt[:, 1, :], Gt[:, 2, :], vis(1), ALU.add, ["Gt", "cvec"], ["Gt"])
        ts("dve", Gt[:, 0, :], Gt[:, 0, :], vis(0), ALU.add, ["Gt", "cvec"], ["Gt"])
        ts("dve", Gt[:, 2, :], Gt[:, 2, :], 0.0, ALU.mult, ["Gt"], ["Gt"], s2=vis(2), op1=ALU.add)
        for i in range(3):
            tt("dve", gbT[:, i, 0:NB * 8].rearrange("p (b h) -> p b h", h=8), gbT[:, i, 0:NB * 8].rearrange("p (b h) -> p b h", h=8),
               Gt[:, i, :].unsqueeze(1).to_broadcast([128, NB, 8]), ALU.add, ["gbT", "Gt"], ["gbT"])
        ts("dve", bown[:, :, :], lcS[:, :, :], -1.0, ALU.mult, [("lcS", b) for b in range(NB + 1)], ["bown"])
        tt("dve", bmeta[:, 0, :], m0S[:, :], lcS[:, 0, :], ALU.subtract, ["m0S", ("lcS", 0)], ["bmeta"])
        tt("dve", bmeta[:, 0, :], bmeta[:, 0, :], Gt[:, 3, :], ALU.add, ["bmeta", "Gt"], ["bmeta"])
        ts("dve", bmeta[:, 1, :], lcS[:, 0, :], -1.0, ALU.mult, [("lcS", 0)], ["bmeta"])
        mk2 = A.mark()
        ubig = [[A.alloc([128, CH * 258], F32) for _ in range(NCH)] for _ in range(3)]
        for i in range(3):
            for c_ in range(NCH):
                dma("sp", ubig[i][c_][0][:, :], GU[l][c_][i * 128:(i + 1) * 128, :], [("GU", l, c_)], [ubig[i][c_][1]])
        ome, omek = A.alloc([128, 3], F32)
        ts("dve", ome[:, :], cvec[:, 0:3], -1.0, ALU.mult, ["cvec"], [omek], s2=1.0, op1=ALU.add)
        dpa = [A.alloc([128, NB, 2], F32) for _ in range(3)]
        Fs = [A.alloc([128, 2, 128], F32) for _ in range(3)]
        Da = [A.alloc([128, 2], F32) for _ in range(3)]
        for i in range(3):
            for c_ in range(NCH):
                ubt, uk = ubig[i][c_]
                ts("dve", dpa[i][0][:, c_ * CH:(c_ + 1) * CH, :], ubt[:, :].rearrange("p (c w) -> p c w", c=CH)[:, :, 256:258], e_(i), ALU.mult,
                   [uk, "cvec"], [dpa[i][1]], s2=ome[:, i:i + 1], op1=ALU.add)
            memset("pool", Fs[i][0][:, :, :], 0.0, [Fs[i][1]])
            memset("pool", Da[i][0][:, :], 1.0, [Da[i][1]])
        for ck in range(NB):
            for i in range(3):
                tt("dve", Fs[i][0][:, :, :], Fs[i][0][:, :, :], dpa[i][0][:, ck, :].unsqueeze(2).to_broadcast([128, 2, 128]), ALU.mult,
                   [Fs[i][1], dpa[i][1]], [Fs[i][1]])
            for i in range(3):
                ubt, uk = ubig[i][ck // CH]
                u = ubt[:, (ck % CH) * 258:(ck % CH + 1) * 258]
                stt("dve", Fs[i][0][:, :, :], u[:, 0:256].rearrange("p (a v) -> p a v", a=2), e_(i), Fs[i][0][:, :, :], ALU.mult, ALU.add,
                    [uk, "cvec", Fs[i][1]], [Fs[i][1]])
            for i in range(3):
                tt("pool", Da[i][0][:, :], Da[i][0][:, :], dpa[i][0][:, ck, :], ALU.mult, [Da[i][1], dpa[i][1]], [Da[i][1]])
        cp("dve", Sst[:, :, :], U0[:, 0:256].rearrange("p (a v) -> p a v", a=2), ["U0"], ["Sst"])
        for i in range(3):
            tt("dve", Sst[:, :, :], Sst[:, :, :], Da[i][0][:, 0:2].unsqueeze(2).to_broadcast([128, 2, 128]), ALU.mult, ["Sst", Da[i][1]], ["Sst"])
            tt("dve", Sst[:, :, :], Sst[:, :, :], Fs[i][0][:, :, :], ALU.add, ["Sst", Fs[i][1]], ["Sst"])
        S.barrier()
        A.release(mk2)
        if mix_stop == "prep":
            S.barrier()
            A.release(mk0)
            return
        mk3 = A.mark()
        QT, QTk = A.alloc([128, 8, 512], BF16)
        OT, OTk = A.alloc([128, 4, 512], BF16)
        GT, GTk = A.alloc([128, 4, 512], BF16)
        sgr, sgrk = A.alloc([128, 4, 512], BF16)
        qg, qgk = A.alloc([128, 2, 512], BF16)
        kg, kgk = A.alloc([128, 2, 512], BF16)
        yT, yTk = A.alloc([128, DC, 512], BF16)
        oT, oTk = A.alloc([128, DC, 512], F32)
        memset("pool", QT[:, :, :], 0.0, [QTk])
        c3, c3k = A.alloc([128, 8], F32)
        r3, r3k = A.alloc([128, 8], F32)
        mk4 = A.mark()
        for (ti, c0, n, blks) in tiles:
            prenorm(l, 16, ti, c0, n, hn, hnk, sqb, sqk, rstd, rstdk)
            wt, wk = nextw()
            load_w(wt, wk, win, C_FQ, 512)
            for h in range(8):
                bk = bank()
                for c in range(DC):
                    mm(ps[bk][0:64, 0:n], wt[:, c, h * 64:(h + 1) * 64], hn[:, c, 0:n], c == 0, c == DC - 1, [wk, hnk], [PSK(bk)])
                act(QT[0:64, h, 0:n], ps[bk][0:64, 0:n], AF.Copy, [PSK(bk)], [QTk], scale=0.125)
            for bi, blk in enumerate(blks):
                bc, nb = bcol(blk)
                lo = bc - c0
                cp("dve", split[0:nb, :, 64], lcS[0:nb, blk, :], [("lcS", blk)], ["split"])
                tt("dve", r3[0:nb, :], lcS[0:nb, blk, :], split[0:nb, :, 64], ALU.subtract, [("lcS", blk), "split"], [r3k])
                cp("dve", split[0:nb, :, 65], r3[0:nb, :], [r3k], ["split"])
                tt("dve", c3[0:nb, :], r3[0:nb, :], split[0:nb, :, 65], ALU.subtract, [r3k, "split"], [c3k])
                cp("dve", split[0:nb, :, 66], c3[0:nb, :], [c3k], ["split"])
                for h in range(8):
                    bk = bank()
                    mm(ps[bk][0:67, 0:nb], split[0:nb, h, :], identb[0:nb, 0:nb], True, True, ["split", "identb"], [PSK(bk)])
                    cp("act" if h % 2 == 0 else "dve", QT[64:67, h, lo:lo + nb], ps[bk][64:67, 0:nb], [PSK(bk)], [QTk])
            if mix_stop == "q":
                continue
            gla_common(ti, c0, n, blks)
            wt, wk = nextw()
            load_w(wt, wk, win, C_GQ, 512)
            for pr in range(2):
                bk = bank()
                for c in range(DC):
                    mm(ps[bk][:, 0:n], wt[:, c, pr * 128:(pr + 1) * 128], hn[:, c, 0:n], c == 0, c == DC - 1, [wk, hnk], [PSK(bk)])
                cp("act", qg[:, pr, 0:n], ps[bk][:, 0:n], [PSK(bk)], [qgk])
                bk = bank()
                for c in range(DC):
                    mm(ps[bk][:, 0:n], wt[:, c, 256 + pr * 128:256 + (pr + 1) * 128], hn[:, c, 0:n], c == 0, c == DC - 1, [wk, hnk], [PSK(bk)])
                cp("dve", kg[:, pr, 0:n], ps[bk][:, 0:n], [PSK(bk)], [kgk])
            wt, wk = nextw()
            load_w(wt, wk, win, C_GR, 512)
            for hh in range(4):
                bk = bank()
                for c in range(DC):
                    mm(ps[bk][:, 0:n], wt[:, c, hh * 128:(hh + 1) * 128], hn[:, c, 0:n], c == 0, c == DC - 1, [wk, hnk], [PSK(bk)])
                act(sgr[:, hh, 0:n], ps[bk][:, 0:n], AF.Silu, [PSK(bk)], [sgrk])
            if mix_stop == "glaproj":
                continue
            mkf = A.mark()
            KTb = [A.alloc([128, SEG], BF16) for _ in range(2)]
            Vb = [A.alloc([128, NB, 128], BF16) for _ in range(2)]
            PT = [A.alloc([128, 512], BF16) for _ in range(5)]
            rec, reck = A.alloc([128, 512], F32)
            for (t_, k_) in KTb:
                memset("pool", t_[64:67, :], 1.0, [k_])
            for (t_, k_) in Vb:
                memset("pool", t_[:, :, 64:128], 1.0, [(k_, c_) for c_ in range(NCH)])
            LA = 3
            groups = []
            units = []
            for h in range(8):
                bx = 6 + (h % 2)
                hu = []
                if ti == 0:
                    hu.append(dict(kt=kmeta[0:67, h, 0:16], v=vmeta[0:16, h, :], nk=16, bias=bmeta[0:16, 1, h:h + 1], rk=["kmeta", "vmeta"],
                                   mask=maskA[0:16, 0, 0:16], q0=0, grp=None))
                else:
                    hu.append(dict(kt=kmeta[0:67, h, 0:16], v=vmeta[0:16, h, :], nk=16, bias=bmeta[0:16, 0, h:h + 1], rk=["kmeta", "vmeta"],
                                   mask=None, q0=0, grp=None))
                    for src in (3, 0, 1, 2):
                        nblk_src = NB if src < 3 else blks[-1]
                        gi = len(groups)
                        groups.append((h, src, nblk_src))
                        for kb in range(nblk_src):
                            u = dict(nk=128, q0=0, mask=None, grp=gi, kb=kb)
                            if src < 3:
                                u["bias"] = gbT[:, src, kb * 8 + h:kb * 8 + h + 1]
                            else:
                                u["bias"] = bown[:, kb + 1, h:h + 1]
                                r = (kb + 1) - blks[0]
                                if r >= 0:
                                    u["q0"] = r * 128
                                    u["mask"] = maskA[:, r, r * 128:n]
                            hu.append(u)
                for i_, u in enumerate(hu):
                    u["h"], u["bx"], u["first"], u["last"] = h, bx, i_ == 0, i_ == len(hu) - 1
                units += hu
            gbuf = {}

            def load_group(gi):
                if gi >= len(groups) or gi in gbuf:
                    return
                h, src, nblk_src = groups[gi]
                kt_, ktk_ = KTb[gi % 2]
                v_, vk_ = Vb[gi % 2]
                ncs = nblk_src // CH
                if src < 3:
                    ksrc = GKa[l][:, src * 512 + h * 64:src * 512 + (h + 1) * 64, :]
                    vsrc = GVa[l][:, src * 1024 + h * 128:src * 1024 + (h + 1) * 128, :]
                    kkeys = [("GK", l, c_) for c_ in range(ncs)]
                    vkeys = [("GV", l, c_) for c_ in range(ncs)]
                else:
                    ksrc = PKa[l][0:ncs, h * 64:(h + 1) * 64, :]
                    vsrc = PVa[l][0:ncs, h * 128:(h + 1) * 128, :]
                    kkeys = [("PK", l, c_) for c_ in range(ncs)]
                    vkeys = [("PV", l, c_) for c_ in range(ncs)]
                dma("sp", kt_[0:64, 0:ncs * CH * 128].rearrange("d (c t) -> d c t", c=ncs), ksrc.rearrange("c d t -> d c t"), kkeys, [ktk_])
                for c_ in range(ncs):
                    dma("sp", v_[:, c_ * CH:(c_ + 1) * CH, 0:64], vsrc[c_].rearrange("s (b d) -> s b d", b=CH), [vkeys[c_]], [(vk_, c_)])
                gbuf[gi] = (kt_, ktk_, v_, vk_)

            def stage_a(ui, u):
                if u["grp"] is not None:
                    load_group(u["grp"])
                    kt_, ktk_, v_, vk_ = gbuf[u["grp"]]
                    kb = u["kb"]
                    u["kt"], u["v"], u["rk"] = kt_[0:67, kb * 128:(kb + 1) * 128], v_[:, kb, :], [ktk_, (vk_, kb // CH)]
                nk, q0, h = u["nk"], u["q0"], u["h"]
                bs = ui % 6
                mm(ps[bs][0:nk, q0:n], u["kt"], QT[0:67, h, q0:n], True, True, u["rk"] + [QTk], [PSK(bs)])
                p_, pk_ = PT[ui % 5]
                act(p_[0:nk, q0:n], ps[bs][0:nk, q0:n], AF.Exp, [PSK(bs), "gbT", "bown", "bmeta"], [pk_], bias=u["bias"])
                if u["mask"] is not None:
                    tt("dve", p_[0:nk, q0:n], p_[0:nk, q0:n], u["mask"], ALU.mult, [pk_, "maskA"], [pk_])

            def stage_b(ui, u):
                nk, q0, h, bx = u["nk"], u["q0"], u["h"], u["bx"]
                p_, pk_ = PT[ui % 5]
                mm(ps[bx][:, q0:n], u["v"], p_[0:nk, q0:n], u["first"], u["last"], u["rk"] + [pk_], [PSK(bx)])
                if u["grp"] is not None and u["kb"] == 0:
                    load_group(u["grp"] + 1)
                if u["last"]:
                    recip(rec[0:64, 0:n], ps[bx][64:128, 0:n], [PSK(bx)], [reck])
                    hb = (h % 2) * 64
                    tt("dve", OT[hb:hb + 64, h // 2, 0:n], ps[bx][0:64, 0:n], rec[0:64, 0:n], ALU.mult, [PSK(bx), reck], [OTk])

            for i_ in range(len(units) + LA):
                if i_ < len(units):
                    stage_a(i_, units[i_])
                if i_ - LA >= 0:
                    stage_b(i_ - LA, units[i_ - LA])
            S.barrier()
            A.release(mkf)
            if mix_stop == "fox":
                continue
            mkg = A.mark()
            qp, qpk = A.alloc([128, 4, 128], BF16)
            kp, kpk = A.alloc([128, 4, 128], BF16)
            Sbh, _ = A.alloc([128, 4, 128], BF16)
            attm, attmk = A.alloc([128, 4, 128], BF16)
            sqg, sqgk = A.alloc([128, 512], F32)
            rsg, rsgk = A.alloc([128, 512], F32)
            t1, t1k = A.alloc([128, 512], F32)
            uc = [A.alloc([128, 258], F32) for _ in range(2)]
            for bi, blk in enumerate(blks):
                bc, nb = bcol(blk)
                lo = bc - c0
                cb_exp(bi, nb, True)
                for hh in range(4):
                    pr, hb = hh // 2, (hh % 2) * 64
                    stt("dve", qp[0:64, hh, 0:nb], qg[hb:hb + 64, pr, lo:lo + nb], 0.125, E1[hb:hb + 64, pr, 0:nb], ALU.mult, ALU.mult, [qgk, (E1k, pr)], [qpk])
                    tt("dve", kp[0:64, hh, 0:nb], kg[hb:hb + 64, pr, lo:lo + nb], E2[hb:hb + 64, pr, 0:nb], ALU.mult, [kgk, (E2k, pr)], [kpk])
                ba = bank()
                for hh in range(4):
                    mm(ps[ba][0:nb, hh * 128:hh * 128 + nb], kp[0:64, hh, 0:nb], qp[0:64, hh, 0:nb], True, True, [kpk, qpk], [PSK(ba)])
                tt("dve", attm[0:nb, :, 0:nb], ps[ba][0:nb, :].rearrange("p (a t) -> p a t", a=4)[:, :, 0:nb],
                   trif[0:nb, 0:nb].unsqueeze(1).to_broadcast([nb, 4, nb]), ALU.mult, [PSK(ba), "constS"], [attmk])
                if blk == 0:
                    memset("dve", Sbh[:, :, :], 0.0, ["Sbh"])
                else:
                    for hh in range(4):
                        pr, hb = hh // 2, (hh % 2) * 64
                        cp("dve", Sbh[0:64, hh, :], Sst[hb:hb + 64, pr, :], ["Sst"], ["Sbh"])
                bo = bank()
                for hh in range(4):
                    mm(ps[bo][:, hh * 128:hh * 128 + nb], Sbh[0:64, hh, :], qp[0:64, hh, 0:nb], True, False, ["Sbh", qpk], [PSK(bo)])
                    mm(ps[bo][:, hh * 128:hh * 128 + nb], vtok[0:nb, bi, hh * 128:(hh + 1) * 128], attm[0:nb, hh, 0:nb], False, True,
                       [(vtokk, bi), attmk], [PSK(bo)])
                if mix_stop == "gla2":
                    continue
                if blk != 0:
                    u, uk = uc[bi % 2]
                    dma("sp", u[:, :], PU[l][(blk - 1) // CH][:, ((blk - 1) % CH) * 258:((blk - 1) % CH + 1) * 258], [("PU", l, (blk - 1) // CH)], [uk])
                    for pr in range(2):
                        stt("dve", Sst[:, pr, :], Sst[:, pr, :], u[:, 256 + pr:257 + pr], u[:, pr * 128:(pr + 1) * 128], ALU.mult, ALU.add,
                            ["Sst", uk], ["Sst"])
                if mix_stop == "gla3":
                    continue
                o4 = ps[bo][:, :].rearrange("p (a t) -> p a t", a=4)[:, :, 0:nb]
                act(sqg[:, 0:4 * nb].rearrange("p (a t) -> p a t", a=4), o4, AF.Square, [PSK(bo)], [sqgk])
                bs = bank()
                mm(ps[bs][:, 0:4 * nb], ones_t[:, :], sqg[:, 0:4 * nb], True, True, [sqgk, "constS2"], [PSK(bs)])
                act(rsg[:, 0:4 * nb], ps[bs][:, 0:4 * nb], AF.Sqrt, [PSK(bs)], [rsgk], bias=epsb[:, 0:1], scale=1.0 / 128)
                recip(rsg[:, 0:4 * nb], rsg[:, 0:4 * nb], [rsgk], [rsgk])
                tt("dve", t1[:, 0:4 * nb].rearrange("p (a t) -> p a t", a=4), o4, rsg[:, 0:4 * nb].rearrange("p (a t) -> p a t", a=4), ALU.mult,
                   [PSK(bo), rsgk], [t1k])
                for hh in range(4):
                    stt("dve", GT[:, hh, lo:lo + nb], t1[:, hh * nb:(hh + 1) * nb], sm(l, 48 + hh, 1), sgr[:, hh, lo:lo + nb], ALU.mult, ALU.mult,
                        [t1k, "smalls", sgrk], [GTk])
            S.barrier()
            A.release(mkg)
            if mix_stop in ("gla", "gla2", "gla3"):
                continue
            mkm = A.mark()
            sga, sgak = A.alloc([128, 512], F32)
            sgbt, sgbk = A.alloc([128, 512], F32)
            t2, t2k = A.alloc([128, 512], F32)
            tm0, tmk0 = A.alloc([128, 512], F32)
            tm1, tmk1 = A.alloc([128, 512], F32)
            for cg in range(2):
                wa, wak = nextw()
                load_w(wa, wak, win, C_MA + cg * 512, 512)
                wbb, wbk = nextw()
                load_w(wbb, wbk, win, C_MB + cg * 512, 512)
                wo, wok = nextw()
                dma("pool", wo[:, 0:4, :], w_fo[l].rearrange("(c p) f -> p c f", p=128)[:, :, cg * 512:(cg + 1) * 512], [], [wok])
                dma("pool", wo[:, 4:8, :], w_go[l].rearrange("(c p) f -> p c f", p=128)[:, :, cg * 512:(cg + 1) * 512], [], [wok])
                for cc in range(4):
                    c = cg * 4 + cc
                    b1, b2, b3, b4 = bank(), bank(), bank(), bank()
                    for k in range(DC):
                        mm(ps[b1][:, 0:n], wa[:, k, cc * 128:(cc + 1) * 128], hn[:, k, 0:n], k == 0, k == DC - 1, [wak, hnk], [PSK(b1)])
                    for k in range(DC):
                        mm(ps[b2][:, 0:n], wbb[:, k, cc * 128:(cc + 1) * 128], hn[:, k, 0:n], k == 0, k == DC - 1, [wbk, hnk], [PSK(b2)])
                    for k in range(4):
                        mm(ps[b3][:, 0:n], wo[:, k, cc * 128:(cc + 1) * 128], OT[:, k, 0:n], k == 0, k == 3, [wok, OTk], [PSK(b3)])
                    for k in range(4):
                        mm(ps[b4][:, 0:n], wo[:, 4 + k, cc * 128:(cc + 1) * 128], GT[:, k, 0:n], k == 0, k == 3, [wok, GTk], [PSK(b4)])
                    act(sga[:, 0:n], ps[b1][:, 0:n], AF.Sigmoid, [PSK(b1)], [sgak])
                    act(sgbt[:, 0:n], ps[b2][:, 0:n], AF.Sigmoid, [PSK(b2)], [sgbk])
                    tt("dve", t2[:, 0:n], sga[:, 0:n], ps[b3][:, 0:n], ALU.mult, [sgak, PSK(b3)], [t2k])
                    tt("dve", sgbt[:, 0:n], sgbt[:, 0:n], ps[b4][:, 0:n], ALU.mult, [sgbk, PSK(b4)], [sgbk])
                    tt("pool", yT[:, c, 0:n], t2[:, 0:n], sgbt[:, 0:n], ALU.add, [t2k, sgbk], [(yTk, c)])
            for c2 in range(DC):
                if c2 % 4 == 0:
                    wo, wok = nextw()
                    load_w(wo, wok, w_out[l], c2 * 128, 512)
                bk = bank()
                for k in range(DC):
                    mm(ps[bk][:, 0:n], wo[:, k, (c2 % 4) * 128:(c2 % 4 + 1) * 128], yT[:, k, 0:n], k == 0, k == DC - 1, [wok, (yTk, k)], [PSK(bk)])
                cp("act", oT[:, c2, 0:n], ps[bk][:, 0:n], [PSK(bk)], [oTk])
            postnorm_residual(l, 24, ti, c0, n, oT, oTk, sqb, sqk, rstd, rstdk, [tm0, tm1], [tmk0, tmk1], 1.0)
            S.barrier()
            A.release(mkm)
        S.barrier()
        A.release(mk0)

    for l in range(L):
        if stages is None or f"mix_{l}" in stages:
            mixer(l)
        elif f"ffn0_{l}" in stages:
            ffn(l, 0)
        if stages is None or f"ffn1_{l}" in stages:
            ffn(l, 1)
    finals = []
    for (ti, c0, n, blks) in tiles[1:]:
        finals.append(dma("sp", outT.rearrange("(c p) t -> p c t", p=128)[:, :, c0 - 16:c0 - 16 + n], hT[:, :, c0:c0 + n], [("hT", ti)], [("outT", ti)]))
    if dumps:
        dump("hT", hT[:, :, :], [("hT", t[0]) for t in tiles], [128, DC, TL])
    S.emit(final_ops=finals + list(dump_outs.values()))
    return nc, S


def make_consts():
    s = np.arange(128)[:, None]
    t = np.arange(128)[None, :]
    le = (s <= t).astype(np.float32)
    c = np.zeros((128, 5 * 128 + 4 * 512), np.float32)
    c[:, 0:128] = np.eye(128, dtype=np.float32)
    c[:, 128:256] = le * (-1.0 / 16.0)
    c[:, 256:384] = (s > t).astype(np.float32) * (-1.0 / 16.0)
    c[:, 384:512] = -le
    c[:, 512:640] = le
    for r in range(4):
        m = np.zeros((128, 512), np.float32)
        for q in range(4):
            if q == r:
                m[:, q * 128:(q + 1) * 128] = le
            elif q > r:
                m[:, q * 128:(q + 1) * 128] = 1.0
        c[:, 640 + r * 512:640 + (r + 1) * 512] = m
    return c


def make_smalls(inp, L):
    sm = np.zeros((128, L * NSM), np.float32)
    names = ["g_pre_ffn1", "g_post_ffn1", "g_pre_mix", "g_post_mix", "g_pre_ffn2", "g_post_ffn2"]
    for l in range(L):
        o = l * NSM
        for i, nm in enumerate(names):
            sm[:, o + i * 8:o + (i + 1) * 8] = np.asarray(inp[nm], np.float32)[l].reshape(8, 128).T
        sm[:, o + 48:o + 52] = np.asarray(inp["g_gla_out"], np.float32)[l].reshape(4, 128).T
        sm[:, o + 52:o + 308] = np.asarray(inp["b_alpha"], np.float32)[l][None, :]
        sm[:, o + 308:o + 316] = np.asarray(inp["b_f"], np.float32)[l][None, :]
    return sm


def make_cvec(j):
    v = np.zeros((128, 8), np.float32)
    for i in range(3):
        v[:, i] = 1.0 if i < j else 0.0
        v[:, 3 + i] = 0.0 if i < j else -30000.0
    return v


_CACHE = {}


def run(inputs, NB, L=2, dumps=None, stages=None, mix_stop=None):
    key = (NB, L, tuple(dumps) if dumps else None)
    x = np.asarray(inputs["x"], np.float32)
    B, SEQ, _ = x.shape
    assert B == 2 and SEQ == 4 * NB * 128
    nc, S = build_program(NB, L, dumps, stages, mix_stop)
    consts = make_consts()
    smalls = make_smalls(inputs, L)
    metaT = np.ascontiguousarray(np.asarray(inputs["meta_tokens"], np.float32).T)
    shared = dict(consts=consts, smalls=smalls, metaT=metaT)
    for nm in ["w_in", "w_alpha_up", "w_fox_o", "w_gla_o", "w_out", "w_ffn1_gu", "w_ffn1_down", "w_ffn2_gu", "w_ffn2_down"]:
        shared[nm] = np.ascontiguousarray(np.asarray(inputs[nm], np.float32)[:L])
    in_maps = []
    SEG = NB * 128
    for core in range(8):
        b, j = core // 4, core % 4
        m = dict(shared)
        m["xT"] = np.ascontiguousarray(x[b, j * SEG:(j + 1) * SEG, :].T)
        m["cvec"] = make_cvec(j)
        in_maps.append(m)
    res = run_bass_kernel_spmd(nc, in_maps, core_ids=list(range(8)))
    out = np.zeros((B, SEQ, D), np.float32)
    for core in range(8):
        b, j = core // 4, core % 4
        out[b, j * SEG:(j + 1) * SEG, :] = np.asarray(res.results[core]["outT"]).T
    return out, res


def kernel(**inputs):
    out, _ = run(inputs, 16, 2)
    return out
```

```python
import numpy as np
import concourse.bass as bass
import concourse.mybir as mybir
from concourse.bass_utils import run_bass_kernel_spmd

F32 = mybir.dt.float32
BF16 = mybir.dt.bfloat16
ALU = mybir.AluOpType
AF = mybir.ActivationFunctionType

EPOCH = 24000
N_DMA_SEMS = 24
D = 1024
DC = 8
FF = 2816
FC = 22
NIN = 5144
C_FQ, C_FK, C_FV, C_FF, C_GQ, C_GK, C_GV, C_GA, C_GR, C_MA, C_MB = 0, 512, 1024, 1536, 1544, 1800, 2056, 2568, 2584, 3096, 4120
EPS = 1e-6
NSM = 316


class MK(tuple):
    pass


def _flat(keys):
    out = []
    for k in keys:
        if isinstance(k, MK):
            out.extend(k)
        else:
            out.append(k)
    return out


class Sched:
    ENGS = ("pe", "act", "dve", "pool", "sp")

    def __init__(self, nc):
        self.nc = nc
        self.ops = []
        self.last_write = {}
        self.readers = {}
        self.pending_barrier = None

    def op(self, eng, fn, reads=(), writes=(), dma=False, cc=False):
        reads = _flat(reads)
        writes = _flat(writes)
        deps = set()
        for r in reads:
            lw = self.last_write.get(r)
            if lw is not None:
                deps.add(lw)
        for w in writes:
            lw = self.last_write.get(w)
            if lw is not None:
                deps.add(lw)
            for rd in self.readers.get(w, ()):
                deps.add(rd)
        oid = len(self.ops)
        if self.pending_barrier is not None:
            pb = self.pending_barrier
            if eng not in pb["done"]:
                deps |= pb["deps"]
                pb["done"].add(eng)
        self.ops.append(dict(eng=eng, fn=fn, deps=deps, dma=dma, cc=cc))
        for r in reads:
            lst = self.readers.setdefault(r, [])
            if not dma:
                lst[:] = [x for x in lst if self.ops[x]["dma"] or self.ops[x]["eng"] != eng]
            lst.append(oid)
        for w in writes:
            self.last_write[w] = oid
            self.readers[w] = []
        return oid

    def barrier(self):
        deps = set()
        seen = set()
        nd = 0
        for i in range(len(self.ops) - 1, -1, -1):
            o = self.ops[i]
            if o["dma"]:
                if nd < N_DMA_SEMS:
                    deps.add(i)
                    nd += 1
            elif o["eng"] not in seen:
                seen.add(o["eng"])
                deps.add(i)
            if len(seen) == 5 and nd >= N_DMA_SEMS:
                break
        self.pending_barrier = dict(deps=deps, done=set())

    def emit(self, final_ops=()):
        nc = self.nc
        ops = self.ops
        n = len(ops)
        needs = [False] * n
        for o in ops:
            for d in o["deps"]:
                if ops[d]["eng"] == "pe" and o["eng"] == "pe" and not ops[d]["dma"] and not o["dma"]:
                    continue
                needs[d] = True
        for d in final_ops:
            needs[d] = True
        cnt = {e: 0 for e in self.ENGS}
        for i, o in enumerate(ops):
            if not o["dma"] and needs[i]:
                cnt[o["eng"]] += 1
        sems = {e: [nc.alloc_semaphore(f"s_{e}_{i}") for i in range(cnt[e] // EPOCH + 1)] for e in self.ENGS}
        dsems = [nc.alloc_semaphore(f"s_dma_{i}") for i in range(N_DMA_SEMS)]
        dcount = [0] * N_DMA_SEMS
        dlast = [None] * N_DMA_SEMS
        token = [None] * n
        ecount = {e: 0 for e in self.ENGS}
        dk = 0
        for i, o in enumerate(ops):
            if o["cc"]:
                token[i] = (nc.alloc_semaphore(f"s_cc_{i}"), 1, 1)
            elif o["dma"]:
                k = dk % N_DMA_SEMS
                dk += 1
                if dlast[k] is not None:
                    o["deps"].add(dlast[k])
                dcount[k] += 16
                token[i] = (dsems[k], dcount[k], 16)
                dlast[k] = i
            elif needs[i]:
                c = ecount[o["eng"]]
                ecount[o["eng"]] = c + 1
                token[i] = (sems[o["eng"]][c // EPOCH], c % EPOCH + 1, 1)
        streams = {e: [] for e in self.ENGS}
        for i, o in enumerate(ops):
            streams[o["eng"]].append(i)
        self.n_waits = 0

        def run_stream(e, eng):
            waited = {}
            for i in streams[e]:
                o = ops[i]
                for d in sorted(o["deps"]):
                    od = ops[d]
                    if e == "pe" and od["eng"] == "pe" and not od["dma"] and not o["dma"]:
                        continue
                    sem, val, _ = token[d]
                    key = id(sem)
                    if waited.get(key, 0) >= val:
                        continue
                    eng.wait_ge(sem, val)
                    self.n_waits += 1
                    waited[key] = val
                ins = o["fn"](eng)
                if token[i] is not None:
                    sem, val, step = token[i]
                    ins.then_inc(sem, step)
            if e == "sp":
                for d in final_ops:
                    sem, val, _ = token[d]
                    eng.wait_ge(sem, val)

        with nc.Block() as block:
            @block.tensor
            def _(eng):
                run_stream("pe", eng)

            @block.scalar
            def _(eng):
                run_stream("act", eng)

            @block.vector
            def _(eng):
                run_stream("dve", eng)

            @block.gpsimd
            def _(eng):
                run_stream("pool", eng)

            @block.sync
            def _(eng):
                run_stream("sp", eng)


class Arena:
    def __init__(self, nc, nbytes):
        self.t = nc.alloc_sbuf_tensor("arena", [128, nbytes // 2], BF16)
        self.n = nbytes // 2
        self.top = 0
        self.uid = 0

    def alloc(self, shape, dtype):
        free = 1
        for s in shape[1:]:
            free *= s
        ne = free * (2 if dtype == F32 else 1)
        ne = (ne + 15) // 16 * 16
        assert self.top + ne <= self.n, f"arena overflow {self.top + ne} > {self.n}"
        ap = self.t[:, self.top:self.top + ne]
        self.top += ne
        if dtype == F32:
            ap = ap.bitcast(F32)
        ap = ap[:, 0:free]
        if len(shape) == 3:
            ap = ap.rearrange("p (a b) -> p a b", a=shape[1])
        self.uid += 1
        return ap, ("ar", self.uid)

    def mark(self):
        return self.top

    def release(self, m):
        self.top = m


def build_program(NB, L=2, dumps=None, stages=None, mix_stop=None):
    nc = bass.Bass("TRN2", target_bir_lowering=False)
    SEG = NB * 128
    TL = 16 + SEG
    NPB = NB * 8 + 8
    NPU = NB * 258
    S = Sched(nc)

    def dram_in(name, shape, dt=F32):
        return nc.dram_tensor(name, shape, dt, kind="ExternalInput").ap()

    xT = dram_in("xT", [D, SEG])
    metaT = dram_in("metaT", [D, 16])
    cvec_d = dram_in("cvec", [128, 8])
    consts_d = dram_in("consts", [128, 5 * 128 + 4 * 512])
    smalls_d = dram_in("smalls", [128, L * NSM])
    w_in = dram_in("w_in", [L, D, NIN])
    w_au = dram_in("w_alpha_up", [L, 16, 256])
    w_fo = dram_in("w_fox_o", [L, 512, D])
    w_go = dram_in("w_gla_o", [L, 512, D])
    w_out = dram_in("w_out", [L, D, D])
    w_gu = [dram_in("w_ffn1_gu", [L, D, 2 * FF]), dram_in("w_ffn2_gu", [L, D, 2 * FF])]
    w_dn = [dram_in("w_ffn1_down", [L, FF, D]), dram_in("w_ffn2_down", [L, FF, D])]
    outT = nc.dram_tensor("outT", [D, SEG], F32, kind="ExternalOutput").ap()
    CH = min(4, NB)
    NCH = NB // CH
    PKa = [nc.dram_tensor(f"PKa{l}", [NCH, 512, CH * 128], BF16).ap() for l in range(L)]
    PVa = [nc.dram_tensor(f"PVa{l}", [NCH, 1024, CH * 64], BF16).ap() for l in range(L)]
    PK = [[PKa[l][c] for c in range(NCH)] for l in range(L)]
    PV = [[PVa[l][c] for c in range(NCH)] for l in range(L)]
    PB = [nc.dram_tensor(f"PB{l}", [128, NPB], F32).ap() for l in range(L)]
    PU = [[nc.dram_tensor(f"PU{l}_{c}", [128, CH * 258], F32).ap() for c in range(NCH)] for l in range(L)]
    GKa = [nc.dram_tensor(f"GKa{l}", [NCH, 4 * 512, CH * 128], BF16).ap() for l in range(L)]
    GVa = [nc.dram_tensor(f"GVa{l}", [NCH, 4 * 1024, CH * 64], BF16).ap() for l in range(L)]
    GK = [[GKa[l][c] for c in range(NCH)] for l in range(L)]
    GV = [[GVa[l][c] for c in range(NCH)] for l in range(L)]
    GB = [nc.dram_tensor(f"GB{l}", [4 * 128, NPB], F32).ap() for l in range(L)]
    GU = [[nc.dram_tensor(f"GU{l}_{c}", [4 * 128, CH * 258], F32).ap() for c in range(NCH)] for l in range(L)]
    dump_outs = {}

    def sb(name, shape, dt):
        return nc.alloc_sbuf_tensor("sb_" + name, shape, dt)

    hT = sb("hT", [128, DC, TL], F32)
    constS = sb("constS", [128, 5 * 128], F32)
    maskA = sb("maskA", [128, 4, 512], BF16)
    identb = sb("identb", [128, 128], BF16)
    smalls = sb("smalls", [128, L * NSM], F32)
    cvec = sb("cvec", [128, 8], F32)
    wau = sb("wau", [16, 256], BF16)
    lcS = sb("lcS", [128, NB + 1, 8], F32)
    runS = sb("runS", [128, 8], F32)
    m0S = sb("m0S", [128, 8], F32)
    kmeta = sb("kmeta", [128, 8, 16], BF16)
    vmeta = sb("vmeta", [128, 8, 128], BF16)
    Sst = sb("Sst", [128, 2, 128], F32)
    Sbf = sb("Sbf", [128, 2, 128], BF16)
    U0 = sb("U0", [128, 258], F32)
    gbT = sb("gbT", [128, 3, NPB], F32)
    pbS = sb("pbS", [128, NPB], F32)
    bown = sb("bown", [128, NB + 1, 8], F32)
    bmeta = sb("bmeta", [128, 2, 8], F32)
    Gt = sb("Gt", [128, 4, 8], F32)
    split = sb("split", [128, 8, 67], BF16)
    ps = [nc.alloc_psum_tensor(f"ps{i}", [128, 512], F32) for i in range(8)]
    A = Arena(nc, nc.sbuf_bytes_remaining - 1280)

    ident_f = constS[:, 0:128]
    trin16 = constS[:, 128:256]
    trirev16 = constS[:, 256:384]
    trin1 = constS[:, 384:512]
    trif = constS[:, 512:640]
    PSK = lambda k: ("ps", k)

    def dma(q, out, in_, reads, writes):
        return S.op(q, lambda e: e.dma_start(out=out, in_=in_), reads, writes, dma=True)

    def mm(out, lhsT, rhs, start, stop, reads, writes, **kw):
        return S.op("pe", lambda e: e.matmul(out, lhsT=lhsT, rhs=rhs, start=start, stop=stop, **kw), reads, writes)

    def act(out, in_, func, reads, writes, bias=0.0, scale=1.0):
        return S.op("act", lambda e: e.activation(out=out, in_=in_, func=func, bias=bias, scale=scale), reads, writes)

    def tt(eng, out, in0, in1, op, reads, writes):
        return S.op(eng, lambda e: e.tensor_tensor(out=out, in0=in0, in1=in1, op=op), reads, writes)

    def ts(eng, out, in0, s1, op0, reads, writes, s2=None, op1=None):
        if op1 is None:
            return S.op(eng, lambda e: e.tensor_scalar(out=out, in0=in0, scalar1=s1, scalar2=None, op0=op0), reads, writes)
        return S.op(eng, lambda e: e.tensor_scalar(out=out, in0=in0, scalar1=s1, scalar2=s2, op0=op0, op1=op1), reads, writes)

    def stt(eng, out, in0, scalar, in1, op0, op1, reads, writes):
        return S.op(eng, lambda e: e.scalar_tensor_tensor(out=out, in0=in0, scalar=scalar, in1=in1, op0=op0, op1=op1), reads, writes)

    def cp(eng, out, in_, reads, writes):
        if eng == "act":
            return S.op("act", lambda e: e.copy(out=out, in_=in_), reads, writes)
        return S.op(eng, lambda e: e.tensor_copy(out=out, in_=in_), reads, writes)

    def memset(eng, ap, val, writes):
        return S.op(eng, lambda e: e.memset(ap, val), (), writes)

    def recip(out, in_, reads, writes):
        return S.op("dve", lambda e: e.reciprocal(out=out, in_=in_), reads, writes)

    def dump(name, ap, keys, shape):
        if dumps is None or name not in dumps:
            return
        d = nc.dram_tensor("dbg_" + name, list(shape), F32 if ap.dtype == F32 else BF16, kind="ExternalOutput").ap()
        dump_outs[name] = dma("sp", d, ap, list(keys), [("dbgd", name)])

    bank_rr = [0]

    def bank():
        b = bank_rr[0]
        bank_rr[0] = (b + 1) % 8
        return b

    mk_init = A.mark()
    maskAf, _ = A.alloc([128, 4 * 512], F32)
    dma("sp", constS[:, :], consts_d[:, 0:640], [], ["constS"])
    dma("sp", maskAf[:, :], consts_d[:, 640:640 + 2048], [], ["maskAf"])
    dma("sp", smalls[:, :], smalls_d[:, :], [], ["smalls"])
    dma("sp", cvec[:, :], cvec_d[:, :], [], ["cvec"])
    cp("dve", maskA[:, :, :], maskAf[:, :].rearrange("p (a b) -> p a b", a=4), ["maskAf"], ["maskA"])
    cp("dve", identb[:, :], ident_f, ["constS"], ["identb"])
    S.barrier()
    A.release(mk_init)
    memset("pool", split[:, :, :], 0.0, ["split"])
    memset("pool", lcS[:, :, :], 0.0, [("lcS", b_) for b_ in range(NB + 1)])
    memset("pool", kmeta[:, :, :], 1.0, ["kmeta"])
    memset("pool", vmeta[:, :, :], 1.0, ["vmeta"])
    dma("sp", hT[:, :, 0:16], metaT.rearrange("(c p) t -> p c t", p=128), [], [("hT", 0)])
    tiles = [(0, 0, 16, [0])]
    b = 1
    while b <= NB:
        nbk = min(4, NB - b + 1)
        tiles.append((len(tiles), 16 + (b - 1) * 128, nbk * 128, list(range(b, b + nbk))))
        b += nbk
    for (ti, c0, n, blks) in tiles[1:]:
        dma("sp", hT[:, :, c0:c0 + n], xT.rearrange("(c p) t -> p c t", p=128)[:, :, c0 - 16:c0 - 16 + n], [], [("hT", ti)])

    def bcol(blk):
        return (0, 16) if blk == 0 else (16 + (blk - 1) * 128, 128)

    def sm(l, off, w):
        return smalls[:, l * NSM + off:l * NSM + off + w]

    def norm_rstd(srcs, n, skeys, sqb, sqk, rstd, rstdk, nfeat):
        bk = bank()
        for i, s_ap in enumerate(srcs):
            j = i % 2
            act(sqb[j][:, 0:n], s_ap, AF.Square, list(skeys[i]), [sqk[j]])
            mm(ps[bk][:, 0:n], trif_ones, sqb[j][:, 0:n], i == 0, i == len(srcs) - 1, [sqk[j], "constS2"], [PSK(bk)])
        act(rstd[:, 0:n], ps[bk][:, 0:n], AF.Sqrt, [PSK(bk)], [rstdk], bias=epsb[:, 0:1], scale=1.0 / nfeat)
        recip(rstd[:, 0:n], rstd[:, 0:n], [rstdk], [rstdk])

    ones_t = sb("ones_f", [128, 128], F32)
    memset("dve", ones_t[:, :], 1.0, ["constS2"])
    trif_ones = ones_t[:, :]
    nones_t = sb("nones_f", [128, 128], F32)
    memset("dve", nones_t[:, :], -1.0, ["constS3"])
    epsb = sb("epsb", [128, 1], F32)
    memset("dve", epsb[:, :], EPS, ["epsb"])
    oneb = sb("oneb", [128, 1], F32)
    memset("dve", oneb[:, :], 1.0, ["oneb"])

    def load_w(dst, dstk, w2d, col0, ncols, kc=DC, dcol=0):
        return dma("pool", dst[:, 0:kc, dcol:dcol + ncols], w2d.rearrange("(c p) f -> p c f", p=128)[:, :, col0:col0 + ncols], [], [dstk])

    def prenorm(l, goff, ti, c0, n, hn, hnk, sqb, sqk, rstd, rstdk):
        norm_rstd([hT[:, c, c0:c0 + n] for c in range(DC)], n, [[("hT", ti)]] * DC, sqb, sqk, rstd, rstdk, D)
        for c in range(DC):
            stt("dve", hn[:, c, 0:n], hT[:, c, c0:c0 + n], sm(l, goff + c, 1), rstd[:, 0:n], ALU.mult, ALU.mult,
                [("hT", ti), rstdk, "smalls"], [hnk])

    def postnorm_residual(l, goff, ti, c0, n, oT, oTk, sqb, sqk, rstd, rstdk, tmp, tmpk, scale):
        norm_rstd([oT[:, c, 0:n] for c in range(DC)], n, [[oTk]] * DC, sqb, sqk, rstd, rstdk, D)
        for c in range(DC):
            j = c % 2
            stt("dve", tmp[j][:, 0:n], oT[:, c, 0:n], sm(l, goff + c, 1), rstd[:, 0:n], ALU.mult, ALU.mult, [oTk, rstdk, "smalls"], [tmpk[j]])
            stt("dve", hT[:, c, c0:c0 + n], tmp[j][:, 0:n], scale, hT[:, c, c0:c0 + n], ALU.mult, ALU.add, [tmpk[j], ("hT", ti)], [("hT", ti)])

    def ffn(l, which, shared=None, after_group=None):
        mk = A.mark()
        W = 528
        if shared is None:
            hn, hnk = A.alloc([128, DC, W], BF16)
            sq0, sqk0 = A.alloc([128, 512], F32)
            sq1, sqk1 = A.alloc([128, 512], F32)
            rstd, rstdk = A.alloc([128, W], F32)
            wb = [A.alloc([128, DC, 512], BF16) for _ in range(4)]
        else:
            hn, hnk, (sq0, sq1), (sqk0, sqk1), rstd, rstdk, wb = shared
        actT, actk = A.alloc([128, FC, W], BF16)
        sil0, silk0 = A.alloc([128, 512], F32)
        sil1, silk1 = A.alloc([128, 512], F32)
        oT, oTk = A.alloc([128, DC, W], F32)
        wd = [A.alloc([128, FC, 128], BF16) for _ in range(2)]
        sqb, sqk, sil, silk = [sq0, sq1], [sqk0, sqk1], [sil0, sil1], [silk0, silk1]
        gpre = 0 if which == 0 else 32
        gpost = 8 if which == 0 else 40
        wgu2 = w_gu[which][l]
        wdn2 = w_dn[which][l]
        wi = 0
        di = 0
        tgroups = [[tiles[1] + (0,), tiles[0] + (512,)]] + [[t + (0,)] for t in tiles[2:]]
        for grp in tgroups:
            for (ti, c0, n, blks, bo_) in grp:
                prenorm(l, gpre, ti, c0, n, hn[:, :, bo_:bo_ + n], (hnk, bo_), sqb, sqk, rstd[:, bo_:bo_ + n], (rstdk, bo_))
            for fp in range(FC // 2):
                wt, wk = wb[wi % len(wb)]
                wi += 1
                load_w(wt, (wk, 0), wgu2, fp * 256, 256, dcol=0)
                load_w(wt, (wk, 1), wgu2, FF + fp * 256, 256, dcol=256)
                for sub in range(2):
                    fi = fp * 2 + sub
                    for (ti, c0, n, blks, bo_) in grp:
                        bg, bu = bank(), bank()
                        for c in range(DC):
                            mm(ps[bg][:, 0:n], wt[:, c, sub * 128:(sub + 1) * 128], hn[:, c, bo_:bo_ + n], c == 0, c == DC - 1, [(wk, 0), (hnk, bo_)], [PSK(bg)])
                        for c in range(DC):
                            mm(ps[bu][:, 0:n], wt[:, c, 256 + sub * 128:256 + (sub + 1) * 128], hn[:, c, bo_:bo_ + n], c == 0, c == DC - 1,
                               [(wk, 1), (hnk, bo_)], [PSK(bu)])
                        j = fi % 2
                        act(sil[j][:, 0:n], ps[bg][:, 0:n], AF.Silu, [PSK(bg)], [silk[j]])
                        tt("dve", actT[:, fi, bo_:bo_ + n], sil[j][:, 0:n], ps[bu][:, 0:n], ALU.mult, [silk[j], PSK(bu)], [(actk, fi, bo_)])
            for jd in range(DC):
                wt, wk = wd[di % 2]
                di += 1
                dma("pool", wt[:, :, :], wdn2.rearrange("(fc p) d -> p fc d", p=128)[:, :, jd * 128:(jd + 1) * 128], [], [wk])
                for (ti, c0, n, blks, bo_) in grp:
                    bo = bank()
                    for fc in range(FC):
                        mm(ps[bo][:, 0:n], wt[:, fc, :], actT[:, fc, bo_:bo_ + n], fc == 0, fc == FC - 1, [wk, (actk, fc, bo_)], [PSK(bo)])
                    cp("act", oT[:, jd, bo_:bo_ + n], ps[bo][:, 0:n], [PSK(bo)], [(oTk, bo_)])
            for (ti, c0, n, blks, bo_) in grp:
                postnorm_residual(l, gpost, ti, c0, n, oT[:, :, bo_:bo_ + n], (oTk, bo_), sqb, sqk, rstd[:, bo_:bo_ + n], (rstdk, bo_), sil, silk, 0.5)
            if after_group is not None:
                after_group(grp)
        S.barrier()
        A.release(mk)

    def logsig_pos(x_ap, xk, out_ap, outk, n_rows):
        act(out_ap, x_ap, AF.Exp, [xk], [outk], scale=-1.0)
        act(out_ap, out_ap, AF.Ln, [outk], [outk], bias=oneb[0:n_rows, 0:1])

    def mixer(l):
        win = w_in[l]
        mk0 = A.mark()
        hn, hnkb = A.alloc([128, DC, 528], BF16)
        hnk = (hnkb, 0)
        sq0, sqk0 = A.alloc([128, 512], F32)
        sq1, sqk1 = A.alloc([128, 512], F32)
        rstd, rstdkb = A.alloc([128, 528], F32)
        rstdk = (rstdkb, 0)
        sqb, sqk = [sq0, sq1], [sqk0, sqk1]
        wb = [A.alloc([128, DC, 512], BF16) for _ in range(3)]
        wsm, wsmk = A.alloc([128, DC, 32], BF16)
        vtok, vtokk = A.alloc([128, 4, 512], BF16)
        Ltok, Ltokk = A.alloc([128, 4, 256], F32)
        E1, E1k = A.alloc([128, 2, 128], F32)
        E2, E2k = A.alloc([128, 2, 128], F32)
        gaT, gaTk = A.alloc([128, 512], BF16)
        xf, xfk = A.alloc([128, 8], F32)
        wcnt = [0]

        def nextw():
            w = wb[wcnt[0] % 3]
            wcnt[0] += 1
            return w[0], MK(((w[1], 0), (w[1], 1)))

        load_w(wsm, wsmk, win, C_FF, 8, dcol=0)
        load_w(wsm, wsmk, win, C_GA, 16, dcol=8)
        dma("pool", wau[:, :], w_au[l], [], ["wau"])
        b_alpha = sm(l, 52, 256)
        b_f = sm(l, 308, 8)

        def gla_common(ti, c0, n, blks):
            bk = bank()
            for c in range(DC):
                mm(ps[bk][0:16, 0:n], wsm[:, c, 8:24], hn[:, c, 0:n], c == 0, c == DC - 1, [wsmk, hnk], [PSK(bk)])
            cp("act", gaT[0:16, 0:n], ps[bk][0:16, 0:n], [PSK(bk)], [gaTk])
            wt, wk = nextw()
            load_w(wt, wk, win, C_GV, 512)
            for bi, blk in enumerate(blks):
                bc, nb = bcol(blk)
                lo = bc - c0
                bk = bank()
                for c in range(DC):
                    mm(ps[bk][0:nb, 0:512], hn[:, c, lo:lo + nb], wt[:, c, :], c == 0, c == DC - 1, [wk, hnk], [PSK(bk)])
                cp("act", vtok[0:nb, bi, :], ps[bk][0:nb, 0:512], [PSK(bk)], [(vtokk, bi)])
                bk = bank()
                mm(ps[bk][0:nb, 0:256], gaT[0:16, lo:lo + nb], wau[:, :], True, True, [gaTk, "wau"], [PSK(bk)])
                tt("dve", Ltok[0:nb, bi, :], ps[bk][0:nb, 0:256], b_alpha[0:nb, :], ALU.add, [PSK(bk), "smalls"], [(Ltokk, bi)])
                logsig_pos(Ltok[0:nb, bi, :], (Ltokk, bi), Ltok[0:nb, bi, :], (Ltokk, bi), nb)

        def cb_exp(bi, nb, need_e2):
            for pr in range(2):
                bk = bank()
                mm(ps[bk][:, 0:nb], Ltok[0:nb, bi, pr * 128:(pr + 1) * 128], trin16[0:nb, 0:nb], True, True, [(Ltokk, bi), "constS"], [PSK(bk)])
                act(E1[:, pr, 0:nb], ps[bk][:, 0:nb], AF.Exp, [PSK(bk)], [(E1k, pr)])
                if need_e2:
                    act(E2[:, pr, 0:nb], ps[bk][:, 0:nb], AF.Exp, [PSK(bk)], [(E2k, pr)], scale=-1.0)

        rg = [[0, 1, 2, 3], [4, 5, 6, 7]]
        gathered = set()

        def gather_chunk(c_):
            if c_ in gathered or c_ < 0 or c_ >= NCH:
                return
            gathered.add(c_)
            for (P_, G_, nm) in ((PK[l][c_], GK[l][c_], "K"), (PV[l][c_], GV[l][c_], "V"), (PU[l][c_], GU[l][c_], "U")):
                S.op("pool", lambda e, P_=P_, G_=G_: e.collective_compute("AllGather", ALU.bypass, replica_groups=rg, ins=[P_.opt()], outs=[G_.opt()]),
                     [("P" + nm, l, c_)], [("G" + nm, l, c_)], dma=True, cc=True)

        mk1 = A.mark()
        ktile, ktilek = A.alloc([128, 8, 512], BF16)
        vt, vtk = A.alloc([128, 512], BF16)
        Lf, Lfk = A.alloc([128, 8], F32)
        ktk, ktkk = A.alloc([128, 256], BF16)
        Er, Erk = A.alloc([128, 256], F32)
        usb, usbk = A.alloc([128, 258], F32)
        memset("dve", runS[:, :], 0.0, ["runS"])

        def p1_tile(ti, c0, n, blks):
            prenorm(l, 16, ti, c0, n, hn, hnk, sqb, sqk, rstd, rstdk)
            wt, wk = nextw()
            load_w(wt, wk, win, C_FK, 512)
            for h in range(8):
                bk = bank()
                for c in range(DC):
                    mm(ps[bk][0:64, 0:n], wt[:, c, h * 64:(h + 1) * 64], hn[:, c, 0:n], c == 0, c == DC - 1, [wk, hnk], [PSK(bk)])
                cp("act" if h % 2 == 0 else "dve", ktile[0:64, h, 0:n], ps[bk][0:64, 0:n], [PSK(bk)], [ktilek])
            if ti == 0:
                cp("dve", kmeta[0:64, :, 0:16], ktile[0:64, :, 0:16], [ktilek], ["kmeta"])
            else:
                dma("sp", PK[l][ti - 1].rearrange("(h d) t -> d h t", h=8)[:, :, 0:n], ktile[0:64, :, 0:n], [ktilek], [("PK", l, ti - 1)])
            wt, wk = nextw()
            load_w(wt, wk, win, C_FV, 512)
            for bi, blk in enumerate(blks):
                bc, nb = bcol(blk)
                lo = bc - c0
                bk = bank()
                for c in range(DC):
                    mm(ps[bk][0:nb, 0:512], hn[:, c, lo:lo + nb], wt[:, c, :], c == 0, c == DC - 1, [wk, hnk], [PSK(bk)])
                if blk == 0:
                    cp("act", vmeta[0:16, :, 0:64], ps[bk][0:16, 0:512].rearrange("p (h d) -> p h d", h=8), [PSK(bk)], ["vmeta"])
                else:
                    cp("act", vt[0:nb, :], ps[bk][0:nb, 0:512], [PSK(bk)], [vtk])
                    dma("sp", PV[l][(blk - 1) // CH].rearrange("(h s) (b d) -> s h b d", h=8, b=CH)[:, :, (blk - 1) % CH, :],
                        vt[0:nb, :].rearrange("p (h d) -> p h d", h=8), [vtk], [("PV", l, (blk - 1) // CH)])
                bk = bank()
                for c in range(DC):
                    mm(ps[bk][0:nb, 0:8], hn[:, c, lo:lo + nb], wsm[:, c, 0:8], c == 0, c == DC - 1, [wsmk, hnk], [PSK(bk)])
                tt("dve", xf[0:nb, :], ps[bk][0:nb, 0:8], b_f[0:nb, :], ALU.add, [PSK(bk), "smalls"], [xfk])
                logsig_pos(xf[0:nb, :], xfk, Lf[0:nb, :], Lfk, nb)
                bk = bank()
                mm(ps[bk][0:nb, 0:8], trin1[0:nb, 0:nb], Lf[0:nb, :], True, True, [Lfk, "constS"], [PSK(bk)])
                mm(ps[bk][:, 8:16], nones_t[0:nb, :], Lf[0:nb, :], True, True, [Lfk, "constS3"], [PSK(bk)])
                if blk == 0:
                    cp("dve", lcS[0:nb, 0, :], ps[bk][0:nb, 0:8], [PSK(bk)], [("lcS", 0)])
                    cp("dve", m0S[:, :], ps[bk][:, 8:16], [PSK(bk)], ["m0S"])
                else:
                    tt("dve", lcS[0:nb, blk, :], ps[bk][0:nb, 0:8], runS[0:nb, :], ALU.add, [PSK(bk), "runS"], [("lcS", blk)])
                    tt("dve", runS[:, :], runS[:, :], ps[bk][:, 8:16], ALU.add, [PSK(bk), "runS"], ["runS"])
            gla_common(ti, c0, n, blks)
            wt, wk = nextw()
            load_w(wt, wk, win, C_GK, 256)
            for bi, blk in enumerate(blks):
                bc, nb = bcol(blk)
                lo = bc - c0
                cb_exp(bi, nb, False)
                bk = bank()
                for c in range(DC):
                    mm(ps[bk][0:nb, 0:256], hn[:, c, lo:lo + nb], wt[:, c, 0:256], c == 0, c == DC - 1, [wk, hnk], [PSK(bk)])
                bk2 = bank()
                mm(ps[bk2][0:nb, 0:256], trirev16[0:nb, 0:nb], Ltok[0:nb, bi, :], True, True, [(Ltokk, bi), "constS"], [PSK(bk2)])
                act(Er[0:nb, :], ps[bk2][0:nb, 0:256], AF.Exp, [PSK(bk2)], [Erk])
                tt("dve", ktk[0:nb, :], ps[bk][0:nb, 0:256], Er[0:nb, :], ALU.mult, [PSK(bk), Erk], [ktkk])
                bk = bank()
                for hh in range(4):
                    pr, hb = hh // 2, (hh % 2) * 64
                    mm(ps[bk][hb:hb + 64, pr * 128:(pr + 1) * 128], ktk[0:nb, hh * 64:(hh + 1) * 64], vtok[0:nb, bi, hh * 128:(hh + 1) * 128],
                       True, True, [ktkk, (vtokk, bi)], [PSK(bk)], tile_position=(0, hb))
                dst, dstk = (U0, "U0") if blk == 0 else (usb, usbk)
                cp("act", dst[:, 0:256], ps[bk][:, 0:256], [PSK(bk)], [dstk])
                cp("dve", dst[:, 256:258], E1[:, :, nb - 1], [(E1k, 0), (E1k, 1)], [dstk])
                if blk != 0:
                    dma("sp", PU[l][(blk - 1) // CH][:, ((blk - 1) % CH) * 258:((blk - 1) % CH + 1) * 258], usb[:, :], [usbk], [("PU", l, (blk - 1) // CH)])

        def after_group(grp):
            for t_ in sorted(grp, key=lambda t: t[0]):
                p1_tile(t_[0], t_[1], t_[2], t_[3])
                if t_[0] >= 1:
                    gather_chunk(t_[0] - 1)

        if stages is None or f"ffn0_{l}" in stages:
            ffn(l, 0, shared=(hn, hnkb, (sq0, sq1), (sqk0, sqk1), rstd, rstdkb, wb), after_group=after_group)
        else:
            for t_ in tiles:
                after_group([t_])
        for blk in range(1, NB + 1):
            tt("dve", pbS[:, (blk - 1) * 8:blk * 8], runS[:, :], lcS[:, blk, :], ALU.subtract, ["runS", ("lcS", blk)], ["pbS"])
        cp("dve", pbS[:, NB * 8:NB * 8 + 8], runS[:, :], ["runS"], ["pbS"])
        dma("sp", PB[l][:, :], pbS[:, :], ["pbS"], [("PB", l)])
        if mix_stop == "p1":
            S.barrier()
            A.release(mk0)
            return
        for c_ in range(NCH):
            gather_chunk(c_)
        S.op("pool", lambda e: e.collective_compute("AllGather", ALU.bypass, replica_groups=rg, ins=[PB[l].opt()], outs=[GB[l].opt()]),
             [("PB", l)], [("GB", l)], dma=True, cc=True)
        S.barrier()
        A.release(mk1)
        if mix_stop == "p2":
            S.barrier()
            A.release(mk0)
            return
        for i in range(3):
            dma("sp", gbT[:, i, :], GB[l][i * 128:(i + 1) * 128, :], [("GB", l)], ["gbT"])
        Tb = lambda i: gbT[:, i, NB * 8:NB * 8 + 8]
        e_ = lambda i: cvec[:, i:i + 1]
        vis = lambda i: cvec[:, 3 + i:4 + i]
        ts("dve", Gt[:, 2, :], Tb(2), e_(2), ALU.mult, ["gbT", "cvec"], ["Gt"])
        stt("dve", Gt[:, 0, :], Tb(1), e_(1), Gt[:, 2, :], ALU.mult, ALU.add, ["gbT", "cvec", "Gt"], ["Gt"])
        stt("dve", Gt[:, 3, :], Tb(0), e_(0), Gt[:, 0, :], ALU.mult, ALU.add, ["gbT", "cvec", "Gt"], ["Gt"])
        ts("dve", Gt[:, 1, :], Gt[:, 2, :], vis(1), ALU.add, ["Gt", "cvec"], ["Gt"])
        ts("dve", Gt[:, 0, :], Gt[:, 0, :], vis(0), ALU.add, ["Gt", "cvec"], ["Gt"])
        ts("dve", Gt[:, 2, :], Gt[:, 2, :], 0.0, ALU.mult, ["Gt"], ["Gt"], s2=vis(2), op1=ALU.add)
        for i in range(3):
            tt("dve", gbT[:, i, 0:NB * 8].rearrange("p (b h) -> p b h", h=8), gbT[:, i, 0:NB * 8].rearrange("p (b h) -> p b h", h=8),
               Gt[:, i, :].unsqueeze(1).to_broadcast([128, NB, 8]), ALU.add, ["gbT", "Gt"], ["gbT"])
        ts("dve", bown[:, :, :], lcS[:, :, :], -1.0, ALU.mult, [("lcS", b) for b in range(NB + 1)], ["bown"])
        tt("dve", bmeta[:, 0, :], m0S[:, :], lcS[:, 0, :], ALU.subtract, ["m0S", ("lcS", 0)], ["bmeta"])
        tt("dve", bmeta[:, 0, :], bmeta[:, 0, :], Gt[:, 3, :], ALU.add, ["bmeta", "Gt"], ["bmeta"])
        ts("dve", bmeta[:, 1, :], lcS[:, 0, :], -1.0, ALU.mult, [("lcS", 0)], ["bmeta"])
        mk2 = A.mark()
        ubig = [[A.alloc([128, CH * 258], F32) for _ in range(NCH)] for _ in range(3)]
        for i in range(3):
            for c_ in range(NCH):
                dma("sp", ubig[i][c_][0][:, :], GU[l][c_][i * 128:(i + 1) * 128, :], [("GU", l, c_)], [ubig[i][c_][1]])
        ome, omek = A.alloc([128, 3], F32)
        ts("dve", ome[:, :], cvec[:, 0:3], -1.0, ALU.mult, ["cvec"], [omek], s2=1.0, op1=ALU.add)
        dpa = [A.alloc([128, NB, 2], F32) for _ in range(3)]
        Fs = [A.alloc([128, 2, 128], F32) for _ in range(3)]
        Da = [A.alloc([128, 2], F32) for _ in range(3)]
        for i in range(3):
            for c_ in range(NCH):
                ubt, uk = ubig[i][c_]
                ts("dve", dpa[i][0][:, c_ * CH:(c_ + 1) * CH, :], ubt[:, :].rearrange("p (c w) -> p c w", c=CH)[:, :, 256:258], e_(i), ALU.mult,
                   [uk, "cvec"], [dpa[i][1]], s2=ome[:, i:i + 1], op1=ALU.add)
            memset("pool", Fs[i][0][:, :, :], 0.0, [Fs[i][1]])
            memset("pool", Da[i][0][:, :], 1.0, [Da[i][1]])
        for ck in range(NB):
            for i in range(3):
                tt("dve", Fs[i][0][:, :, :], Fs[i][0][:, :, :], dpa[i][0][:, ck, :].unsqueeze(2).to_broadcast([128, 2, 128]), ALU.mult,
                   [Fs[i][1], dpa[i][1]], [Fs[i][1]])
            for i in range(3):
                ubt, uk = ubig[i][ck // CH]
                u = ubt[:, (ck % CH) * 258:(ck % CH + 1) * 258]
                stt("dve", Fs[i][0][:, :, :], u[:, 0:256].rearrange("p (a v) -> p a v", a=2), e_(i), Fs[i][0][:, :, :], ALU.mult, ALU.add,
                    [uk, "cvec", Fs[i][1]], [Fs[i][1]])
            for i in range(3):
                tt("pool", Da[i][0][:, :], Da[i][0][:, :], dpa[i][0][:, ck, :], ALU.mult, [Da[i][1], dpa[i][1]], [Da[i][1]])
        cp("dve", Sst[:, :, :], U0[:, 0:256].rearrange("p (a v) -> p a v", a=2), ["U0"], ["Sst"])
        for i in range(3):
            tt("dve", Sst[:, :, :], Sst[:, :, :], Da[i][0][:, 0:2].unsqueeze(2).to_broadcast([128, 2, 128]), ALU.mult, ["Sst", Da[i][1]], ["Sst"])
            tt("dve", Sst[:, :, :], Sst[:, :, :], Fs[i][0][:, :, :], ALU.add, ["Sst", Fs[i][1]], ["Sst"])
        S.barrier()
        A.release(mk2)
        if mix_stop == "prep":
            S.barrier()
            A.release(mk0)
            return
        mk3 = A.mark()
        QT, QTk = A.alloc([128, 8, 512], BF16)
        OT, OTk = A.alloc([128, 4, 512], BF16)
        GT, GTk = A.alloc([128, 4, 512], BF16)
        sgr, sgrk = A.alloc([128, 4, 512], BF16)
        qg, qgk = A.alloc([128, 2, 512], BF16)
        kg, kgk = A.alloc([128, 2, 512], BF16)
        yT, yTk = A.alloc([128, DC, 512], BF16)
        oT, oTk = A.alloc([128, DC, 512], F32)
        memset("pool", QT[:, :, :], 0.0, [QTk])
        c3, c3k = A.alloc([128, 8], F32)
        r3, r3k = A.alloc([128, 8], F32)
        mk4 = A.mark()
        for (ti, c0, n, blks) in tiles:
            prenorm(l, 16, ti, c0, n, hn, hnk, sqb, sqk, rstd, rstdk)
            wt, wk = nextw()
            load_w(wt, wk, win, C_FQ, 512)
            for h in range(8):
                bk = bank()
                for c in range(DC):
                    mm(ps[bk][0:64, 0:n], wt[:, c, h * 64:(h + 1) * 64], hn[:, c, 0:n], c == 0, c == DC - 1, [wk, hnk], [PSK(bk)])
                act(QT[0:64, h, 0:n], ps[bk][0:64, 0:n], AF.Copy, [PSK(bk)], [QTk], scale=0.125)
            for bi, blk in enumerate(blks):
                bc, nb = bcol(blk)
                lo = bc - c0
                cp("dve", split[0:nb, :, 64], lcS[0:nb, blk, :], [("lcS", blk)], ["split"])
                tt("dve", r3[0:nb, :], lcS[0:nb, blk, :], split[0:nb, :, 64], ALU.subtract, [("lcS", blk), "split"], [r3k])
                cp("dve", split[0:nb, :, 65], r3[0:nb, :], [r3k], ["split"])
                tt("dve", c3[0:nb, :], r3[0:nb, :], split[0:nb, :, 65], ALU.subtract, [r3k, "split"], [c3k])
                cp("dve", split[0:nb, :, 66], c3[0:nb, :], [c3k], ["split"])
                for h in range(8):
                    bk = bank()
                    mm(ps[bk][0:67, 0:nb], split[0:nb, h, :], identb[0:nb, 0:nb], True, True, ["split", "identb"], [PSK(bk)])
                    cp("act" if h % 2 == 0 else "dve", QT[64:67, h, lo:lo + nb], ps[bk][64:67, 0:nb], [PSK(bk)], [QTk])
            if mix_stop == "q":
                continue
            gla_common(ti, c0, n, blks)
            wt, wk = nextw()
            load_w(wt, wk, win, C_GQ, 512)
            for pr in range(2):
                bk = bank()
                for c in range(DC):
                    mm(ps[bk][:, 0:n], wt[:, c, pr * 128:(pr + 1) * 128], hn[:, c, 0:n], c == 0, c == DC - 1, [wk, hnk], [PSK(bk)])
                cp("act", qg[:, pr, 0:n], ps[bk][:, 0:n], [PSK(bk)], [qgk])
                bk = bank()
                for c in range(DC):
                    mm(ps[bk][:, 0:n], wt[:, c, 256 + pr * 128:256 + (pr + 1) * 128], hn[:, c, 0:n], c == 0, c == DC - 1, [wk, hnk], [PSK(bk)])
                cp("dve", kg[:, pr, 0:n], ps[bk][:, 0:n], [PSK(bk)], [kgk])
            wt, wk = nextw()
            load_w(wt, wk, win, C_GR, 512)
            for hh in range(4):
                bk = bank()
                for c in range(DC):
                    mm(ps[bk][:, 0:n], wt[:, c, hh * 128:(hh + 1) * 128], hn[:, c, 0:n], c == 0, c == DC - 1, [wk, hnk], [PSK(bk)])
                act(sgr[:, hh, 0:n], ps[bk][:, 0:n], AF.Silu, [PSK(bk)], [sgrk])
            if mix_stop == "glaproj":
                continue
            mkf = A.mark()
            KTb = [A.alloc([128, SEG], BF16) for _ in range(2)]
            Vb = [A.alloc([128, NB, 128], BF16) for _ in range(2)]
            PT = [A.alloc([128, 512], BF16) for _ in range(5)]
            rec, reck = A.alloc([128, 512], F32)
            for (t_, k_) in KTb:
                memset("pool", t_[64:67, :], 1.0, [k_])
            for (t_, k_) in Vb:
                memset("pool", t_[:, :, 64:128], 1.0, [(k_, c_) for c_ in range(NCH)])
            LA = 3
            groups = []
            units = []
            for h in range(8):
                bx = 6 + (h % 2)
                hu = []
                if ti == 0:
                    hu.append(dict(kt=kmeta[0:67, h, 0:16], v=vmeta[0:16, h, :], nk=16, bias=bmeta[0:16, 1, h:h + 1], rk=["kmeta", "vmeta"],
                                   mask=maskA[0:16, 0, 0:16], q0=0, grp=None))
                else:
                    hu.append(dict(kt=kmeta[0:67, h, 0:16], v=vmeta[0:16, h, :], nk=16, bias=bmeta[0:16, 0, h:h + 1], rk=["kmeta", "vmeta"],
                                   mask=None, q0=0, grp=None))
                    for src in (3, 0, 1, 2):
                        nblk_src = NB if src < 3 else blks[-1]
                        gi = len(groups)
                        groups.append((h, src, nblk_src))
                        for kb in range(nblk_src):
                            u = dict(nk=128, q0=0, mask=None, grp=gi, kb=kb)
                            if src < 3:
                                u["bias"] = gbT[:, src, kb * 8 + h:kb * 8 + h + 1]
                            else:
                                u["bias"] = bown[:, kb + 1, h:h + 1]
                                r = (kb + 1) - blks[0]
                                if r >= 0:
                                    u["q0"] = r * 128
                                    u["mask"] = maskA[:, r, r * 128:n]
                            hu.append(u)
                for i_, u in enumerate(hu):
                    u["h"], u["bx"], u["first"], u["last"] = h, bx, i_ == 0, i_ == len(hu) - 1
                units += hu
            gbuf = {}

            def load_group(gi):
                if gi >= len(groups) or gi in gbuf:
                    return
                h, src, nblk_src = groups[gi]
                kt_, ktk_ = KTb[gi % 2]
                v_, vk_ = Vb[gi % 2]
                ncs = nblk_src // CH
                if src < 3:
                    ksrc = GKa[l][:, src * 512 + h * 64:src * 512 + (h + 1) * 64, :]
                    vsrc = GVa[l][:, src * 1024 + h * 128:src * 1024 + (h + 1) * 128, :]
                    kkeys = [("GK", l, c_) for c_ in range(ncs)]
                    vkeys = [("GV", l, c_) for c_ in range(ncs)]
                else:
                    ksrc = PKa[l][0:ncs, h * 64:(h + 1) * 64, :]
                    vsrc = PVa[l][0:ncs, h * 128:(h + 1) * 128, :]
                    kkeys = [("PK", l, c_) for c_ in range(ncs)]
                    vkeys = [("PV", l, c_) for c_ in range(ncs)]
                dma("sp", kt_[0:64, 0:ncs * CH * 128].rearrange("d (c t) -> d c t", c=ncs), ksrc.rearrange("c d t -> d c t"), kkeys, [ktk_])
                for c_ in range(ncs):
                    dma("sp", v_[:, c_ * CH:(c_ + 1) * CH, 0:64], vsrc[c_].rearrange("s (b d) -> s b d", b=CH), [vkeys[c_]], [(vk_, c_)])
                gbuf[gi] = (kt_, ktk_, v_, vk_)

            def stage_a(ui, u):
                if u["grp"] is not None:
                    load_group(u["grp"])
                    kt_, ktk_, v_, vk_ = gbuf[u["grp"]]
                    kb = u["kb"]
                    u["kt"], u["v"], u["rk"] = kt_[0:67, kb * 128:(kb + 1) * 128], v_[:, kb, :], [ktk_, (vk_, kb // CH)]
                nk, q0, h = u["nk"], u["q0"], u["h"]
                bs = ui % 6
                mm(ps[bs][0:nk, q0:n], u["kt"], QT[0:67, h, q0:n], True, True, u["rk"] + [QTk], [PSK(bs)])
                p_, pk_ = PT[ui % 5]
                act(p_[0:nk, q0:n], ps[bs][0:nk, q0:n], AF.Exp, [PSK(bs), "gbT", "bown", "bmeta"], [pk_], bias=u["bias"])
                if u["mask"] is not None:
                    tt("dve", p_[0:nk, q0:n], p_[0:nk, q0:n], u["mask"], ALU.mult, [pk_, "maskA"], [pk_])

            def stage_b(ui, u):
                nk, q0, h, bx = u["nk"], u["q0"], u["h"], u["bx"]
                p_, pk_ = PT[ui % 5]
                mm(ps[bx][:, q0:n], u["v"], p_[0:nk, q0:n], u["first"], u["last"], u["rk"] + [pk_], [PSK(bx)])
                if u["grp"] is not None and u["kb"] == 0:
                    load_group(u["grp"] + 1)
                if u["last"]:
                    recip(rec[0:64, 0:n], ps[bx][64:128, 0:n], [PSK(bx)], [reck])
                    hb = (h % 2) * 64
                    tt("dve", OT[hb:hb + 64, h // 2, 0:n], ps[bx][0:64, 0:n], rec[0:64, 0:n], ALU.mult, [PSK(bx), reck], [OTk])

            for i_ in range(len(units) + LA):
                if i_ < len(units):
                    stage_a(i_, units[i_])
                if i_ - LA >= 0:
                    stage_b(i_ - LA, units[i_ - LA])
            S.barrier()
            A.release(mkf)
            if mix_stop == "fox":
                continue
            mkg = A.mark()
            qp, qpk = A.alloc([128, 4, 128], BF16)
            kp, kpk = A.alloc([128, 4, 128], BF16)
            Sbh, _ = A.alloc([128, 4, 128], BF16)
            attm, attmk = A.alloc([128, 4, 128], BF16)
            sqg, sqgk = A.alloc([128, 512], F32)
            rsg, rsgk = A.alloc([128, 512], F32)
            t1, t1k = A.alloc([128, 512], F32)
            uc = [A.alloc([128, 258], F32) for _ in range(2)]
            for bi, blk in enumerate(blks):
                bc, nb = bcol(blk)
                lo = bc - c0
                cb_exp(bi, nb, True)
                for hh in range(4):
                    pr, hb = hh // 2, (hh % 2) * 64
                    stt("dve", qp[0:64, hh, 0:nb], qg[hb:hb + 64, pr, lo:lo + nb], 0.125, E1[hb:hb + 64, pr, 0:nb], ALU.mult, ALU.mult, [qgk, (E1k, pr)], [qpk])
                    tt("dve", kp[0:64, hh, 0:nb], kg[hb:hb + 64, pr, lo:lo + nb], E2[hb:hb + 64, pr, 0:nb], ALU.mult, [kgk, (E2k, pr)], [kpk])
                ba = bank()
                for hh in range(4):
                    mm(ps[ba][0:nb, hh * 128:hh * 128 + nb], kp[0:64, hh, 0:nb], qp[0:64, hh, 0:nb], True, True, [kpk, qpk], [PSK(ba)])
                tt("dve", attm[0:nb, :, 0:nb], ps[ba][0:nb, :].rearrange("p (a t) -> p a t", a=4)[:, :, 0:nb],
                   trif[0:nb, 0:nb].unsqueeze(1).to_broadcast([nb, 4, nb]), ALU.mult, [PSK(ba), "constS"], [attmk])
                if blk == 0:
                    memset("dve", Sbh[:, :, :], 0.0, ["Sbh"])
                else:
                    for hh in range(4):
                        pr, hb = hh // 2, (hh % 2) * 64
                        cp("dve", Sbh[0:64, hh, :], Sst[hb:hb + 64, pr, :], ["Sst"], ["Sbh"])
                bo = bank()
                for hh in range(4):
                    mm(ps[bo][:, hh * 128:hh * 128 + nb], Sbh[0:64, hh, :], qp[0:64, hh, 0:nb], True, False, ["Sbh", qpk], [PSK(bo)])
                    mm(ps[bo][:, hh * 128:hh * 128 + nb], vtok[0:nb, bi, hh * 128:(hh + 1) * 128], attm[0:nb, hh, 0:nb], False, True,
                       [(vtokk, bi), attmk], [PSK(bo)])
                if mix_stop == "gla2":
                    continue
                if blk != 0:
                    u, uk = uc[bi % 2]
                    dma("sp", u[:, :], PU[l][(blk - 1) // CH][:, ((blk - 1) % CH) * 258:((blk - 1) % CH + 1) * 258], [("PU", l, (blk - 1) // CH)], [uk])
                    for pr in range(2):
                        stt("dve", Sst[:, pr, :], Sst[:, pr, :], u[:, 256 + pr:257 + pr], u[:, pr * 128:(pr + 1) * 128], ALU.mult, ALU.add,
                            ["Sst", uk], ["Sst"])
                if mix_stop == "gla3":
                    continue
                o4 = ps[bo][:, :].rearrange("p (a t) -> p a t", a=4)[:, :, 0:nb]
                act(sqg[:, 0:4 * nb].rearrange("p (a t) -> p a t", a=4), o4, AF.Square, [PSK(bo)], [sqgk])
                bs = bank()
                mm(ps[bs][:, 0:4 * nb], ones_t[:, :], sqg[:, 0:4 * nb], True, True, [sqgk, "constS2"], [PSK(bs)])
                act(rsg[:, 0:4 * nb], ps[bs][:, 0:4 * nb], AF.Sqrt, [PSK(bs)], [rsgk], bias=epsb[:, 0:1], scale=1.0 / 128)
                recip(rsg[:, 0:4 * nb], rsg[:, 0:4 * nb], [rsgk], [rsgk])
                tt("dve", t1[:, 0:4 * nb].rearrange("p (a t) -> p a t", a=4), o4, rsg[:, 0:4 * nb].rearrange("p (a t) -> p a t", a=4), ALU.mult,
                   [PSK(bo), rsgk], [t1k])
                for hh in range(4):
                    stt("dve", GT[:, hh, lo:lo + nb], t1[:, hh * nb:(hh + 1) * nb], sm(l, 48 + hh, 1), sgr[:, hh, lo:lo + nb], ALU.mult, ALU.mult,
                        [t1k, "smalls", sgrk], [GTk])
            S.barrier()
            A.release(mkg)
            if mix_stop in ("gla", "gla2", "gla3"):
                continue
            mkm = A.mark()
            sga, sgak = A.alloc([128, 512], F32)
            sgbt, sgbk = A.alloc([128, 512], F32)
            t2, t2k = A.alloc([128, 512], F32)
            tm0, tmk0 = A.alloc([128, 512], F32)
            tm1, tmk1 = A.alloc([128, 512], F32)
            for cg in range(2):
                wa, wak = nextw()
                load_w(wa, wak, win, C_MA + cg * 512, 512)
                wbb, wbk = nextw()
                load_w(wbb, wbk, win, C_MB + cg * 512, 512)
                wo, wok = nextw()
                dma("pool", wo[:, 0:4, :], w_fo[l].rearrange("(c p) f -> p c f", p=128)[:, :, cg * 512:(cg + 1) * 512], [], [wok])
                dma("pool", wo[:, 4:8, :], w_go[l].rearrange("(c p) f -> p c f", p=128)[:, :, cg * 512:(cg + 1) * 512], [], [wok])
                for cc in range(4):
                    c = cg * 4 + cc
                    b1, b2, b3, b4 = bank(), bank(), bank(), bank()
                    for k in range(DC):
                        mm(ps[b1][:, 0:n], wa[:, k, cc * 128:(cc + 1) * 128], hn[:, k, 0:n], k == 0, k == DC - 1, [wak, hnk], [PSK(b1)])
                    for k in range(DC):
                        mm(ps[b2][:, 0:n], wbb[:, k, cc * 128:(cc + 1) * 128], hn[:, k, 0:n], k == 0, k == DC - 1, [wbk, hnk], [PSK(b2)])
                    for k in range(4):
                        mm(ps[b3][:, 0:n], wo[:, k, cc * 128:(cc + 1) * 128], OT[:, k, 0:n], k == 0, k == 3, [wok, OTk], [PSK(b3)])
                    for k in range(4):
                        mm(ps[b4][:, 0:n], wo[:, 4 + k, cc * 128:(cc + 1) * 128], GT[:, k, 0:n], k == 0, k == 3, [wok, GTk], [PSK(b4)])
                    act(sga[:, 0:n], ps[b1][:, 0:n], AF.Sigmoid, [PSK(b1)], [sgak])
                    act(sgbt[:, 0:n], ps[b2][:, 0:n], AF.Sigmoid, [PSK(b2)], [sgbk])
                    tt("dve", t2[:, 0:n], sga[:, 0:n], ps[b3][:, 0:n], ALU.mult, [sgak, PSK(b3)], [t2k])
                    tt("dve", sgbt[:, 0:n], sgbt[:, 0:n], ps[b4][:, 0:n], ALU.mult, [sgbk, PSK(b4)], [sgbk])
                    tt("pool", yT[:, c, 0:n], t2[:, 0:n], sgbt[:, 0:n], ALU.add, [t2k, sgbk], [(yTk, c)])
            for c2 in range(DC):
                if c2 % 4 == 0:
                    wo, wok = nextw()
                    load_w(wo, wok, w_out[l], c2 * 128, 512)
                bk = bank()
                for k in range(DC):
                    mm(ps[bk][:, 0:n], wo[:, k, (c2 % 4) * 128:(c2 % 4 + 1) * 128], yT[:, k, 0:n], k == 0, k == DC - 1, [wok, (yTk, k)], [PSK(bk)])
                cp("act", oT[:, c2, 0:n], ps[bk][:, 0:n], [PSK(bk)], [oTk])
            postnorm_residual(l, 24, ti, c0, n, oT, oTk, sqb, sqk, rstd, rstdk, [tm0, tm1], [tmk0, tmk1], 1.0)
            S.barrier()
            A.release(mkm)
        S.barrier()
        A.release(mk0)

    for l in range(L):
        if stages is None or f"mix_{l}" in stages:
            mixer(l)
        elif f"ffn0_{l}" in stages:
            ffn(l, 0)
        if stages is None or f"ffn1_{l}" in stages:
            ffn(l, 1)
    finals = []
    for (ti, c0, n, blks) in tiles[1:]:
        finals.append(dma("sp", outT.rearrange("(c p) t -> p c t", p=128)[:, :, c0 - 16:c0 - 16 + n], hT[:, :, c0:c0 + n], [("hT", ti)], [("outT", ti)]))
    if dumps:
        dump("hT", hT[:, :, :], [("hT", t[0]) for t in tiles], [128, DC, TL])
    S.emit(final_ops=finals + list(dump_outs.values()))
    return nc, S


def make_consts():
    s = np.arange(128)[:, None]
    t = np.arange(128)[None, :]
    le = (s <= t).astype(np.float32)
    c = np.zeros((128, 5 * 128 + 4 * 512), np.float32)
    c[:, 0:128] = np.eye(128, dtype=np.float32)
    c[:, 128:256] = le * (-1.0 / 16.0)
    c[:, 256:384] = (s > t).astype(np.float32) * (-1.0 / 16.0)
    c[:, 384:512] = -le
    c[:, 512:640] = le
    for r in range(4):
        m = np.zeros((128, 512), np.float32)
        for q in range(4):
            if q == r:
                m[:, q * 128:(q + 1) * 128] = le
            elif q > r:
                m[:, q * 128:(q + 1) * 128] = 1.0
        c[:, 640 + r * 512:640 + (r + 1) * 512] = m
    return c


def make_smalls(inp, L):
    sm = np.zeros((128, L * NSM), np.float32)
    names = ["g_pre_ffn1", "g_post_ffn1", "g_pre_mix", "g_post_mix", "g_pre_ffn2", "g_post_ffn2"]
    for l in range(L):
        o = l * NSM
        for i, nm in enumerate(names):
            sm[:, o + i * 8:o + (i + 1) * 8] = np.asarray(inp[nm], np.float32)[l].reshape(8, 128).T
        sm[:, o + 48:o + 52] = np.asarray(inp["g_gla_out"], np.float32)[l].reshape(4, 128).T
        sm[:, o + 52:o + 308] = np.asarray(inp["b_alpha"], np.float32)[l][None, :]
        sm[:, o + 308:o + 316] = np.asarray(inp["b_f"], np.float32)[l][None, :]
    return sm


def make_cvec(j):
    v = np.zeros((128, 8), np.float32)
    for i in range(3):
        v[:, i] = 1.0 if i < j else 0.0
        v[:, 3 + i] = 0.0 if i < j else -30000.0
    return v


_CACHE = {}


def run(inputs, NB, L=2, dumps=None, stages=None, mix_stop=None):
    key = (NB, L, tuple(dumps) if dumps else None)
    x = np.asarray(inputs["x"], np.float32)
    B, SEQ, _ = x.shape
    assert B == 2 and SEQ == 4 * NB * 128
    nc, S = build_program(NB, L, dumps, stages, mix_stop)
    consts = make_consts()
    smalls = make_smalls(inputs, L)
    metaT = np.ascontiguousarray(np.asarray(inputs["meta_tokens"], np.float32).T)
    shared = dict(consts=consts, smalls=smalls, metaT=metaT)
    for nm in ["w_in", "w_alpha_up", "w_fox_o", "w_gla_o", "w_out", "w_ffn1_gu", "w_ffn1_down", "w_ffn2_gu", "w_ffn2_down"]:
        shared[nm] = np.ascontiguousarray(np.asarray(inputs[nm], np.float32)[:L])
    in_maps = []
    SEG = NB * 128
    for core in range(8):
        b, j = core // 4, core % 4
        m = dict(shared)
        m["xT"] = np.ascontiguousarray(x[b, j * SEG:(j + 1) * SEG, :].T)
        m["cvec"] = make_cvec(j)
        in_maps.append(m)
    res = run_bass_kernel_spmd(nc, in_maps, core_ids=list(range(8)))
    out = np.zeros((B, SEQ, D), np.float32)
    for core in range(8):
        b, j = core // 4, core % 4
        out[b, j * SEG:(j + 1) * SEG, :] = np.asarray(res.results[core]["outT"]).T
    return out, res


def kernel(**inputs):
    out, _ = run(inputs, 16, 2)
    return out
```

```python
import numpy as np
import concourse.bass as bass
import concourse.mybir as mybir
from concourse.bass_utils import run_bass_kernel_spmd

F32 = mybir.dt.float32
BF16 = mybir.dt.bfloat16
ALU = mybir.AluOpType
AF = mybir.ActivationFunctionType

EPOCH = 24000
N_DMA_SEMS = 24
D = 1024
DC = 8
FF = 2816
FC = 22
NIN = 5144
C_FQ, C_FK, C_FV, C_FF, C_GQ, C_GK, C_GV, C_GA, C_GR, C_MA, C_MB = 0, 512, 1024, 1536, 1544, 1800, 2056, 2568, 2584, 3096, 4120
EPS = 1e-6
NSM = 316


class MK(tuple):
    pass


def _flat(keys):
    out = []
    for k in keys:
        if isinstance(k, MK):
            out.extend(k)
        else:
            out.append(k)
    return out


class Sched:
    ENGS = ("pe", "act", "dve", "pool", "sp")

    def __init__(self, nc):
        self.nc = nc
        self.ops = []
        self.last_write = {}
        self.readers = {}
        self.pending_barrier = None

    def op(self, eng, fn, reads=(), writes=(), dma=False, cc=False):
        reads = _flat(reads)
        writes = _flat(writes)
        deps = set()
        for r in reads:
            lw = self.last_write.get(r)
            if lw is not None:
                deps.add(lw)
        for w in writes:
            lw = self.last_write.get(w)
            if lw is not None:
                deps.add(lw)
            for rd in self.readers.get(w, ()):
                deps.add(rd)
        oid = len(self.ops)
        if self.pending_barrier is not None:
            pb = self.pending_barrier
            if eng not in pb["done"]:
                deps |= pb["deps"]
                pb["done"].add(eng)
        self.ops.append(dict(eng=eng, fn=fn, deps=deps, dma=dma, cc=cc))
        for r in reads:
            lst = self.readers.setdefault(r, [])
            if not dma:
                lst[:] = [x for x in lst if self.ops[x]["dma"] or self.ops[x]["eng"] != eng]
            lst.append(oid)
        for w in writes:
            self.last_write[w] = oid
            self.readers[w] = []
        return oid

    def barrier(self):
        deps = set()
        seen = set()
        nd = 0
        for i in range(len(self.ops) - 1, -1, -1):
            o = self.ops[i]
            if o["dma"]:
                if nd < N_DMA_SEMS:
                    deps.add(i)
                    nd += 1
            elif o["eng"] not in seen:
                seen.add(o["eng"])
                deps.add(i)
            if len(seen) == 5 and nd >= N_DMA_SEMS:
                break
        self.pending_barrier = dict(deps=deps, done=set())

    def emit(self, final_ops=()):
        nc = self.nc
        ops = self.ops
        n = len(ops)
        needs = [False] * n
        for o in ops:
            for d in o["deps"]:
                if ops[d]["eng"] == "pe" and o["eng"] == "pe" and not ops[d]["dma"] and not o["dma"]:
                    continue
                needs[d] = True
        for d in final_ops:
            needs[d] = True
        cnt = {e: 0 for e in self.ENGS}
        for i, o in enumerate(ops):
            if not o["dma"] and needs[i]:
                cnt[o["eng"]] += 1
        sems = {e: [nc.alloc_semaphore(f"s_{e}_{i}") for i in range(cnt[e] // EPOCH + 1)] for e in self.ENGS}
        dsems = [nc.alloc_semaphore(f"s_dma_{i}") for i in range(N_DMA_SEMS)]
        dcount = [0] * N_DMA_SEMS
        dlast = [None] * N_DMA_SEMS
        token = [None] * n
        ecount = {e: 0 for e in self.ENGS}
        dk = 0
        for i, o in enumerate(ops):
            if o["cc"]:
                token[i] = (nc.alloc_semaphore(f"s_cc_{i}"), 1, 1)
            elif o["dma"]:
                k = dk % N_DMA_SEMS
                dk += 1
                if dlast[k] is not None:
                    o["deps"].add(dlast[k])
                dcount[k] += 16
                token[i] = (dsems[k], dcount[k], 16)
                dlast[k] = i
            elif needs[i]:
                c = ecount[o["eng"]]
                ecount[o["eng"]] = c + 1
                token[i] = (sems[o["eng"]][c // EPOCH], c % EPOCH + 1, 1)
        streams = {e: [] for e in self.ENGS}
        for i, o in enumerate(ops):
            streams[o["eng"]].append(i)
        self.n_waits = 0

        def run_stream(e, eng):
            waited = {}
            for i in streams[e]:
                o = ops[i]
                for d in sorted(o["deps"]):
                    od = ops[d]
                    if e == "pe" and od["eng"] == "pe" and not od["dma"] and not o["dma"]:
                        continue
                    sem, val, _ = token[d]
                    key = id(sem)
                    if waited.get(key, 0) >= val:
                        continue
                    eng.wait_ge(sem, val)
                    self.n_waits += 1
                    waited[key] = val
                ins = o["fn"](eng)
                if token[i] is not None:
                    sem, val, step = token[i]
                    ins.then_inc(sem, step)
            if e == "sp":
                for d in final_ops:
                    sem, val, _ = token[d]
                    eng.wait_ge(sem, val)

        with nc.Block() as block:
            @block.tensor
            def _(eng):
                run_stream("pe", eng)

            @block.scalar
            def _(eng):
                run_stream("act", eng)

            @block.vector
            def _(eng):
                run_stream("dve", eng)

            @block.gpsimd
            def _(eng):
                run_stream("pool", eng)

            @block.sync
            def _(eng):
                run_stream("sp", eng)


class Arena:
    def __init__(self, nc, nbytes):
        self.t = nc.alloc_sbuf_tensor("arena", [128, nbytes // 2], BF16)
        self.n = nbytes // 2
        self.top = 0
        self.uid = 0

    def alloc(self, shape, dtype):
        free = 1
        for s in shape[1:]:
            free *= s
        ne = free * (2 if dtype == F32 else 1)
        ne = (ne + 15) // 16 * 16
        assert self.top + ne <= self.n, f"arena overflow {self.top + ne} > {self.n}"
        ap = self.t[:, self.top:self.top + ne]
        self.top += ne
        if dtype == F32:
            ap = ap.bitcast(F32)
        ap = ap[:, 0:free]
        if len(shape) == 3:
            ap = ap.rearrange("p (a b) -> p a b", a=shape[1])
        self.uid += 1
        return ap, ("ar", self.uid)

    def mark(self):
        return self.top

    def release(self, m):
        self.top = m


def build_program(NB, L=2, dumps=None, stages=None, mix_stop=None):
    nc = bass.Bass("TRN2", target_bir_lowering=False)
    SEG = NB * 128
    TL = 16 + SEG
    NPB = NB * 8 + 8
    NPU = NB * 258
    S = Sched(nc)

    def dram_in(name, shape, dt=F32):
        return nc.dram_tensor(name, shape, dt, kind="ExternalInput").ap()

    xT = dram_in("xT", [D, SEG])
    metaT = dram_in("metaT", [D, 16])
    cvec_d = dram_in("cvec", [128, 8])
    consts_d = dram_in("consts", [128, 5 * 128 + 4 * 512])
    smalls_d = dram_in("smalls", [128, L * NSM])
    w_in = dram_in("w_in", [L, D, NIN])
    w_au = dram_in("w_alpha_up", [L, 16, 256])
    w_fo = dram_in("w_fox_o", [L, 512, D])
    w_go = dram_in("w_gla_o", [L, 512, D])
    w_out = dram_in("w_out", [L, D, D])
    w_gu = [dram_in("w_ffn1_gu", [L, D, 2 * FF]), dram_in("w_ffn2_gu", [L, D, 2 * FF])]
    w_dn = [dram_in("w_ffn1_down", [L, FF, D]), dram_in("w_ffn2_down", [L, FF, D])]
    outT = nc.dram_tensor("outT", [D, SEG], F32, kind="ExternalOutput").ap()
    CH = min(4, NB)
    NCH = NB // CH
    PKa = [nc.dram_tensor(f"PKa{l}", [NCH, 512, CH * 128], BF16).ap() for l in range(L)]
    PVa = [nc.dram_tensor(f"PVa{l}", [NCH, 1024, CH * 64], BF16).ap() for l in range(L)]
    PK = [[PKa[l][c] for c in range(NCH)] for l in range(L)]
    PV = [[PVa[l][c] for c in range(NCH)] for l in range(L)]
    PB = [nc.dram_tensor(f"PB{l}", [128, NPB], F32).ap() for l in range(L)]
    PU = [[nc.dram_tensor(f"PU{l}_{c}", [128, CH * 258], F32).ap() for c in range(NCH)] for l in range(L)]
    GKa = [nc.dram_tensor(f"GKa{l}", [NCH, 4 * 512, CH * 128], BF16).ap() for l in range(L)]
    GVa = [nc.dram_tensor(f"GVa{l}", [NCH, 4 * 1024, CH * 64], BF16).ap() for l in range(L)]
    GK = [[GKa[l][c] for c in range(NCH)] for l in range(L)]
    GV = [[GVa[l][c] for c in range(NCH)] for l in range(L)]
    GB = [nc.dram_tensor(f"GB{l}", [4 * 128, NPB], F32).ap() for l in range(L)]
    GU = [[nc.dram_tensor(f"GU{l}_{c}", [4 * 128, CH * 258], F32).ap() for c in range(NCH)] for l in range(L)]
    dump_outs = {}

    def sb(name, shape, dt):
        return nc.alloc_sbuf_tensor("sb_" + name, shape, dt)

    hT = sb("hT", [128, DC, TL], F32)
    constS = sb("constS", [128, 5 * 128], F32)
    maskA = sb("maskA", [128, 4, 512], BF16)
    identb = sb("identb", [128, 128], BF16)
    smalls = sb("smalls", [128, L * NSM], F32)
    cvec = sb("cvec", [128, 8], F32)
    wau = sb("wau", [16, 256], BF16)
    lcS = sb("lcS", [128, NB + 1, 8], F32)
    runS = sb("runS", [128, 8], F32)
    m0S = sb("m0S", [128, 8], F32)
    kmeta = sb("kmeta", [128, 8, 16], BF16)
    vmeta = sb("vmeta", [128, 8, 128], BF16)
    Sst = sb("Sst", [128, 2, 128], F32)
    Sbf = sb("Sbf", [128, 2, 128], BF16)
    U0 = sb("U0", [128, 258], F32)
    gbT = sb("gbT", [128, 3, NPB], F32)
    pbS = sb("pbS", [128, NPB], F32)
    bown = sb("bown", [128, NB + 1, 8], F32)
    bmeta = sb("bmeta", [128, 2, 8], F32)
    Gt = sb("Gt", [128, 4, 8], F32)
    split = sb("split", [128, 8, 67], BF16)
    ps = [nc.alloc_psum_tensor(f"ps{i}", [128, 512], F32) for i in range(8)]
    A = Arena(nc, nc.sbuf_bytes_remaining - 1280)

    ident_f = constS[:, 0:128]
    trin16 = constS[:, 128:256]
    trirev16 = constS[:, 256:384]
    trin1 = constS[:, 384:512]
    trif = constS[:, 512:640]
    PSK = lambda k: ("ps", k)

    def dma(q, out, in_, reads, writes):
        return S.op(q, lambda e: e.dma_start(out=out, in_=in_), reads, writes, dma=True)

    def mm(out, lhsT, rhs, start, stop, reads, writes, **kw):
        return S.op("pe", lambda e: e.matmul(out, lhsT=lhsT, rhs=rhs, start=start, stop=stop, **kw), reads, writes)

    def act(out, in_, func, reads, writes, bias=0.0, scale=1.0):
        return S.op("act", lambda e: e.activation(out=out, in_=in_, func=func, bias=bias, scale=scale), reads, writes)

    def tt(eng, out, in0, in1, op, reads, writes):
        return S.op(eng, lambda e: e.tensor_tensor(out=out, in0=in0, in1=in1, op=op), reads, writes)

    def ts(eng, out, in0, s1, op0, reads, writes, s2=None, op1=None):
        if op1 is None:
            return S.op(eng, lambda e: e.tensor_scalar(out=out, in0=in0, scalar1=s1, scalar2=None, op0=op0), reads, writes)
        return S.op(eng, lambda e: e.tensor_scalar(out=out, in0=in0, scalar1=s1, scalar2=s2, op0=op0, op1=op1), reads, writes)

    def stt(eng, out, in0, scalar, in1, op0, op1, reads, writes):
        return S.op(eng, lambda e: e.scalar_tensor_tensor(out=out, in0=in0, scalar=scalar, in1=in1, op0=op0, op1=op1), reads, writes)

    def cp(eng, out, in_, reads, writes):
        if eng == "act":
            return S.op("act", lambda e: e.copy(out=out, in_=in_), reads, writes)
        return S.op(eng, lambda e: e.tensor_copy(out=out, in_=in_), reads, writes)

    def memset(eng, ap, val, writes):
        return S.op(eng, lambda e: e.memset(ap, val), (), writes)

    def recip(out, in_, reads, writes):
        return S.op("dve", lambda e: e.reciprocal(out=out, in_=in_), reads, writes)

    def dump(name, ap, keys, shape):
        if dumps is None or name not in dumps:
            return
        d = nc.dram_tensor("dbg_" + name, list(shape), F32 if ap.dtype == F32 else BF16, kind="ExternalOutput").ap()
        dump_outs[name] = dma("sp", d, ap, list(keys), [("dbgd", name)])

    bank_rr = [0]

    def bank():
        b = bank_rr[0]
        bank_rr[0] = (b + 1) % 8
        return b

    mk_init = A.mark()
    maskAf, _ = A.alloc([128, 4 * 512], F32)
    dma("sp", constS[:, :], consts_d[:, 0:640], [], ["constS"])
    dma("sp", maskAf[:, :], consts_d[:, 640:640 + 2048], [], ["maskAf"])
    dma("sp", smalls[:, :], smalls_d[:, :], [], ["smalls"])
    dma("sp", cvec[:, :], cvec_d[:, :], [], ["cvec"])
    cp("dve", maskA[:, :, :], maskAf[:, :].rearrange("p (a b) -> p a b", a=4), ["maskAf"], ["maskA"])
    cp("dve", identb[:, :], ident_f, ["constS"], ["identb"])
    S.barrier()
    A.release(mk_init)
    memset("pool", split[:, :, :], 0.0, ["split"])
    memset("pool", lcS[:, :, :], 0.0, [("lcS", b_) for b_ in range(NB + 1)])
    memset("pool", kmeta[:, :, :], 1.0, ["kmeta"])
    memset("pool", vmeta[:, :, :], 1.0, ["vmeta"])
    dma("sp", hT[:, :, 0:16], metaT.rearrange("(c p) t -> p c t", p=128), [], [("hT", 0)])
    tiles = [(0, 0, 16, [0])]
    b = 1
    while b <= NB:
        nbk = min(4, NB - b + 1)
        tiles.append((len(tiles), 16 + (b - 1) * 128, nbk * 128, list(range(b, b + nbk))))
        b += nbk
    for (ti, c0, n, blks) in tiles[1:]:
        dma("sp", hT[:, :, c0:c0 + n], xT.rearrange("(c p) t -> p c t", p=128)[:, :, c0 - 16:c0 - 16 + n], [], [("hT", ti)])

    def bcol(blk):
        return (0, 16) if blk == 0 else (16 + (blk - 1) * 128, 128)

    def sm(l, off, w):
        return smalls[:, l * NSM + off:l * NSM + off + w]

    def norm_rstd(srcs, n, skeys, sqb, sqk, rstd, rstdk, nfeat):
        bk = bank()
        for i, s_ap in enumerate(srcs):
            j = i % 2
            act(sqb[j][:, 0:n], s_ap, AF.Square, list(skeys[i]), [sqk[j]])
            mm(ps[bk][:, 0:n], trif_ones, sqb[j][:, 0:n], i == 0, i == len(srcs) - 1, [sqk[j], "constS2"], [PSK(bk)])
        act(rstd[:, 0:n], ps[bk][:, 0:n], AF.Sqrt, [PSK(bk)], [rstdk], bias=epsb[:, 0:1], scale=1.0 / nfeat)
        recip(rstd[:, 0:n], rstd[:, 0:n], [rstdk], [rstdk])

    ones_t = sb("ones_f", [128, 128], F32)
    memset("dve", ones_t[:, :], 1.0, ["constS2"])
    trif_ones = ones_t[:, :]
    nones_t = sb("nones_f", [128, 128], F32)
    memset("dve", nones_t[:, :], -1.0, ["constS3"])
    epsb = sb("epsb", [128, 1], F32)
    memset("dve", epsb[:, :], EPS, ["epsb"])
    oneb = sb("oneb", [128, 1], F32)
    memset("dve", oneb[:, :], 1.0, ["oneb"])

    def load_w(dst, dstk, w2d, col0, ncols, kc=DC, dcol=0):
        return dma("pool", dst[:, 0:kc, dcol:dcol + ncols], w2d.rearrange("(c p) f -> p c f", p=128)[:, :, col0:col0 + ncols], [], [dstk])

    def prenorm(l, goff, ti, c0, n, hn, hnk, sqb, sqk, rstd, rstdk):
        norm_rstd([hT[:, c, c0:c0 + n] for c in range(DC)], n, [[("hT", ti)]] * DC, sqb, sqk, rstd, rstdk, D)
        for c in range(DC):
            stt("dve", hn[:, c, 0:n], hT[:, c, c0:c0 + n], sm(l, goff + c, 1), rstd[:, 0:n], ALU.mult, ALU.mult,
                [("hT", ti), rstdk, "smalls"], [hnk])

    def postnorm_residual(l, goff, ti, c0, n, oT, oTk, sqb, sqk, rstd, rstdk, tmp, tmpk, scale):
        norm_rstd([oT[:, c, 0:n] for c in range(DC)], n, [[oTk]] * DC, sqb, sqk, rstd, rstdk, D)
        for c in range(DC):
            j = c % 2
            stt("dve", tmp[j][:, 0:n], oT[:, c, 0:n], sm(l, goff + c, 1), rstd[:, 0:n], ALU.mult, ALU.mult, [oTk, rstdk, "smalls"], [tmpk[j]])
            stt("dve", hT[:, c, c0:c0 + n], tmp[j][:, 0:n], scale, hT[:, c, c0:c0 + n], ALU.mult, ALU.add, [tmpk[j], ("hT", ti)], [("hT", ti)])

    def ffn(l, which, shared=None, after_group=None):
        mk = A.mark()
        W = 528
        if shared is None:
            hn, hnk = A.alloc([128, DC, W], BF16)
            sq0, sqk0 = A.alloc([128, 512], F32)
            sq1, sqk1 = A.alloc([128, 512], F32)
            rstd, rstdk = A.alloc([128, W], F32)
            wb = [A.alloc([128, DC, 512], BF16) for _ in range(4)]
        else:
            hn, hnk, (sq0, sq1), (sqk0, sqk1), rstd, rstdk, wb = shared
        actT, actk = A.alloc([128, FC, W], BF16)
        sil0, silk0 = A.alloc([128, 512], F32)
        sil1, silk1 = A.alloc([128, 512], F32)
        oT, oTk = A.alloc([128, DC, W], F32)
        wd = [A.alloc([128, FC, 128], BF16) for _ in range(2)]
        sqb, sqk, sil, silk = [sq0, sq1], [sqk0, sqk1], [sil0, sil1], [silk0, silk1]
        gpre = 0 if which == 0 else 32
        gpost = 8 if which == 0 else 40
        wgu2 = w_gu[which][l]
        wdn2 = w_dn[which][l]
        wi = 0
        di = 0
        tgroups = [[tiles[1] + (0,), tiles[0] + (512,)]] + [[t + (0,)] for t in tiles[2:]]
        for grp in tgroups:
            for (ti, c0, n, blks, bo_) in grp:
                prenorm(l, gpre, ti, c0, n, hn[:, :, bo_:bo_ + n], (hnk, bo_), sqb, sqk, rstd[:, bo_:bo_ + n], (rstdk, bo_))
            for fp in range(FC // 2):
                wt, wk = wb[wi % len(wb)]
                wi += 1
                load_w(wt, (wk, 0), wgu2, fp * 256, 256, dcol=0)
                load_w(wt, (wk, 1), wgu2, FF + fp * 256, 256, dcol=256)
                for sub in range(2):
                    fi = fp * 2 + sub
                    for (ti, c0, n, blks, bo_) in grp:
                        bg, bu = bank(), bank()
                        for c in range(DC):
                            mm(ps[bg][:, 0:n], wt[:, c, sub * 128:(sub + 1) * 128], hn[:, c, bo_:bo_ + n], c == 0, c == DC - 1, [(wk, 0), (hnk, bo_)], [PSK(bg)])
                        for c in range(DC):
                            mm(ps[bu][:, 0:n], wt[:, c, 256 + sub * 128:256 + (sub + 1) * 128], hn[:, c, bo_:bo_ + n], c == 0, c == DC - 1,
                               [(wk, 1), (hnk, bo_)], [PSK(bu)])
                        j = fi % 2
                        act(sil[j][:, 0:n], ps[bg][:, 0:n], AF.Silu, [PSK(bg)], [silk[j]])
                        tt("dve", actT[:, fi, bo_:bo_ + n], sil[j][:, 0:n], ps[bu][:, 0:n], ALU.mult, [silk[j], PSK(bu)], [(actk, fi, bo_)])
            for jd in range(DC):
                wt, wk = wd[di % 2]
                di += 1
                dma("pool", wt[:, :, :], wdn2.rearrange("(fc p) d -> p fc d", p=128)[:, :, jd * 128:(jd + 1) * 128], [], [wk])
                for (ti, c0, n, blks, bo_) in grp:
                    bo = bank()
                    for fc in range(FC):
                        mm(ps[bo][:, 0:n], wt[:, fc, :], actT[:, fc, bo_:bo_ + n], fc == 0, fc == FC - 1, [wk, (actk, fc, bo_)], [PSK(bo)])
                    cp("act", oT[:, jd, bo_:bo_ + n], ps[bo][:, 0:n], [PSK(bo)], [(oTk, bo_)])
            for (ti, c0, n, blks, bo_) in grp:
                postnorm_residual(l, gpost, ti, c0, n, oT[:, :, bo_:bo_ + n], (oTk, bo_), sqb, sqk, rstd[:, bo_:bo_ + n], (rstdk, bo_), sil, silk, 0.5)
            if after_group is not None:
                after_group(grp)
        S.barrier()
        A.release(mk)

    def logsig_pos(x_ap, xk, out_ap, outk, n_rows):
        act(out_ap, x_ap, AF.Exp, [xk], [outk], scale=-1.0)
        act(out_ap, out_ap, AF.Ln, [outk], [outk], bias=oneb[0:n_rows, 0:1])

    def mixer(l):
        win = w_in[l]
        mk0 = A.mark()
        hn, hnkb = A.alloc([128, DC, 528], BF16)
        hnk = (hnkb, 0)
        sq0, sqk0 = A.alloc([128, 512], F32)
        sq1, sqk1 = A.alloc([128, 512], F32)
        rstd, rstdkb = A.alloc([128, 528], F32)
        rstdk = (rstdkb, 0)
        sqb, sqk = [sq0, sq1], [sqk0, sqk1]
        wb = [A.alloc([128, DC, 512], BF16) for _ in range(3)]
        wsm, wsmk = A.alloc([128, DC, 32], BF16)
        vtok, vtokk = A.alloc([128, 4, 512], BF16)
        Ltok, Ltokk = A.alloc([128, 4, 256], F32)
        E1, E1k = A.alloc([128, 2, 128], F32)
        E2, E2k = A.alloc([128, 2, 128], F32)
        gaT, gaTk = A.alloc([128, 512], BF16)
        xf, xfk = A.alloc([128, 8], F32)
        wcnt = [0]

        def nextw():
            w = wb[wcnt[0] % 3]
            wcnt[0] += 1
            return w[0], MK(((w[1], 0), (w[1], 1)))

        load_w(wsm, wsmk, win, C_FF, 8, dcol=0)
        load_w(wsm, wsmk, win, C_GA, 16, dcol=8)
        dma("pool", wau[:, :], w_au[l], [], ["wau"])
        b_alpha = sm(l, 52, 256)
        b_f = sm(l, 308, 8)

        def gla_common(ti, c0, n, blks):
            bk = bank()
            for c in range(DC):
                mm(ps[bk][0:16, 0:n], wsm[:, c, 8:24], hn[:, c, 0:n], c == 0, c == DC - 1, [wsmk, hnk], [PSK(bk)])
            cp("act", gaT[0:16, 0:n], ps[bk][0:16, 0:n], [PSK(bk)], [gaTk])
            wt, wk = nextw()
            load_w(wt, wk, win, C_GV, 512)
            for bi, blk in enumerate(blks):
                bc, nb = bcol(blk)
                lo = bc - c0
                bk = bank()
                for c in range(DC):
                    mm(ps[bk][0:nb, 0:512], hn[:, c, lo:lo + nb], wt[:, c, :], c == 0, c == DC - 1, [wk, hnk], [PSK(bk)])
                cp("act", vtok[0:nb, bi, :], ps[bk][0:nb, 0:512], [PSK(bk)], [(vtokk, bi)])
                bk = bank()
                mm(ps[bk][0:nb, 0:256], gaT[0:16, lo:lo + nb], wau[:, :], True, True, [gaTk, "wau"], [PSK(bk)])
                tt("dve", Ltok[0:nb, bi, :], ps[bk][0:nb, 0:256], b_alpha[0:nb, :], ALU.add, [PSK(bk), "smalls"], [(Ltokk, bi)])
                logsig_pos(Ltok[0:nb, bi, :], (Ltokk, bi), Ltok[0:nb, bi, :], (Ltokk, bi), nb)

        def cb_exp(bi, nb, need_e2):
            for pr in range(2):
                bk = bank()
                mm(ps[bk][:, 0:nb], Ltok[0:nb, bi, pr * 128:(pr + 1) * 128], trin16[0:nb, 0:nb], True, True, [(Ltokk, bi), "constS"], [PSK(bk)])
                act(E1[:, pr, 0:nb], ps[bk][:, 0:nb], AF.Exp, [PSK(bk)], [(E1k, pr)])
                if need_e2:
                    act(E2[:, pr, 0:nb], ps[bk][:, 0:nb], AF.Exp, [PSK(bk)], [(E2k, pr)], scale=-1.0)

        rg = [[0, 1, 2, 3], [4, 5, 6, 7]]
        gathered = set()

        def gather_chunk(c_):
            if c_ in gathered or c_ < 0 or c_ >= NCH:
                return
            gathered.add(c_)
            for (P_, G_, nm) in ((PK[l][c_], GK[l][c_], "K"), (PV[l][c_], GV[l][c_], "V"), (PU[l][c_], GU[l][c_], "U")):
                S.op("pool", lambda e, P_=P_, G_=G_: e.collective_compute("AllGather", ALU.bypass, replica_groups=rg, ins=[P_.opt()], outs=[G_.opt()]),
                     [("P" + nm, l, c_)], [("G" + nm, l, c_)], dma=True, cc=True)

        mk1 = A.mark()
        ktile, ktilek = A.alloc([128, 8, 512], BF16)
        vt, vtk = A.alloc([128, 512], BF16)
        Lf, Lfk = A.alloc([128, 8], F32)
        ktk, ktkk = A.alloc([128, 256], BF16)
        Er, Erk = A.alloc([128, 256], F32)
        usb, usbk = A.alloc([128, 258], F32)
        memset("dve", runS[:, :], 0.0, ["runS"])

        def p1_tile(ti, c0, n, blks):
            prenorm(l, 16, ti, c0, n, hn, hnk, sqb, sqk, rstd, rstdk)
            wt, wk = nextw()
            load_w(wt, wk, win, C_FK, 512)
            for h in range(8):
                bk = bank()
                for c in range(DC):
                    mm(ps[bk][0:64, 0:n], wt[:, c, h * 64:(h + 1) * 64], hn[:, c, 0:n], c == 0, c == DC - 1, [wk, hnk], [PSK(bk)])
                cp("act" if h % 2 == 0 else "dve", ktile[0:64, h, 0:n], ps[bk][0:64, 0:n], [PSK(bk)], [(ktilek, h)])
            if ti == 0:
                cp("dve", kmeta[0:64, :, 0:16], ktile[0:64, :, 0:16], [(ktilek, h_) for h_ in range(8)], ["kmeta"])
            else:
                dma("sp", PK[l][ti - 1].rearrange("(h d) t -> d h t", h=8)[:, :, 0:n], ktile[0:64, :, 0:n], [(ktilek, h_) for h_ in range(8)], [("PK", l, ti - 1)])
            wt, wk = nextw()
            load_w(wt, wk, win, C_FV, 512)
            for bi, blk in enumerate(blks):
                bc, nb = bcol(blk)
                lo = bc - c0
                bk = bank()
                for c in range(DC):
                    mm(ps[bk][0:nb, 0:512], hn[:, c, lo:lo + nb], wt[:, c, :], c == 0, c == DC - 1, [wk, hnk], [PSK(bk)])
                if blk == 0:
                    cp("act", vmeta[0:16, :, 0:64], ps[bk][0:16, 0:512].rearrange("p (h d) -> p h d", h=8), [PSK(bk)], ["vmeta"])
                else:
                    cp("act", vt[0:nb, :], ps[bk][0:nb, 0:512], [PSK(bk)], [vtk])
                    dma("sp", PV[l][(blk - 1) // CH].rearrange("(h s) (b d) -> s h b d", h=8, b=CH)[:, :, (blk - 1) % CH, :],
                        vt[0:nb, :].rearrange("p (h d) -> p h d", h=8), [vtk], [("PV", l, (blk - 1) // CH)])
                bk = bank()
                for c in range(DC):
                    mm(ps[bk][0:nb, 0:8], hn[:, c, lo:lo + nb], wsm[:, c, 0:8], c == 0, c == DC - 1, [wsmk, hnk], [PSK(bk)])
                tt("dve", xf[0:nb, :], ps[bk][0:nb, 0:8], b_f[0:nb, :], ALU.add, [PSK(bk), "smalls"], [xfk])
                logsig_pos(xf[0:nb, :], xfk, Lf[0:nb, :], Lfk, nb)
                bk = bank()
                mm(ps[bk][0:nb, 0:8], trin1[0:nb, 0:nb], Lf[0:nb, :], True, True, [Lfk, "constS"], [PSK(bk)])
                mm(ps[bk][:, 8:16], nones_t[0:nb, :], Lf[0:nb, :], True, True, [Lfk, "constS3"], [PSK(bk)])
                if blk == 0:
                    cp("dve", lcS[0:nb, 0, :], ps[bk][0:nb, 0:8], [PSK(bk)], [("lcS", 0)])
                    cp("dve", m0S[:, :], ps[bk][:, 8:16], [PSK(bk)], ["m0S"])
                else:
                    tt("dve", lcS[0:nb, blk, :], ps[bk][0:nb, 0:8], runS[0:nb, :], ALU.add, [PSK(bk), "runS"], [("lcS", blk)])
                    tt("dve", runS[:, :], runS[:, :], ps[bk][:, 8:16], ALU.add, [PSK(bk), "runS"], ["runS"])
            gla_common(ti, c0, n, blks)
            wt, wk = nextw()
            load_w(wt, wk, win, C_GK, 256)
            for bi, blk in enumerate(blks):
                bc, nb = bcol(blk)
                lo = bc - c0
                cb_exp(bi, nb, False)
                bk = bank()
                for c in range(DC):
                    mm(ps[bk][0:nb, 0:256], hn[:, c, lo:lo + nb], wt[:, c, 0:256], c == 0, c == DC - 1, [wk, hnk], [PSK(bk)])
                bk2 = bank()
                mm(ps[bk2][0:nb, 0:256], trirev16[0:nb, 0:nb], Ltok[0:nb, bi, :], True, True, [(Ltokk, bi), "constS"], [PSK(bk2)])
                act(Er[0:nb, :], ps[bk2][0:nb, 0:256], AF.Exp, [PSK(bk2)], [Erk])
                tt("dve", ktk[0:nb, :], ps[bk][0:nb, 0:256], Er[0:nb, :], ALU.mult, [PSK(bk), Erk], [ktkk])
                bk = bank()
                for hh in range(4):
                    pr, hb = hh // 2, (hh % 2) * 64
                    mm(ps[bk][hb:hb + 64, pr * 128:(pr + 1) * 128], ktk[0:nb, hh * 64:(hh + 1) * 64], vtok[0:nb, bi, hh * 128:(hh + 1) * 128],
                       True, True, [ktkk, (vtokk, bi)], [PSK(bk)], tile_position=(0, hb))
                dst, dstk = (U0, "U0") if blk == 0 else (usb, usbk)
                cp("act", dst[:, 0:256], ps[bk][:, 0:256], [PSK(bk)], [dstk])
                cp("dve", dst[:, 256:258], E1[:, :, nb - 1], [(E1k, 0), (E1k, 1)], [dstk])
                if blk != 0:
                    dma("sp", PU[l][(blk - 1) // CH][:, ((blk - 1) % CH) * 258:((blk - 1) % CH + 1) * 258], usb[:, :], [usbk], [("PU", l, (blk - 1) // CH)])

        def after_group(grp):
            for t_ in sorted(grp, key=lambda t: t[0]):
                p1_tile(t_[0], t_[1], t_[2], t_[3])
                if t_[0] >= 1:
                    gather_chunk(t_[0] - 1)

        if stages is None or f"ffn0_{l}" in stages:
            ffn(l, 0, shared=(hn, hnkb, (sq0, sq1), (sqk0, sqk1), rstd, rstdkb, wb), after_group=after_group)
        else:
            for t_ in tiles:
                after_group([t_])
        for blk in range(1, NB + 1):
            tt("dve", pbS[:, (blk - 1) * 8:blk * 8], runS[:, :], lcS[:, blk, :], ALU.subtract, ["runS", ("lcS", blk)], ["pbS"])
        cp("dve", pbS[:, NB * 8:NB * 8 + 8], runS[:, :], ["runS"], ["pbS"])
        dma("sp", PB[l][:, :], pbS[:, :], ["pbS"], [("PB", l)])
        if mix_stop == "p1":
            S.barrier()
            A.release(mk0)
            return
        for c_ in range(NCH):
            gather_chunk(c_)
        S.op("pool", lambda e: e.collective_compute("AllGather", ALU.bypass, replica_groups=rg, ins=[PB[l].opt()], outs=[GB[l].opt()]),
             [("PB", l)], [("GB", l)], dma=True, cc=True)
        S.barrier()
        A.release(mk1)
        if mix_stop == "p2":
            S.barrier()
            A.release(mk0)
            return
        for i in range(3):
            dma("sp", gbT[:, i, :], GB[l][i * 128:(i + 1) * 128, :], [("GB", l)], ["gbT"])
        Tb = lambda i: gbT[:, i, NB * 8:NB * 8 + 8]
        e_ = lambda i: cvec[:, i:i + 1]
        vis = lambda i: cvec[:, 3 + i:4 + i]
        ts("dve", Gt[:, 2, :], Tb(2), e_(2), ALU.mult, ["gbT", "cvec"], ["Gt"])
        stt("dve", Gt[:, 0, :], Tb(1), e_(1), Gt[:, 2, :], ALU.mult, ALU.add, ["gbT", "cvec", "Gt"], ["Gt"])
        stt("dve", Gt[:, 3, :], Tb(0), e_(0), Gt[:, 0, :], ALU.mult, ALU.add, ["gbT", "cvec", "Gt"], ["Gt"])
        ts("dve", Gt[:, 1, :], Gt[:, 2, :], vis(1), ALU.add, ["Gt", "cvec"], ["Gt"])
        ts("dve", Gt[:, 0, :], Gt[:, 0, :], vis(0), ALU.add, ["Gt", "cvec"], ["Gt"])
        ts("dve", Gt[:, 2, :], Gt[:, 2, :], 0.0, ALU.mult, ["Gt"], ["Gt"], s2=vis(2), op1=ALU.add)
        for i in range(3):
            tt("dve", gbT[:, i, 0:NB * 8].rearrange("p (b h) -> p b h", h=8), gbT[:, i, 0:NB * 8].rearrange("p (b h) -> p b h", h=8),
               Gt[:, i, :].unsqueeze(1).to_broadcast([128, NB, 8]), ALU.add, ["gbT", "Gt"], ["gbT"])
        ts("dve", bown[:, :, :], lcS[:, :, :], -1.0, ALU.mult, [("lcS", b) for b in range(NB + 1)], ["bown"])
        tt("dve", bmeta[:, 0, :], m0S[:, :], lcS[:, 0, :], ALU.subtract, ["m0S", ("lcS", 0)], ["bmeta"])
        tt("dve", bmeta[:, 0, :], bmeta[:, 0, :], Gt[:, 3, :], ALU.add, ["bmeta", "Gt"], ["bmeta"])
        ts("dve", bmeta[:, 1, :], lcS[:, 0, :], -1.0, ALU.mult, [("lcS", 0)], ["bmeta"])
        mk2 = A.mark()
        ubig = [[A.alloc([128, CH * 258], F32) for _ in range(NCH)] for _ in range(3)]
        for i in range(3):
            for c_ in range(NCH):
                dma("sp", ubig[i][c_][0][:, :], GU[l][c_][i * 128:(i + 1) * 128, :], [("GU", l, c_)], [ubig[i][c_][1]])
        ome, omek = A.alloc([128, 3], F32)
        ts("dve", ome[:, :], cvec[:, 0:3], -1.0, ALU.mult, ["cvec"], [omek], s2=1.0, op1=ALU.add)
        dpa = [A.alloc([128, NB, 2], F32) for _ in range(3)]
        Fs = [A.alloc([128, 2, 128], F32) for _ in range(3)]
        Da = [A.alloc([128, 2], F32) for _ in range(3)]
        for i in range(3):
            for c_ in range(NCH):
                ubt, uk = ubig[i][c_]
                ts("dve", dpa[i][0][:, c_ * CH:(c_ + 1) * CH, :], ubt[:, :].rearrange("p (c w) -> p c w", c=CH)[:, :, 256:258], e_(i), ALU.mult,
                   [uk, "cvec"], [dpa[i][1]], s2=ome[:, i:i + 1], op1=ALU.add)
            memset("pool", Fs[i][0][:, :, :], 0.0, [Fs[i][1]])
            memset("pool", Da[i][0][:, :], 1.0, [Da[i][1]])
        for ck in range(NB):
            for i in range(3):
                tt("dve", Fs[i][0][:, :, :], Fs[i][0][:, :, :], dpa[i][0][:, ck, :].unsqueeze(2).to_broadcast([128, 2, 128]), ALU.mult,
                   [Fs[i][1], dpa[i][1]], [Fs[i][1]])
            for i in range(3):
                ubt, uk = ubig[i][ck // CH]
                u = ubt[:, (ck % CH) * 258:(ck % CH + 1) * 258]
                stt("dve", Fs[i][0][:, :, :], u[:, 0:256].rearrange("p (a v) -> p a v", a=2), e_(i), Fs[i][0][:, :, :], ALU.mult, ALU.add,
                    [uk, "cvec", Fs[i][1]], [Fs[i][1]])
            for i in range(3):
                tt("pool", Da[i][0][:, :], Da[i][0][:, :], dpa[i][0][:, ck, :], ALU.mult, [Da[i][1], dpa[i][1]], [Da[i][1]])
        cp("dve", Sst[:, :, :], U0[:, 0:256].rearrange("p (a v) -> p a v", a=2), ["U0"], ["Sst"])
        for i in range(3):
            tt("dve", Sst[:, :, :], Sst[:, :, :], Da[i][0][:, 0:2].unsqueeze(2).to_broadcast([128, 2, 128]), ALU.mult, ["Sst", Da[i][1]], ["Sst"])
            tt("dve", Sst[:, :, :], Sst[:, :, :], Fs[i][0][:, :, :], ALU.add, ["Sst", Fs[i][1]], ["Sst"])
        S.barrier()
        A.release(mk2)
        if mix_stop == "prep":
            S.barrier()
            A.release(mk0)
            return
        mk3 = A.mark()
        QT, QTk = A.alloc([128, 8, 512], BF16)
        OT, OTk = A.alloc([128, 4, 512], BF16)
        GT, GTk = A.alloc([128, 4, 512], BF16)
        sgr, sgrk = A.alloc([128, 4, 512], BF16)
        qg, qgk = A.alloc([128, 2, 512], BF16)
        kg, kgk = A.alloc([128, 2, 512], BF16)
        yT, yTk = A.alloc([128, DC, 512], BF16)
        oT, oTk = A.alloc([128, DC, 512], F32)
        memset("pool", QT[:, :, :], 0.0, [(QTk, h_) for h_ in range(8)])
        c3, c3k = A.alloc([128, 8], F32)
        r3, r3k = A.alloc([128, 8], F32)
        mk4 = A.mark()
        for (ti, c0, n, blks) in tiles:
            prenorm(l, 16, ti, c0, n, hn, hnk, sqb, sqk, rstd, rstdk)
            wt, wk = nextw()
            load_w(wt, wk, win, C_FQ, 512)
            for h in range(8):
                bk = bank()
                for c in range(DC):
                    mm(ps[bk][0:64, 0:n], wt[:, c, h * 64:(h + 1) * 64], hn[:, c, 0:n], c == 0, c == DC - 1, [wk, hnk], [PSK(bk)])
                act(QT[0:64, h, 0:n], ps[bk][0:64, 0:n], AF.Copy, [PSK(bk)], [(QTk, h)], scale=0.125)
            for bi, blk in enumerate(blks):
                bc, nb = bcol(blk)
                lo = bc - c0
                cp("dve", split[0:nb, :, 64], lcS[0:nb, blk, :], [("lcS", blk)], ["split"])
                tt("dve", r3[0:nb, :], lcS[0:nb, blk, :], split[0:nb, :, 64], ALU.subtract, [("lcS", blk), "split"], [r3k])
                cp("dve", split[0:nb, :, 65], r3[0:nb, :], [r3k], ["split"])
                tt("dve", c3[0:nb, :], r3[0:nb, :], split[0:nb, :, 65], ALU.subtract, [r3k, "split"], [c3k])
                cp("dve", split[0:nb, :, 66], c3[0:nb, :], [c3k], ["split"])
                for h in range(8):
                    bk = bank()
                    mm(ps[bk][0:67, 0:nb], split[0:nb, h, :], identb[0:nb, 0:nb], True, True, ["split", "identb"], [PSK(bk)])
                    cp("act" if h % 2 == 0 else "dve", QT[64:67, h, lo:lo + nb], ps[bk][64:67, 0:nb], [PSK(bk)], [(QTk, h)])
            if mix_stop == "q":
                continue
            gla_common(ti, c0, n, blks)
            wt, wk = nextw()
            load_w(wt, wk, win, C_GQ, 512)
            for pr in range(2):
                bk = bank()
                for c in range(DC):
                    mm(ps[bk][:, 0:n], wt[:, c, pr * 128:(pr + 1) * 128], hn[:, c, 0:n], c == 0, c == DC - 1, [wk, hnk], [PSK(bk)])
                cp("act", qg[:, pr, 0:n], ps[bk][:, 0:n], [PSK(bk)], [qgk])
                bk = bank()
                for c in range(DC):
                    mm(ps[bk][:, 0:n], wt[:, c, 256 + pr * 128:256 + (pr + 1) * 128], hn[:, c, 0:n], c == 0, c == DC - 1, [wk, hnk], [PSK(bk)])
                cp("dve", kg[:, pr, 0:n], ps[bk][:, 0:n], [PSK(bk)], [kgk])
            wt, wk = nextw()
            load_w(wt, wk, win, C_GR, 512)
            for hh in range(4):
                bk = bank()
                for c in range(DC):
                    mm(ps[bk][:, 0:n], wt[:, c, hh * 128:(hh + 1) * 128], hn[:, c, 0:n], c == 0, c == DC - 1, [wk, hnk], [PSK(bk)])
                act(sgr[:, hh, 0:n], ps[bk][:, 0:n], AF.Silu, [PSK(bk)], [sgrk])
            if mix_stop == "glaproj":
                continue
            mkf = A.mark()
            KTb = [A.alloc([128, SEG], BF16) for _ in range(2)]
            Vb = [A.alloc([128, NB, 128], BF16) for _ in range(2)]
            PT = [A.alloc([128, 512], BF16) for _ in range(5)]
            rec, reck = A.alloc([128, 512], F32)
            for (t_, k_) in KTb:
                memset("pool", t_[64:67, :], 1.0, [k_])
            for (t_, k_) in Vb:
                memset("pool", t_[:, :, 64:128], 1.0, [(k_, c_) for c_ in range(NCH)])
            LA = 3
            groups = []
            units = []
            for h in range(8):
                bx = 6 + (h % 2)
                hu = []
                if ti == 0:
                    hu.append(dict(kt=kmeta[0:67, h, 0:16], v=vmeta[0:16, h, :], nk=16, bias=bmeta[0:16, 1, h:h + 1], rk=["kmeta", "vmeta"],
                                   mask=maskA[0:16, 0, 0:16], q0=0, grp=None))
                else:
                    hu.append(dict(kt=kmeta[0:67, h, 0:16], v=vmeta[0:16, h, :], nk=16, bias=bmeta[0:16, 0, h:h + 1], rk=["kmeta", "vmeta"],
                                   mask=None, q0=0, grp=None))
                    for src in (3, 0, 1, 2):
                        nblk_src = NB if src < 3 else blks[-1]
                        gi = len(groups)
                        groups.append((h, src, nblk_src))
                        for kb in range(nblk_src):
                            u = dict(nk=128, q0=0, mask=None, grp=gi, kb=kb)
                            if src < 3:
                                u["bias"] = gbT[:, src, kb * 8 + h:kb * 8 + h + 1]
                            else:
                                u["bias"] = bown[:, kb + 1, h:h + 1]
                                r = (kb + 1) - blks[0]
                                if r >= 0:
                                    u["q0"] = r * 128
                                    u["mask"] = maskA[:, r, r * 128:n]
                            hu.append(u)
                for i_, u in enumerate(hu):
                    u["h"], u["bx"], u["first"], u["last"] = h, bx, i_ == 0, i_ == len(hu) - 1
                units += hu
            gbuf = {}

            def load_group(gi):
                if gi >= len(groups) or gi in gbuf:
                    return
                h, src, nblk_src = groups[gi]
                kt_, ktk_ = KTb[gi % 2]
                v_, vk_ = Vb[gi % 2]
                ncs = nblk_src // CH
                if src < 3:
                    ksrc = GKa[l][:, src * 512 + h * 64:src * 512 + (h + 1) * 64, :]
                    vsrc = GVa[l][:, src * 1024 + h * 128:src * 1024 + (h + 1) * 128, :]
                    kkeys = [("GK", l, c_) for c_ in range(ncs)]
                    vkeys = [("GV", l, c_) for c_ in range(ncs)]
                else:
                    ksrc = PKa[l][0:ncs, h * 64:(h + 1) * 64, :]
                    vsrc = PVa[l][0:ncs, h * 128:(h + 1) * 128, :]
                    kkeys = [("PK", l, c_) for c_ in range(ncs)]
                    vkeys = [("PV", l, c_) for c_ in range(ncs)]
                dma("sp", kt_[0:64, 0:ncs * CH * 128].rearrange("d (c t) -> d c t", c=ncs), ksrc.rearrange("c d t -> d c t"), kkeys, [ktk_])
                for c_ in range(ncs):
                    dma("sp", v_[:, c_ * CH:(c_ + 1) * CH, 0:64], vsrc[c_].rearrange("s (b d) -> s b d", b=CH), [vkeys[c_]], [(vk_, c_)])
                gbuf[gi] = (kt_, ktk_, v_, vk_)

            def stage_a(ui, u):
                if u["grp"] is not None:
                    load_group(u["grp"])
                    kt_, ktk_, v_, vk_ = gbuf[u["grp"]]
                    kb = u["kb"]
                    u["kt"], u["v"], u["rk"] = kt_[0:67, kb * 128:(kb + 1) * 128], v_[:, kb, :], [ktk_, (vk_, kb // CH)]
                nk, q0, h = u["nk"], u["q0"], u["h"]
                bs = ui % 6
                mm(ps[bs][0:nk, q0:n], u["kt"], QT[0:67, h, q0:n], True, True, u["rk"] + [(QTk, h)], [PSK(bs)])
                p_, pk_ = PT[ui % 5]
                act(p_[0:nk, q0:n], ps[bs][0:nk, q0:n], AF.Exp, [PSK(bs), "gbT", "bown", "bmeta"], [pk_], bias=u["bias"])
                if u["mask"] is not None:
                    tt("dve", p_[0:nk, q0:n], p_[0:nk, q0:n], u["mask"], ALU.mult, [pk_, "maskA"], [pk_])

            def stage_b(ui, u):
                nk, q0, h, bx = u["nk"], u["q0"], u["h"], u["bx"]
                p_, pk_ = PT[ui % 5]
                mm(ps[bx][:, q0:n], u["v"], p_[0:nk, q0:n], u["first"], u["last"], u["rk"] + [pk_], [PSK(bx)])
                if u["grp"] is not None and u["kb"] == 0:
                    load_group(u["grp"] + 1)
                if u["last"]:
                    recip(rec[0:64, 0:n], ps[bx][64:128, 0:n], [PSK(bx)], [reck])
                    hb = (h % 2) * 64
                    tt("dve", OT[hb:hb + 64, h // 2, 0:n], ps[bx][0:64, 0:n], rec[0:64, 0:n], ALU.mult, [PSK(bx), reck], [OTk])

            for i_ in range(len(units) + LA):
                if i_ < len(units):
                    stage_a(i_, units[i_])
                if i_ - LA >= 0:
                    stage_b(i_ - LA, units[i_ - LA])
            S.barrier()
            A.release(mkf)
            if mix_stop == "fox":
                continue
            mkg = A.mark()
            qp, qpk = A.alloc([128, 4, 128], BF16)
            kp, kpk = A.alloc([128, 4, 128], BF16)
            Sbh, _ = A.alloc([128, 4, 128], BF16)
            attm, attmk = A.alloc([128, 4, 128], BF16)
            sqg, sqgk = A.alloc([128, 512], F32)
            rsg, rsgk = A.alloc([128, 512], F32)
            t1, t1k = A.alloc([128, 512], F32)
            uc = [A.alloc([128, 258], F32) for _ in range(2)]
            for bi, blk in enumerate(blks):
                bc, nb = bcol(blk)
                lo = bc - c0
                cb_exp(bi, nb, True)
                for hh in range(4):
                    pr, hb = hh // 2, (hh % 2) * 64
                    stt("dve", qp[0:64, hh, 0:nb], qg[hb:hb + 64, pr, lo:lo + nb], 0.125, E1[hb:hb + 64, pr, 0:nb], ALU.mult, ALU.mult, [qgk, (E1k, pr)], [qpk])
                    tt("dve", kp[0:64, hh, 0:nb], kg[hb:hb + 64, pr, lo:lo + nb], E2[hb:hb + 64, pr, 0:nb], ALU.mult, [kgk, (E2k, pr)], [kpk])
                ba = bank()
                for hh in range(4):
                    mm(ps[ba][0:nb, hh * 128:hh * 128 + nb], kp[0:64, hh, 0:nb], qp[0:64, hh, 0:nb], True, True, [kpk, qpk], [PSK(ba)])
                tt("dve", attm[0:nb, :, 0:nb], ps[ba][0:nb, :].rearrange("p (a t) -> p a t", a=4)[:, :, 0:nb],
                   trif[0:nb, 0:nb].unsqueeze(1).to_broadcast([nb, 4, nb]), ALU.mult, [PSK(ba), "constS"], [attmk])
                if blk == 0:
                    memset("dve", Sbh[:, :, :], 0.0, ["Sbh"])
                else:
                    for hh in range(4):
                        pr, hb = hh // 2, (hh % 2) * 64
                        cp("dve", Sbh[0:64, hh, :], Sst[hb:hb + 64, pr, :], ["Sst"], ["Sbh"])
                bo = bank()
                for hh in range(4):
                    mm(ps[bo][:, hh * 128:hh * 128 + nb], Sbh[0:64, hh, :], qp[0:64, hh, 0:nb], True, False, ["Sbh", qpk], [PSK(bo)])
                    mm(ps[bo][:, hh * 128:hh * 128 + nb], vtok[0:nb, bi, hh * 128:(hh + 1) * 128], attm[0:nb, hh, 0:nb], False, True,
                       [(vtokk, bi), attmk], [PSK(bo)])
                if mix_stop == "gla2":
                    continue
                if blk != 0:
                    u, uk = uc[bi % 2]
                    dma("sp", u[:, :], PU[l][(blk - 1) // CH][:, ((blk - 1) % CH) * 258:((blk - 1) % CH + 1) * 258], [("PU", l, (blk - 1) // CH)], [uk])
                    for pr in range(2):
                        stt("dve", Sst[:, pr, :], Sst[:, pr, :], u[:, 256 + pr:257 + pr], u[:, pr * 128:(pr + 1) * 128], ALU.mult, ALU.add,
                            ["Sst", uk], ["Sst"])
                if mix_stop == "gla3":
                    continue
                o4 = ps[bo][:, :].rearrange("p (a t) -> p a t", a=4)[:, :, 0:nb]
                act(sqg[:, 0:4 * nb].rearrange("p (a t) -> p a t", a=4), o4, AF.Square, [PSK(bo)], [sqgk])
                bs = bank()
                mm(ps[bs][:, 0:4 * nb], ones_t[:, :], sqg[:, 0:4 * nb], True, True, [sqgk, "constS2"], [PSK(bs)])
                act(rsg[:, 0:4 * nb], ps[bs][:, 0:4 * nb], AF.Sqrt, [PSK(bs)], [rsgk], bias=epsb[:, 0:1], scale=1.0 / 128)
                recip(rsg[:, 0:4 * nb], rsg[:, 0:4 * nb], [rsgk], [rsgk])
                tt("dve", t1[:, 0:4 * nb].rearrange("p (a t) -> p a t", a=4), o4, rsg[:, 0:4 * nb].rearrange("p (a t) -> p a t", a=4), ALU.mult,
                   [PSK(bo), rsgk], [t1k])
                for hh in range(4):
                    stt("dve", GT[:, hh, lo:lo + nb], t1[:, hh * nb:(hh + 1) * nb], sm(l, 48 + hh, 1), sgr[:, hh, lo:lo + nb], ALU.mult, ALU.mult,
                        [t1k, "smalls", sgrk], [GTk])
            S.barrier()
            A.release(mkg)
            if mix_stop in ("gla", "gla2", "gla3"):
                continue
            mkm = A.mark()
            sga, sgak = A.alloc([128, 512], F32)
            sgbt, sgbk = A.alloc([128, 512], F32)
            t2, t2k = A.alloc([128, 512], F32)
            tm0, tmk0 = A.alloc([128, 512], F32)
            tm1, tmk1 = A.alloc([128, 512], F32)
            for cg in range(2):
                wa, wak = nextw()
                load_w(wa, wak, win, C_MA + cg * 512, 512)
                wbb, wbk = nextw()
                load_w(wbb, wbk, win, C_MB + cg * 512, 512)
                wo, wok = nextw()
                dma("pool", wo[:, 0:4, :], w_fo[l].rearrange("(c p) f -> p c f", p=128)[:, :, cg * 512:(cg + 1) * 512], [], [wok])
                dma("pool", wo[:, 4:8, :], w_go[l].rearrange("(c p) f -> p c f", p=128)[:, :, cg * 512:(cg + 1) * 512], [], [wok])
                for cc in range(4):
                    c = cg * 4 + cc
                    b1, b2, b3, b4 = bank(), bank(), bank(), bank()
                    for k in range(DC):
                        mm(ps[b1][:, 0:n], wa[:, k, cc * 128:(cc + 1) * 128], hn[:, k, 0:n], k == 0, k == DC - 1, [wak, hnk], [PSK(b1)])
                    for k in range(DC):
                        mm(ps[b2][:, 0:n], wbb[:, k, cc * 128:(cc + 1) * 128], hn[:, k, 0:n], k == 0, k == DC - 1, [wbk, hnk], [PSK(b2)])
                    for k in range(4):
                        mm(ps[b3][:, 0:n], wo[:, k, cc * 128:(cc + 1) * 128], OT[:, k, 0:n], k == 0, k == 3, [wok, OTk], [PSK(b3)])
                    for k in range(4):
                        mm(ps[b4][:, 0:n], wo[:, 4 + k, cc * 128:(cc + 1) * 128], GT[:, k, 0:n], k == 0, k == 3, [wok, GTk], [PSK(b4)])
                    act(sga[:, 0:n], ps[b1][:, 0:n], AF.Sigmoid, [PSK(b1)], [sgak])
                    act(sgbt[:, 0:n], ps[b2][:, 0:n], AF.Sigmoid, [PSK(b2)], [sgbk])
                    tt("dve", t2[:, 0:n], sga[:, 0:n], ps[b3][:, 0:n], ALU.mult, [sgak, PSK(b3)], [t2k])
                    tt("dve", sgbt[:, 0:n], sgbt[:, 0:n], ps[b4][:, 0:n], ALU.mult, [sgbk, PSK(b4)], [sgbk])
                    tt("dve", yT[:, c, 0:n], t2[:, 0:n], sgbt[:, 0:n], ALU.add, [t2k, sgbk], [(yTk, c)])
            for c2 in range(DC):
                if c2 % 4 == 0:
                    wo, wok = nextw()
                    load_w(wo, wok, w_out[l], c2 * 128, 512)
                bk = bank()
                for k in range(DC):
                    mm(ps[bk][:, 0:n], wo[:, k, (c2 % 4) * 128:(c2 % 4 + 1) * 128], yT[:, k, 0:n], k == 0, k == DC - 1, [wok, (yTk, k)], [PSK(bk)])
                cp("act", oT[:, c2, 0:n], ps[bk][:, 0:n], [PSK(bk)], [oTk])
            postnorm_residual(l, 24, ti, c0, n, oT, oTk, sqb, sqk, rstd, rstdk, [tm0, tm1], [tmk0, tmk1], 1.0)
            S.barrier()
            A.release(mkm)
        S.barrier()
        A.release(mk0)

    for l in range(L):
        if stages is None or f"mix_{l}" in stages:
            mixer(l)
        elif f"ffn0_{l}" in stages:
            ffn(l, 0)
        if stages is None or f"ffn1_{l}" in stages:
            ffn(l, 1)
    finals = []
    for (ti, c0, n, blks) in tiles[1:]:
        finals.append(dma("sp", outT.rearrange("(c p) t -> p c t", p=128)[:, :, c0 - 16:c0 - 16 + n], hT[:, :, c0:c0 + n], [("hT", ti)], [("outT", ti)]))
    if dumps:
        dump("hT", hT[:, :, :], [("hT", t[0]) for t in tiles], [128, DC, TL])
    S.emit(final_ops=finals + list(dump_outs.values()))
    return nc, S


def make_consts():
    s = np.arange(128)[:, None]
    t = np.arange(128)[None, :]
    le = (s <= t).astype(np.float32)
    c = np.zeros((128, 5 * 128 + 4 * 512), np.float32)
    c[:, 0:128] = np.eye(128, dtype=np.float32)
    c[:, 128:256] = le * (-1.0 / 16.0)
    c[:, 256:384] = (s > t).astype(np.float32) * (-1.0 / 16.0)
    c[:, 384:512] = -le
    c[:, 512:640] = le
    for r in range(4):
        m = np.zeros((128, 512), np.float32)
        for q in range(4):
            if q == r:
                m[:, q * 128:(q + 1) * 128] = le
            elif q > r:
                m[:, q * 128:(q + 1) * 128] = 1.0
        c[:, 640 + r * 512:640 + (r + 1) * 512] = m
    return c


def make_smalls(inp, L):
    sm = np.zeros((128, L * NSM), np.float32)
    names = ["g_pre_ffn1", "g_post_ffn1", "g_pre_mix", "g_post_mix", "g_pre_ffn2", "g_post_ffn2"]
    for l in range(L):
        o = l * NSM
        for i, nm in enumerate(names):
            sm[:, o + i * 8:o + (i + 1) * 8] = np.asarray(inp[nm], np.float32)[l].reshape(8, 128).T
        sm[:, o + 48:o + 52] = np.asarray(inp["g_gla_out"], np.float32)[l].reshape(4, 128).T
        sm[:, o + 52:o + 308] = np.asarray(inp["b_alpha"], np.float32)[l][None, :]
        sm[:, o + 308:o + 316] = np.asarray(inp["b_f"], np.float32)[l][None, :]
    return sm


def make_cvec(j):
    v = np.zeros((128, 8), np.float32)
    for i in range(3):
        v[:, i] = 1.0 if i < j else 0.0
        v[:, 3 + i] = 0.0 if i < j else -30000.0
    return v


_CACHE = {}


def run(inputs, NB, L=2, dumps=None, stages=None, mix_stop=None):
    key = (NB, L, tuple(dumps) if dumps else None)
    x = np.asarray(inputs["x"], np.float32)
    B, SEQ, _ = x.shape
    assert B == 2 and SEQ == 4 * NB * 128
    nc, S = build_program(NB, L, dumps, stages, mix_stop)
    consts = make_consts()
    smalls = make_smalls(inputs, L)
    metaT = np.ascontiguousarray(np.asarray(inputs["meta_tokens"], np.float32).T)
    shared = dict(consts=consts, smalls=smalls, metaT=metaT)
    for nm in ["w_in", "w_alpha_up", "w_fox_o", "w_gla_o", "w_out", "w_ffn1_gu", "w_ffn1_down", "w_ffn2_gu", "w_ffn2_down"]:
        shared[nm] = np.ascontiguousarray(np.asarray(inputs[nm], np.float32)[:L])
    in_maps = []
    SEG = NB * 128
    for core in range(8):
        b, j = core // 4, core % 4
        m = dict(shared)
        m["xT"] = np.ascontiguousarray(x[b, j * SEG:(j + 1) * SEG, :].T)
        m["cvec"] = make_cvec(j)
        in_maps.append(m)
    res = run_bass_kernel_spmd(nc, in_maps, core_ids=list(range(8)))
    out = np.zeros((B, SEQ, D), np.float32)
    for core in range(8):
        b, j = core // 4, core % 4
        out[b, j * SEG:(j + 1) * SEG, :] = np.asarray(res.results[core]["outT"]).T
    return out, res


def kernel(**inputs):
    out, _ = run(inputs, 16, 2)
    return out
```

```python
import numpy as np
import concourse.bass as bass
import concourse.mybir as mybir
from concourse.bass_utils import run_bass_kernel_spmd

F32 = mybir.dt.float32
BF16 = mybir.dt.bfloat16
ALU = mybir.AluOpType
AF = mybir.ActivationFunctionType

EPOCH = 24000
N_DMA_SEMS = 24
D = 1024
DC = 8
FF = 2816
FC = 22
NIN = 5144
C_FQ, C_FK, C_FV, C_FF, C_GQ, C_GK, C_GV, C_GA, C_GR, C_MA, C_MB = 0, 512, 1024, 1536, 1544, 1800, 2056, 2568, 2584, 3096, 4120
EPS = 1e-6
NSM = 316


class MK(tuple):
    pass


def _flat(keys):
    out = []
    for k in keys:
        if isinstance(k, MK):
            out.extend(k)
        else:
            out.append(k)
    return out


class Sched:
    ENGS = ("pe", "act", "dve", "pool", "sp")

    def __init__(self, nc):
        self.nc = nc
        self.ops = []
        self.last_write = {}
        self.readers = {}
        self.pending_barrier = None

    def op(self, eng, fn, reads=(), writes=(), dma=False, cc=False):
        reads = _flat(reads)
        writes = _flat(writes)
        deps = set()
        for r in reads:
            lw = self.last_write.get(r)
            if lw is not None:
                deps.add(lw)
        for w in writes:
            lw = self.last_write.get(w)
            if lw is not None:
                deps.add(lw)
            for rd in self.readers.get(w, ()):
                deps.add(rd)
        oid = len(self.ops)
        if self.pending_barrier is not None:
            pb = self.pending_barrier
            if eng not in pb["done"]:
                deps |= pb["deps"]
                pb["done"].add(eng)
        self.ops.append(dict(eng=eng, fn=fn, deps=deps, dma=dma, cc=cc))
        for r in reads:
            lst = self.readers.setdefault(r, [])
            if not dma:
                lst[:] = [x for x in lst if self.ops[x]["dma"] or self.ops[x]["eng"] != eng]
            lst.append(oid)
        for w in writes:
            self.last_write[w] = oid
            self.readers[w] = []
        return oid

    def barrier(self):
        deps = set()
        seen = set()
        nd = 0
        for i in range(len(self.ops) - 1, -1, -1):
            o = self.ops[i]
            if o["dma"]:
                if nd < N_DMA_SEMS:
                    deps.add(i)
                    nd += 1
            elif o["eng"] not in seen:
                seen.add(o["eng"])
                deps.add(i)
            if len(seen) == 5 and nd >= N_DMA_SEMS:
                break
        self.pending_barrier = dict(deps=deps, done=set())

    def emit(self, final_ops=()):
        nc = self.nc
        ops = self.ops
        n = len(ops)
        needs = [False] * n
        for o in ops:
            for d in o["deps"]:
                if ops[d]["eng"] == "pe" and o["eng"] == "pe" and not ops[d]["dma"] and not o["dma"]:
                    continue
                needs[d] = True
        for d in final_ops:
            needs[d] = True
        cnt = {e: 0 for e in self.ENGS}
        for i, o in enumerate(ops):
            if not o["dma"] and needs[i]:
                cnt[o["eng"]] += 1
        sems = {e: [nc.alloc_semaphore(f"s_{e}_{i}") for i in range(cnt[e] // EPOCH + 1)] for e in self.ENGS}
        dsems = [nc.alloc_semaphore(f"s_dma_{i}") for i in range(N_DMA_SEMS)]
        dcount = [0] * N_DMA_SEMS
        dlast = [None] * N_DMA_SEMS
        token = [None] * n
        ecount = {e: 0 for e in self.ENGS}
        dk = 0
        for i, o in enumerate(ops):
            if o["cc"]:
                token[i] = (nc.alloc_semaphore(f"s_cc_{i}"), 1, 1)
            elif o["dma"]:
                k = dk % N_DMA_SEMS
                dk += 1
                if dlast[k] is not None:
                    o["deps"].add(dlast[k])
                dcount[k] += 16
                token[i] = (dsems[k], dcount[k], 16)
                dlast[k] = i
            elif needs[i]:
                c = ecount[o["eng"]]
                ecount[o["eng"]] = c + 1
                token[i] = (sems[o["eng"]][c // EPOCH], c % EPOCH + 1, 1)
        streams = {e: [] for e in self.ENGS}
        for i, o in enumerate(ops):
            streams[o["eng"]].append(i)
        self.n_waits = 0

        def run_stream(e, eng):
            waited = {}
            for i in streams[e]:
                o = ops[i]
                for d in sorted(o["deps"]):
                    od = ops[d]
                    if e == "pe" and od["eng"] == "pe" and not od["dma"] and not o["dma"]:
                        continue
                    sem, val, _ = token[d]
                    key = id(sem)
                    if waited.get(key, 0) >= val:
                        continue
                    eng.wait_ge(sem, val)
                    self.n_waits += 1
                    waited[key] = val
                ins = o["fn"](eng)
                if token[i] is not None:
                    sem, val, step = token[i]
                    ins.then_inc(sem, step)
            if e == "sp":
                for d in final_ops:
                    sem, val, _ = token[d]
                    eng.wait_ge(sem, val)

        with nc.Block() as block:
            @block.tensor
            def _(eng):
                run_stream("pe", eng)

            @block.scalar
            def _(eng):
                run_stream("act", eng)

            @block.vector
            def _(eng):
                run_stream("dve", eng)

            @block.gpsimd
            def _(eng):
                run_stream("pool", eng)

            @block.sync
            def _(eng):
                run_stream("sp", eng)


class Arena:
    def __init__(self, nc, nbytes):
        self.t = nc.alloc_sbuf_tensor("arena", [128, nbytes // 2], BF16)
        self.n = nbytes // 2
        self.top = 0
        self.uid = 0

    def alloc(self, shape, dtype):
        free = 1
        for s in shape[1:]:
            free *= s
        ne = free * (2 if dtype == F32 else 1)
        ne = (ne + 15) // 16 * 16
        assert self.top + ne <= self.n, f"arena overflow {self.top + ne} > {self.n}"
        ap = self.t[:, self.top:self.top + ne]
        self.top += ne
        if dtype == F32:
            ap = ap.bitcast(F32)
        ap = ap[:, 0:free]
        if len(shape) == 3:
            ap = ap.rearrange("p (a b) -> p a b", a=shape[1])
        self.uid += 1
        return ap, ("ar", self.uid)

    def mark(self):
        return self.top

    def release(self, m):
        self.top = m


def build_program(NB, L=2, dumps=None, stages=None, mix_stop=None):
    nc = bass.Bass("TRN2", target_bir_lowering=False)
    SEG = NB * 128
    TL = 16 + SEG
    NPB = NB * 8 + 8
    NPU = NB * 258
    S = Sched(nc)

    def dram_in(name, shape, dt=F32):
        return nc.dram_tensor(name, shape, dt, kind="ExternalInput").ap()

    xT = dram_in("xT", [D, SEG])
    metaT = dram_in("metaT", [D, 16])
    cvec_d = dram_in("cvec", [128, 8])
    consts_d = dram_in("consts", [128, 5 * 128 + 4 * 512])
    smalls_d = dram_in("smalls", [128, L * NSM])
    w_in = dram_in("w_in", [L, D, NIN])
    w_au = dram_in("w_alpha_up", [L, 16, 256])
    w_fo = dram_in("w_fox_o", [L, 512, D])
    w_go = dram_in("w_gla_o", [L, 512, D])
    w_out = dram_in("w_out", [L, D, D])
    w_gu = [dram_in("w_ffn1_gu", [L, D, 2 * FF]), dram_in("w_ffn2_gu", [L, D, 2 * FF])]
    w_dn = [dram_in("w_ffn1_down", [L, FF, D]), dram_in("w_ffn2_down", [L, FF, D])]
    outT = nc.dram_tensor("outT", [D, SEG], F32, kind="ExternalOutput").ap()
    CH = min(4, NB)
    NCH = NB // CH
    PKa = [nc.dram_tensor(f"PKa{l}", [NCH, 512, CH * 128], BF16).ap() for l in range(L)]
    PVa = [nc.dram_tensor(f"PVa{l}", [NCH, 1024, CH * 64], BF16).ap() for l in range(L)]
    PK = [[PKa[l][c] for c in range(NCH)] for l in range(L)]
    PV = [[PVa[l][c] for c in range(NCH)] for l in range(L)]
    PB = [nc.dram_tensor(f"PB{l}", [128, NPB], F32).ap() for l in range(L)]
    PU = [[nc.dram_tensor(f"PU{l}_{c}", [128, CH * 258], F32).ap() for c in range(NCH)] for l in range(L)]
    GKa = [nc.dram_tensor(f"GKa{l}", [NCH, 4 * 512, CH * 128], BF16).ap() for l in range(L)]
    GVa = [nc.dram_tensor(f"GVa{l}", [NCH, 4 * 1024, CH * 64], BF16).ap() for l in range(L)]
    GK = [[GKa[l][c] for c in range(NCH)] for l in range(L)]
    GV = [[GVa[l][c] for c in range(NCH)] for l in range(L)]
    GB = [nc.dram_tensor(f"GB{l}", [4 * 128, NPB], F32).ap() for l in range(L)]
    GU = [[nc.dram_tensor(f"GU{l}_{c}", [4 * 128, CH * 258], F32).ap() for c in range(NCH)] for l in range(L)]
    dump_outs = {}

    def sb(name, shape, dt):
        return nc.alloc_sbuf_tensor("sb_" + name, shape, dt)

    hT = sb("hT", [128, DC, TL], F32)
    constS = sb("constS", [128, 5 * 128], F32)
    maskA = sb("maskA", [128, 4, 512], BF16)
    identb = sb("identb", [128, 128], BF16)
    smalls = sb("smalls", [128, L * NSM], F32)
    cvec = sb("cvec", [128, 8], F32)
    wau = sb("wau", [16, 256], BF16)
    lcS = sb("lcS", [128, NB + 1, 8], F32)
    runS = sb("runS", [128, 8], F32)
    m0S = sb("m0S", [128, 8], F32)
    kmeta = sb("kmeta", [128, 8, 16], BF16)
    vmeta = sb("vmeta", [128, 8, 128], BF16)
    Sst = sb("Sst", [128, 2, 128], F32)
    Sbf = sb("Sbf", [128, 2, 128], BF16)
    U0 = sb("U0", [128, 258], F32)
    gbT = sb("gbT", [128, 3, NPB], F32)
    pbS = sb("pbS", [128, NPB], F32)
    bown = sb("bown", [128, NB + 1, 8], F32)
    bmeta = sb("bmeta", [128, 2, 8], F32)
    Gt = sb("Gt", [128, 4, 8], F32)
    split = sb("split", [128, 8, 67], BF16)
    ps = [nc.alloc_psum_tensor(f"ps{i}", [128, 512], F32) for i in range(8)]
    A = Arena(nc, nc.sbuf_bytes_remaining - 1280)

    ident_f = constS[:, 0:128]
    trin16 = constS[:, 128:256]
    trirev16 = constS[:, 256:384]
    trin1 = constS[:, 384:512]
    trif = constS[:, 512:640]
    PSK = lambda k: ("ps", k)

    def dma(q, out, in_, reads, writes):
        return S.op(q, lambda e: e.dma_start(out=out, in_=in_), reads, writes, dma=True)

    def mm(out, lhsT, rhs, start, stop, reads, writes, **kw):
        return S.op("pe", lambda e: e.matmul(out, lhsT=lhsT, rhs=rhs, start=start, stop=stop, **kw), reads, writes)

    def act(out, in_, func, reads, writes, bias=0.0, scale=1.0):
        return S.op("act", lambda e: e.activation(out=out, in_=in_, func=func, bias=bias, scale=scale), reads, writes)

    def tt(eng, out, in0, in1, op, reads, writes):
        return S.op(eng, lambda e: e.tensor_tensor(out=out, in0=in0, in1=in1, op=op), reads, writes)

    def ts(eng, out, in0, s1, op0, reads, writes, s2=None, op1=None):
        if op1 is None:
            return S.op(eng, lambda e: e.tensor_scalar(out=out, in0=in0, scalar1=s1, scalar2=None, op0=op0), reads, writes)
        return S.op(eng, lambda e: e.tensor_scalar(out=out, in0=in0, scalar1=s1, scalar2=s2, op0=op0, op1=op1), reads, writes)

    def stt(eng, out, in0, scalar, in1, op0, op1, reads, writes):
        return S.op(eng, lambda e: e.scalar_tensor_tensor(out=out, in0=in0, scalar=scalar, in1=in1, op0=op0, op1=op1), reads, writes)

    def cp(eng, out, in_, reads, writes):
        if eng == "act":
            return S.op("act", lambda e: e.copy(out=out, in_=in_), reads, writes)
        return S.op(eng, lambda e: e.tensor_copy(out=out, in_=in_), reads, writes)

    def memset(eng, ap, val, writes):
        return S.op(eng, lambda e: e.memset(ap, val), (), writes)

    def recip(out, in_, reads, writes):
        return S.op("dve", lambda e: e.reciprocal(out=out, in_=in_), reads, writes)

    def dump(name, ap, keys, shape):
        if dumps is None or name not in dumps:
            return
        d = nc.dram_tensor("dbg_" + name, list(shape), F32 if ap.dtype == F32 else BF16, kind="ExternalOutput").ap()
        dump_outs[name] = dma("sp", d, ap, list(keys), [("dbgd", name)])

    bank_rr = [0]

    def bank():
        b = bank_rr[0]
        bank_rr[0] = (b + 1) % 8
        return b

    mk_init = A.mark()
    maskAf, _ = A.alloc([128, 4 * 512], F32)
    dma("sp", constS[:, :], consts_d[:, 0:640], [], ["constS"])
    dma("sp", maskAf[:, :], consts_d[:, 640:640 + 2048], [], ["maskAf"])
    dma("sp", smalls[:, :], smalls_d[:, :], [], ["smalls"])
    dma("sp", cvec[:, :], cvec_d[:, :], [], ["cvec"])
    cp("dve", maskA[:, :, :], maskAf[:, :].rearrange("p (a b) -> p a b", a=4), ["maskAf"], ["maskA"])
    cp("dve", identb[:, :], ident_f, ["constS"], ["identb"])
    S.barrier()
    A.release(mk_init)
    memset("pool", split[:, :, :], 0.0, ["split"])
    memset("pool", lcS[:, :, :], 0.0, [("lcS", b_) for b_ in range(NB + 1)])
    memset("pool", kmeta[:, :, :], 1.0, ["kmeta"])
    memset("pool", vmeta[:, :, :], 1.0, ["vmeta"])
    dma("sp", hT[:, :, 0:16], metaT.rearrange("(c p) t -> p c t", p=128), [], [MK(tuple(("hT", 0, c_) for c_ in range(DC)))])
    tiles = [(0, 0, 16, [0])]
    b = 1
    while b <= NB:
        nbk = min(4, NB - b + 1)
        tiles.append((len(tiles), 16 + (b - 1) * 128, nbk * 128, list(range(b, b + nbk))))
        b += nbk
    for (ti, c0, n, blks) in tiles[1:]:
        dma("sp", hT[:, :, c0:c0 + n], xT.rearrange("(c p) t -> p c t", p=128)[:, :, c0 - 16:c0 - 16 + n], [], [MK(tuple(("hT", ti, c_) for c_ in range(DC)))])

    def bcol(blk):
        return (0, 16) if blk == 0 else (16 + (blk - 1) * 128, 128)

    def sm(l, off, w):
        return smalls[:, l * NSM + off:l * NSM + off + w]

    def norm_rstd(srcs, n, skeys, sqb, sqk, rstd, rstdk, nfeat):
        bk = bank()
        for i, s_ap in enumerate(srcs):
            j = i % 2
            act(sqb[j][:, 0:n], s_ap, AF.Square, list(skeys[i]), [sqk[j]])
            mm(ps[bk][:, 0:n], trif_ones, sqb[j][:, 0:n], i == 0, i == len(srcs) - 1, [sqk[j], "constS2"], [PSK(bk)])
        act(rstd[:, 0:n], ps[bk][:, 0:n], AF.Sqrt, [PSK(bk)], [rstdk], bias=epsb[:, 0:1], scale=1.0 / nfeat)
        recip(rstd[:, 0:n], rstd[:, 0:n], [rstdk], [rstdk])

    ones_t = sb("ones_f", [128, 128], F32)
    memset("dve", ones_t[:, :], 1.0, ["constS2"])
    trif_ones = ones_t[:, :]
    nones_t = sb("nones_f", [128, 128], F32)
    memset("dve", nones_t[:, :], -1.0, ["constS3"])
    epsb = sb("epsb", [128, 1], F32)
    memset("dve", epsb[:, :], EPS, ["epsb"])
    oneb = sb("oneb", [128, 1], F32)
    memset("dve", oneb[:, :], 1.0, ["oneb"])

    def load_w(dst, dstk, w2d, col0, ncols, kc=DC, dcol=0):
        return dma("pool", dst[:, 0:kc, dcol:dcol + ncols], w2d.rearrange("(c p) f -> p c f", p=128)[:, :, col0:col0 + ncols], [], [dstk])

    def subkey(k, c):
        return k[c] if isinstance(k, MK) else (k, "c", c)

    def allkeys(k):
        return k if isinstance(k, MK) else MK(tuple((k, "c", c) for c in range(DC)))

    def prenorm(l, goff, ti, c0, n, hn, hnk, sqb, sqk, rstd, rstdk):
        norm_rstd([hT[:, c, c0:c0 + n] for c in range(DC)], n, [[("hT", ti, c)] for c in range(DC)], sqb, sqk, rstd, rstdk, D)
        for c in range(DC):
            stt("dve", hn[:, c, 0:n], hT[:, c, c0:c0 + n], sm(l, goff + c, 1), rstd[:, 0:n], ALU.mult, ALU.mult,
                [("hT", ti, c), rstdk, "smalls"], [subkey(hnk, c)])

    def postnorm_residual(l, goff, ti, c0, n, oT, oTk, sqb, sqk, rstd, rstdk, tmp, tmpk, scale):
        norm_rstd([oT[:, c, 0:n] for c in range(DC)], n, [[subkey(oTk, c)] for c in range(DC)], sqb, sqk, rstd, rstdk, D)
        tmp = list(tmp) + list(sqb)
        tmpk = list(tmpk) + list(sqk)

        def t_op(c):
            j = c % 4
            stt("dve", tmp[j][:, 0:n], oT[:, c, 0:n], sm(l, goff + c, 1), rstd[:, 0:n], ALU.mult, ALU.mult, [subkey(oTk, c), rstdk, "smalls"], [tmpk[j]])

        def h_op(c):
            j = c % 4
            stt("dve", hT[:, c, c0:c0 + n], tmp[j][:, 0:n], scale, hT[:, c, c0:c0 + n], ALU.mult, ALU.add, [tmpk[j], ("hT", ti, c)], [("hT", ti, c)])

        t_op(0)
        for c in range(1, DC):
            t_op(c)
            h_op(c - 1)
        h_op(DC - 1)

    def ffn(l, which, shared=None, after_group=None):
        mk = A.mark()
        W = 528
        if shared is None:
            hn, hnk = A.alloc([128, DC, W], BF16)
            sq0, sqk0 = A.alloc([128, 512], F32)
            sq1, sqk1 = A.alloc([128, 512], F32)
            rstd, rstdk = A.alloc([128, W], F32)
            wb = [A.alloc([128, DC, 512], BF16) for _ in range(4)]
        else:
            hn, hnk, (sq0, sq1), (sqk0, sqk1), rstd, rstdk, wb = shared
        actT, actk = A.alloc([128, FC, W], BF16)
        sil0, silk0 = A.alloc([128, 512], F32)
        sil1, silk1 = A.alloc([128, 512], F32)
        oT, oTk = A.alloc([128, DC, W], F32)
        wd = [A.alloc([128, FC, 128], BF16) for _ in range(2)]
        sqb, sqk, sil, silk = [sq0, sq1], [sqk0, sqk1], [sil0, sil1], [silk0, silk1]
        gpre = 0 if which == 0 else 32
        gpost = 8 if which == 0 else 40
        wgu2 = w_gu[which][l]
        wdn2 = w_dn[which][l]
        wi = 0
        di = 0
        tgroups = [[tiles[1] + (0,), tiles[0] + (512,)]] + [[t + (0,)] for t in tiles[2:]]
        for grp in tgroups:
            for (ti, c0, n, blks, bo_) in grp:
                prenorm(l, gpre, ti, c0, n, hn[:, :, bo_:bo_ + n], (hnk, bo_), sqb, sqk, rstd[:, bo_:bo_ + n], (rstdk, bo_))
            for fp in range(FC // 2):
                wt, wk = wb[wi % len(wb)]
                wi += 1
                load_w(wt, (wk, 0), wgu2, fp * 256, 256, dcol=0)
                load_w(wt, (wk, 1), wgu2, FF + fp * 256, 256, dcol=256)
                for sub in range(2):
                    fi = fp * 2 + sub
                    for (ti, c0, n, blks, bo_) in grp:
                        bg, bu = bank(), bank()
                        for c in range(DC):
                            mm(ps[bg][:, 0:n], wt[:, c, sub * 128:(sub + 1) * 128], hn[:, c, bo_:bo_ + n], c == 0, c == DC - 1, [(wk, 0), ((hnk, bo_), "c", c)], [PSK(bg)])
                        for c in range(DC):
                            mm(ps[bu][:, 0:n], wt[:, c, 256 + sub * 128:256 + (sub + 1) * 128], hn[:, c, bo_:bo_ + n], c == 0, c == DC - 1,
                               [(wk, 1), ((hnk, bo_), "c", c)], [PSK(bu)])
                        j = fi % 2
                        act(sil[j][:, 0:n], ps[bg][:, 0:n], AF.Silu, [PSK(bg)], [silk[j]])
                        tt("dve", actT[:, fi, bo_:bo_ + n], sil[j][:, 0:n], ps[bu][:, 0:n], ALU.mult, [silk[j], PSK(bu)], [(actk, fi, bo_)])
            for jd in range(DC):
                wt, wk = wd[di % 2]
                di += 1
                dma("pool", wt[:, :, :], wdn2.rearrange("(fc p) d -> p fc d", p=128)[:, :, jd * 128:(jd + 1) * 128], [], [wk])
                for (ti, c0, n, blks, bo_) in grp:
                    bo = bank()
                    for fc in range(FC):
                        mm(ps[bo][:, 0:n], wt[:, fc, :], actT[:, fc, bo_:bo_ + n], fc == 0, fc == FC - 1, [wk, (actk, fc, bo_)], [PSK(bo)])
                    cp("act", oT[:, jd, bo_:bo_ + n], ps[bo][:, 0:n], [PSK(bo)], [((oTk, bo_), "c", jd)])
            for (ti, c0, n, blks, bo_) in grp:
                postnorm_residual(l, gpost, ti, c0, n, oT[:, :, bo_:bo_ + n], (oTk, bo_), sqb, sqk, rstd[:, bo_:bo_ + n], (rstdk, bo_), sil, silk, 0.5)
            if after_group is not None:
                after_group(grp)
        S.barrier()
        A.release(mk)

    def logsig_pos(x_ap, xk, out_ap, outk, n_rows):
        act(out_ap, x_ap, AF.Exp, [xk], [outk], scale=-1.0)
        act(out_ap, out_ap, AF.Ln, [outk], [outk], bias=oneb[0:n_rows, 0:1])

    def mixer(l):
        win = w_in[l]
        mk0 = A.mark()
        hn, hnkb = A.alloc([128, DC, 528], BF16)
        hnk = MK(tuple(((hnkb, 0), "c", c_) for c_ in range(DC)))
        sq0, sqk0 = A.alloc([128, 512], F32)
        sq1, sqk1 = A.alloc([128, 512], F32)
        rstd, rstdkb = A.alloc([128, 528], F32)
        rstdk = (rstdkb, 0)
        sqb, sqk = [sq0, sq1], [sqk0, sqk1]
        wb = [A.alloc([128, DC, 512], BF16) for _ in range(3)]
        wsm, wsmk = A.alloc([128, DC, 32], BF16)
        vtok, vtokk = A.alloc([128, 4, 512], BF16)
        Ltok, Ltokk = A.alloc([128, 4, 256], F32)
        E1, E1k = A.alloc([128, 2, 128], F32)
        E2, E2k = A.alloc([128, 2, 128], F32)
        gaT, gaTk = A.alloc([128, 512], BF16)
        xf, xfk = A.alloc([128, 8], F32)
        wcnt = [0]

        def nextw():
            w = wb[wcnt[0] % 3]
            wcnt[0] += 1
            return w[0], MK(((w[1], 0), (w[1], 1)))

        load_w(wsm, wsmk, win, C_FF, 8, dcol=0)
        load_w(wsm, wsmk, win, C_GA, 16, dcol=8)
        dma("pool", wau[:, :], w_au[l], [], ["wau"])
        b_alpha = sm(l, 52, 256)
        b_f = sm(l, 308, 8)

        def gla_common(ti, c0, n, blks):
            bk = bank()
            for c in range(DC):
                mm(ps[bk][0:16, 0:n], wsm[:, c, 8:24], hn[:, c, 0:n], c == 0, c == DC - 1, [wsmk, hnk], [PSK(bk)])
            cp("act", gaT[0:16, 0:n], ps[bk][0:16, 0:n], [PSK(bk)], [gaTk])
            wt, wk = nextw()
            load_w(wt, wk, win, C_GV, 512)
            for bi, blk in enumerate(blks):
                bc, nb = bcol(blk)
                lo = bc - c0
                bk = bank()
                for c in range(DC):
                    mm(ps[bk][0:nb, 0:512], hn[:, c, lo:lo + nb], wt[:, c, :], c == 0, c == DC - 1, [wk, hnk], [PSK(bk)])
                cp("act", vtok[0:nb, bi, :], ps[bk][0:nb, 0:512], [PSK(bk)], [(vtokk, bi)])
                bk = bank()
                mm(ps[bk][0:nb, 0:256], gaT[0:16, lo:lo + nb], wau[:, :], True, True, [gaTk, "wau"], [PSK(bk)])
                tt("dve", Ltok[0:nb, bi, :], ps[bk][0:nb, 0:256], b_alpha[0:nb, :], ALU.add, [PSK(bk), "smalls"], [(Ltokk, bi)])
                logsig_pos(Ltok[0:nb, bi, :], (Ltokk, bi), Ltok[0:nb, bi, :], (Ltokk, bi), nb)

        def cb_exp(bi, nb, need_e2):
            for pr in range(2):
                bk = bank()
                mm(ps[bk][:, 0:nb], Ltok[0:nb, bi, pr * 128:(pr + 1) * 128], trin16[0:nb, 0:nb], True, True, [(Ltokk, bi), "constS"], [PSK(bk)])
                act(E1[:, pr, 0:nb], ps[bk][:, 0:nb], AF.Exp, [PSK(bk)], [(E1k, pr)])
                if need_e2:
                    act(E2[:, pr, 0:nb], ps[bk][:, 0:nb], AF.Exp, [PSK(bk)], [(E2k, pr)], scale=-1.0)

        rg = [[0, 1, 2, 3], [4, 5, 6, 7]]
        gathered = set()

        def gather_chunk(c_):
            if c_ in gathered or c_ < 0 or c_ >= NCH:
                return
            gathered.add(c_)
            for (P_, G_, nm) in ((PK[l][c_], GK[l][c_], "K"), (PV[l][c_], GV[l][c_], "V"), (PU[l][c_], GU[l][c_], "U")):
                S.op("pool", lambda e, P_=P_, G_=G_: e.collective_compute("AllGather", ALU.bypass, replica_groups=rg, ins=[P_.opt()], outs=[G_.opt()]),
                     [("P" + nm, l, c_)], [("G" + nm, l, c_)], dma=True, cc=True)

        mk1 = A.mark()
        ktile, ktilek = A.alloc([128, 8, 512], BF16)
        vt, vtk = A.alloc([128, 512], BF16)
        Lf, Lfk = A.alloc([128, 8], F32)
        ktk, ktkk = A.alloc([128, 256], BF16)
        Er, Erk = A.alloc([128, 256], F32)
        usb, usbk = A.alloc([128, 258], F32)
        memset("dve", runS[:, :], 0.0, ["runS"])

        def p1_tile(ti, c0, n, blks):
            prenorm(l, 16, ti, c0, n, hn, hnk, sqb, sqk, rstd, rstdk)
            wt, wk = nextw()
            load_w(wt, wk, win, C_FK, 512)
            for h in range(8):
                bk = bank()
                for c in range(DC):
                    mm(ps[bk][0:64, 0:n], wt[:, c, h * 64:(h + 1) * 64], hn[:, c, 0:n], c == 0, c == DC - 1, [wk, hnk], [PSK(bk)])
                cp("act" if h % 2 == 0 else "dve", ktile[0:64, h, 0:n], ps[bk][0:64, 0:n], [PSK(bk)], [(ktilek, h)])
            if ti == 0:
                cp("dve", kmeta[0:64, :, 0:16], ktile[0:64, :, 0:16], [(ktilek, h_) for h_ in range(8)], ["kmeta"])
            else:
                dma("sp", PK[l][ti - 1].rearrange("(h d) t -> d h t", h=8)[:, :, 0:n], ktile[0:64, :, 0:n], [(ktilek, h_) for h_ in range(8)], [("PK", l, ti - 1)])
            wt, wk = nextw()
            load_w(wt, wk, win, C_FV, 512)
            for bi, blk in enumerate(blks):
                bc, nb = bcol(blk)
                lo = bc - c0
                bk = bank()
                for c in range(DC):
                    mm(ps[bk][0:nb, 0:512], hn[:, c, lo:lo + nb], wt[:, c, :], c == 0, c == DC - 1, [wk, hnk], [PSK(bk)])
                if blk == 0:
                    cp("act", vmeta[0:16, :, 0:64], ps[bk][0:16, 0:512].rearrange("p (h d) -> p h d", h=8), [PSK(bk)], ["vmeta"])
                else:
                    cp("act", vt[0:nb, :], ps[bk][0:nb, 0:512], [PSK(bk)], [vtk])
                    dma("sp", PV[l][(blk - 1) // CH].rearrange("(h s) (b d) -> s h b d", h=8, b=CH)[:, :, (blk - 1) % CH, :],
                        vt[0:nb, :].rearrange("p (h d) -> p h d", h=8), [vtk], [("PV", l, (blk - 1) // CH)])
                bk = bank()
                for c in range(DC):
                    mm(ps[bk][0:nb, 0:8], hn[:, c, lo:lo + nb], wsm[:, c, 0:8], c == 0, c == DC - 1, [wsmk, hnk], [PSK(bk)])
                tt("dve", xf[0:nb, :], ps[bk][0:nb, 0:8], b_f[0:nb, :], ALU.add, [PSK(bk), "smalls"], [xfk])
                logsig_pos(xf[0:nb, :], xfk, Lf[0:nb, :], Lfk, nb)
                bk = bank()
                mm(ps[bk][0:nb, 0:8], trin1[0:nb, 0:nb], Lf[0:nb, :], True, True, [Lfk, "constS"], [PSK(bk)])
                mm(ps[bk][:, 8:16], nones_t[0:nb, :], Lf[0:nb, :], True, True, [Lfk, "constS3"], [PSK(bk)])
                if blk == 0:
                    cp("dve", lcS[0:nb, 0, :], ps[bk][0:nb, 0:8], [PSK(bk)], [("lcS", 0)])
                    cp("dve", m0S[:, :], ps[bk][:, 8:16], [PSK(bk)], ["m0S"])
                else:
                    tt("dve", lcS[0:nb, blk, :], ps[bk][0:nb, 0:8], runS[0:nb, :], ALU.add, [PSK(bk), "runS"], [("lcS", blk)])
                    tt("dve", runS[:, :], runS[:, :], ps[bk][:, 8:16], ALU.add, [PSK(bk), "runS"], ["runS"])
            gla_common(ti, c0, n, blks)
            wt, wk = nextw()
            load_w(wt, wk, win, C_GK, 256)
            for bi, blk in enumerate(blks):
                bc, nb = bcol(blk)
                lo = bc - c0
                cb_exp(bi, nb, False)
                bk = bank()
                for c in range(DC):
                    mm(ps[bk][0:nb, 0:256], hn[:, c, lo:lo + nb], wt[:, c, 0:256], c == 0, c == DC - 1, [wk, hnk], [PSK(bk)])
                bk2 = bank()
                mm(ps[bk2][0:nb, 0:256], trirev16[0:nb, 0:nb], Ltok[0:nb, bi, :], True, True, [(Ltokk, bi), "constS"], [PSK(bk2)])
                act(Er[0:nb, :], ps[bk2][0:nb, 0:256], AF.Exp, [PSK(bk2)], [Erk])
                tt("dve", ktk[0:nb, :], ps[bk][0:nb, 0:256], Er[0:nb, :], ALU.mult, [PSK(bk), Erk], [ktkk])
                bk = bank()
                for hh in range(4):
                    pr, hb = hh // 2, (hh % 2) * 64
                    mm(ps[bk][hb:hb + 64, pr * 128:(pr + 1) * 128], ktk[0:nb, hh * 64:(hh + 1) * 64], vtok[0:nb, bi, hh * 128:(hh + 1) * 128],
                       True, True, [ktkk, (vtokk, bi)], [PSK(bk)], tile_position=(0, hb))
                dst, dstk = (U0, "U0") if blk == 0 else (usb, usbk)
                cp("act", dst[:, 0:256], ps[bk][:, 0:256], [PSK(bk)], [dstk])
                cp("dve", dst[:, 256:258], E1[:, :, nb - 1], [(E1k, 0), (E1k, 1)], [dstk])
                if blk != 0:
                    dma("sp", PU[l][(blk - 1) // CH][:, ((blk - 1) % CH) * 258:((blk - 1) % CH + 1) * 258], usb[:, :], [usbk], [("PU", l, (blk - 1) // CH)])

        def after_group(grp):
            for t_ in sorted(grp, key=lambda t: t[0]):
                p1_tile(t_[0], t_[1], t_[2], t_[3])
                if t_[0] >= 1:
                    gather_chunk(t_[0] - 1)

        if stages is None or f"ffn0_{l}" in stages:
            ffn(l, 0, shared=(hn, hnkb, (sq0, sq1), (sqk0, sqk1), rstd, rstdkb, wb), after_group=after_group)
        else:
            for t_ in tiles:
                after_group([t_])
        for blk in range(1, NB + 1):
            tt("dve", pbS[:, (blk - 1) * 8:blk * 8], runS[:, :], lcS[:, blk, :], ALU.subtract, ["runS", ("lcS", blk)], ["pbS"])
        cp("dve", pbS[:, NB * 8:NB * 8 + 8], runS[:, :], ["runS"], ["pbS"])
        dma("sp", PB[l][:, :], pbS[:, :], ["pbS"], [("PB", l)])
        if mix_stop == "p1":
            S.barrier()
            A.release(mk0)
            return
        for c_ in range(NCH):
            gather_chunk(c_)
        S.op("pool", lambda e: e.collective_compute("AllGather", ALU.bypass, replica_groups=rg, ins=[PB[l].opt()], outs=[GB[l].opt()]),
             [("PB", l)], [("GB", l)], dma=True, cc=True)
        S.barrier()
        A.release(mk1)
        if mix_stop == "p2":
            S.barrier()
            A.release(mk0)
            return
        for i in range(3):
            dma("sp", gbT[:, i, :], GB[l][i * 128:(i + 1) * 128, :], [("GB", l)], ["gbT"])
        Tb = lambda i: gbT[:, i, NB * 8:NB * 8 + 8]
        e_ = lambda i: cvec[:, i:i + 1]
        vis = lambda i: cvec[:, 3 + i:4 + i]
        ts("dve", Gt[:, 2, :], Tb(2), e_(2), ALU.mult, ["gbT", "cvec"], ["Gt"])
        stt("dve", Gt[:, 0, :], Tb(1), e_(1), Gt[:, 2, :], ALU.mult, ALU.add, ["gbT", "cvec", "Gt"], ["Gt"])
        stt("dve", Gt[:, 3, :], Tb(0), e_(0), Gt[:, 0, :], ALU.mult, ALU.add, ["gbT", "cvec", "Gt"], ["Gt"])
        ts("dve", Gt[:, 1, :], Gt[:, 2, :], vis(1), ALU.add, ["Gt", "cvec"], ["Gt"])
        ts("dve", Gt[:, 0, :], Gt[:, 0, :], vis(0), ALU.add, ["Gt", "cvec"], ["Gt"])
        ts("dve", Gt[:, 2, :], Gt[:, 2, :], 0.0, ALU.mult, ["Gt"], ["Gt"], s2=vis(2), op1=ALU.add)
        for i in range(3):
            tt("dve", gbT[:, i, 0:NB * 8].rearrange("p (b h) -> p b h", h=8), gbT[:, i, 0:NB * 8].rearrange("p (b h) -> p b h", h=8),
               Gt[:, i, :].unsqueeze(1).to_broadcast([128, NB, 8]), ALU.add, ["gbT", "Gt"], ["gbT"])
        ts("dve", bown[:, :, :], lcS[:, :, :], -1.0, ALU.mult, [("lcS", b) for b in range(NB + 1)], ["bown"])
        tt("dve", bmeta[:, 0, :], m0S[:, :], lcS[:, 0, :], ALU.subtract, ["m0S", ("lcS", 0)], ["bmeta"])
        tt("dve", bmeta[:, 0, :], bmeta[:, 0, :], Gt[:, 3, :], ALU.add, ["bmeta", "Gt"], ["bmeta"])
        ts("dve", bmeta[:, 1, :], lcS[:, 0, :], -1.0, ALU.mult, [("lcS", 0)], ["bmeta"])
        mk2 = A.mark()
        ubig = [[A.alloc([128, CH * 258], F32) for _ in range(NCH)] for _ in range(3)]
        for i in range(3):
            for c_ in range(NCH):
                dma("sp", ubig[i][c_][0][:, :], GU[l][c_][i * 128:(i + 1) * 128, :], [("GU", l, c_)], [ubig[i][c_][1]])
        ome, omek = A.alloc([128, 3], F32)
        ts("dve", ome[:, :], cvec[:, 0:3], -1.0, ALU.mult, ["cvec"], [omek], s2=1.0, op1=ALU.add)
        dpa = [A.alloc([128, NB, 2], F32) for _ in range(3)]
        Fs = [A.alloc([128, 2, 128], F32) for _ in range(3)]
        Da = [A.alloc([128, 2], F32) for _ in range(3)]
        for i in range(3):
            for c_ in range(NCH):
                ubt, uk = ubig[i][c_]
                ts("dve", dpa[i][0][:, c_ * CH:(c_ + 1) * CH, :], ubt[:, :].rearrange("p (c w) -> p c w", c=CH)[:, :, 256:258], e_(i), ALU.mult,
                   [uk, "cvec"], [dpa[i][1]], s2=ome[:, i:i + 1], op1=ALU.add)
            memset("pool", Fs[i][0][:, :, :], 0.0, [Fs[i][1]])
            memset("pool", Da[i][0][:, :], 1.0, [Da[i][1]])
        for ck in range(NB):
            for i in range(3):
                tt("dve", Fs[i][0][:, :, :], Fs[i][0][:, :, :], dpa[i][0][:, ck, :].unsqueeze(2).to_broadcast([128, 2, 128]), ALU.mult,
                   [Fs[i][1], dpa[i][1]], [Fs[i][1]])
            for i in range(3):
                ubt, uk = ubig[i][ck // CH]
                u = ubt[:, (ck % CH) * 258:(ck % CH + 1) * 258]
                stt("dve", Fs[i][0][:, :, :], u[:, 0:256].rearrange("p (a v) -> p a v", a=2), e_(i), Fs[i][0][:, :, :], ALU.mult, ALU.add,
                    [uk, "cvec", Fs[i][1]], [Fs[i][1]])
            for i in range(3):
                tt("pool", Da[i][0][:, :], Da[i][0][:, :], dpa[i][0][:, ck, :], ALU.mult, [Da[i][1], dpa[i][1]], [Da[i][1]])
        cp("dve", Sst[:, :, :], U0[:, 0:256].rearrange("p (a v) -> p a v", a=2), ["U0"], ["Sst"])
        for i in range(3):
            tt("dve", Sst[:, :, :], Sst[:, :, :], Da[i][0][:, 0:2].unsqueeze(2).to_broadcast([128, 2, 128]), ALU.mult, ["Sst", Da[i][1]], ["Sst"])
            tt("dve", Sst[:, :, :], Sst[:, :, :], Fs[i][0][:, :, :], ALU.add, ["Sst", Fs[i][1]], ["Sst"])
        S.barrier()
        A.release(mk2)
        if mix_stop == "prep":
            S.barrier()
            A.release(mk0)
            return
        mk3 = A.mark()
        QT, QTk = A.alloc([128, 8, 512], BF16)
        OT, OTk = A.alloc([128, 4, 512], BF16)
        GT, GTk = A.alloc([128, 4, 512], BF16)
        sgr, sgrk = A.alloc([128, 4, 512], BF16)
        qg, qgk = A.alloc([128, 2, 512], BF16)
        kg, kgk = A.alloc([128, 2, 512], BF16)
        yT, yTk = A.alloc([128, DC, 512], BF16)
        oT, oTk = A.alloc([128, DC, 512], F32)
        memset("pool", QT[:, :, :], 0.0, [(QTk, h_) for h_ in range(8)])
        c3, c3k = A.alloc([128, 8], F32)
        r3, r3k = A.alloc([128, 8], F32)
        mk4 = A.mark()
        for (ti, c0, n, blks) in tiles:
            prenorm(l, 16, ti, c0, n, hn, hnk, sqb, sqk, rstd, rstdk)
            wt, wk = nextw()
            load_w(wt, wk, win, C_FQ, 512)
            for h in range(8):
                bk = bank()
                for c in range(DC):
                    mm(ps[bk][0:64, 0:n], wt[:, c, h * 64:(h + 1) * 64], hn[:, c, 0:n], c == 0, c == DC - 1, [wk, hnk], [PSK(bk)])
                act(QT[0:64, h, 0:n], ps[bk][0:64, 0:n], AF.Copy, [PSK(bk)], [(QTk, h)], scale=0.125)
            for bi, blk in enumerate(blks):
                bc, nb = bcol(blk)
                lo = bc - c0
                cp("dve", split[0:nb, :, 64], lcS[0:nb, blk, :], [("lcS", blk)], ["split"])
                tt("dve", r3[0:nb, :], lcS[0:nb, blk, :], split[0:nb, :, 64], ALU.subtract, [("lcS", blk), "split"], [r3k])
                cp("dve", split[0:nb, :, 65], r3[0:nb, :], [r3k], ["split"])
                tt("dve", c3[0:nb, :], r3[0:nb, :], split[0:nb, :, 65], ALU.subtract, [r3k, "split"], [c3k])
                cp("dve", split[0:nb, :, 66], c3[0:nb, :], [c3k], ["split"])
                for h in range(8):
                    bk = bank()
                    mm(ps[bk][0:67, 0:nb], split[0:nb, h, :], identb[0:nb, 0:nb], True, True, ["split", "identb"], [PSK(bk)])
                    cp("act" if h % 2 == 0 else "dve", QT[64:67, h, lo:lo + nb], ps[bk][64:67, 0:nb], [PSK(bk)], [(QTk, h)])
            if mix_stop == "q":
                continue
            gla_common(ti, c0, n, blks)
            wt, wk = nextw()
            load_w(wt, wk, win, C_GQ, 512)
            for pr in range(2):
                bk = bank()
                for c in range(DC):
                    mm(ps[bk][:, 0:n], wt[:, c, pr * 128:(pr + 1) * 128], hn[:, c, 0:n], c == 0, c == DC - 1, [wk, hnk], [PSK(bk)])
                cp("act", qg[:, pr, 0:n], ps[bk][:, 0:n], [PSK(bk)], [qgk])
                bk = bank()
                for c in range(DC):
                    mm(ps[bk][:, 0:n], wt[:, c, 256 + pr * 128:256 + (pr + 1) * 128], hn[:, c, 0:n], c == 0, c == DC - 1, [wk, hnk], [PSK(bk)])
                cp("dve", kg[:, pr, 0:n], ps[bk][:, 0:n], [PSK(bk)], [kgk])
            wt, wk = nextw()
            load_w(wt, wk, win, C_GR, 512)
            for hh in range(4):
                bk = bank()
                for c in range(DC):
                    mm(ps[bk][:, 0:n], wt[:, c, hh * 128:(hh + 1) * 128], hn[:, c, 0:n], c == 0, c == DC - 1, [wk, hnk], [PSK(bk)])
                act(sgr[:, hh, 0:n], ps[bk][:, 0:n], AF.Silu, [PSK(bk)], [sgrk])
            if mix_stop == "glaproj":
                continue
            mkf = A.mark()
            KTb = [A.alloc([128, SEG], BF16) for _ in range(2)]
            Vb = [A.alloc([128, NB, 128], BF16) for _ in range(2)]
            PT = [A.alloc([128, 512], BF16) for _ in range(5)]
            rec, reck = A.alloc([128, 512], F32)
            for (t_, k_) in KTb:
                memset("pool", t_[64:67, :], 1.0, [k_])
            for (t_, k_) in Vb:
                memset("pool", t_[:, :, 64:128], 1.0, [(k_, c_) for c_ in range(NCH)])
            LA = 3
            groups = []
            units = []
            for h in range(8):
                bx = 6 + (h % 2)
                hu = []
                if ti == 0:
                    hu.append(dict(kt=kmeta[0:67, h, 0:16], v=vmeta[0:16, h, :], nk=16, bias=bmeta[0:16, 1, h:h + 1], rk=["kmeta", "vmeta"],
                                   mask=maskA[0:16, 0, 0:16], q0=0, grp=None))
                else:
                    hu.append(dict(kt=kmeta[0:67, h, 0:16], v=vmeta[0:16, h, :], nk=16, bias=bmeta[0:16, 0, h:h + 1], rk=["kmeta", "vmeta"],
                                   mask=None, q0=0, grp=None))
                    for src in (3, 0, 1, 2):
                        nblk_src = NB if src < 3 else blks[-1]
                        gi = len(groups)
                        groups.append((h, src, nblk_src))
                        for kb in range(nblk_src):
                            u = dict(nk=128, q0=0, mask=None, grp=gi, kb=kb)
                            if src < 3:
                                u["bias"] = gbT[:, src, kb * 8 + h:kb * 8 + h + 1]
                            else:
                                u["bias"] = bown[:, kb + 1, h:h + 1]
                                r = (kb + 1) - blks[0]
                                if r >= 0:
                                    u["q0"] = r * 128
                                    u["mask"] = maskA[:, r, r * 128:n]
                            hu.append(u)
                for i_, u in enumerate(hu):
                    u["h"], u["bx"], u["first"], u["last"] = h, bx, i_ == 0, i_ == len(hu) - 1
                units += hu
            gbuf = {}

            def load_group(gi):
                if gi >= len(groups) or gi in gbuf:
                    return
                h, src, nblk_src = groups[gi]
                kt_, ktk_ = KTb[gi % 2]
                v_, vk_ = Vb[gi % 2]
                ncs = nblk_src // CH
                if src < 3:
                    ksrc = GKa[l][:, src * 512 + h * 64:src * 512 + (h + 1) * 64, :]
                    vsrc = GVa[l][:, src * 1024 + h * 128:src * 1024 + (h + 1) * 128, :]
                    kkeys = [("GK", l, c_) for c_ in range(ncs)]
                    vkeys = [("GV", l, c_) for c_ in range(ncs)]
                else:
                    ksrc = PKa[l][0:ncs, h * 64:(h + 1) * 64, :]
                    vsrc = PVa[l][0:ncs, h * 128:(h + 1) * 128, :]
                    kkeys = [("PK", l, c_) for c_ in range(ncs)]
                    vkeys = [("PV", l, c_) for c_ in range(ncs)]
                dma("sp", kt_[0:64, 0:ncs * CH * 128].rearrange("d (c t) -> d c t", c=ncs), ksrc.rearrange("c d t -> d c t"), kkeys, [ktk_])
                for c_ in range(ncs):
                    dma("sp", v_[:, c_ * CH:(c_ + 1) * CH, 0:64], vsrc[c_].rearrange("s (b d) -> s b d", b=CH), [vkeys[c_]], [(vk_, c_)])
                gbuf[gi] = (kt_, ktk_, v_, vk_)

            def stage_a(ui, u):
                if u["grp"] is not None:
                    load_group(u["grp"])
                    kt_, ktk_, v_, vk_ = gbuf[u["grp"]]
                    kb = u["kb"]
                    u["kt"], u["v"], u["rk"] = kt_[0:67, kb * 128:(kb + 1) * 128], v_[:, kb, :], [ktk_, (vk_, kb // CH)]
                nk, q0, h = u["nk"], u["q0"], u["h"]
                bs = ui % 6
                mm(ps[bs][0:nk, q0:n], u["kt"], QT[0:67, h, q0:n], True, True, u["rk"] + [(QTk, h)], [PSK(bs)])
                p_, pk_ = PT[ui % 5]
                act(p_[0:nk, q0:n], ps[bs][0:nk, q0:n], AF.Exp, [PSK(bs), "gbT", "bown", "bmeta"], [pk_], bias=u["bias"])
                if u["mask"] is not None:
                    tt("dve", p_[0:nk, q0:n], p_[0:nk, q0:n], u["mask"], ALU.mult, [pk_, "maskA"], [pk_])

            def stage_b(ui, u):
                nk, q0, h, bx = u["nk"], u["q0"], u["h"], u["bx"]
                p_, pk_ = PT[ui % 5]
                mm(ps[bx][:, q0:n], u["v"], p_[0:nk, q0:n], u["first"], u["last"], u["rk"] + [pk_], [PSK(bx)])
                if u["grp"] is not None and u["kb"] == 0:
                    load_group(u["grp"] + 1)
                if u["last"]:
                    recip(rec[0:64, 0:n], ps[bx][64:128, 0:n], [PSK(bx)], [reck])
                    hb = (h % 2) * 64
                    tt("dve", OT[hb:hb + 64, h // 2, 0:n], ps[bx][0:64, 0:n], rec[0:64, 0:n], ALU.mult, [PSK(bx), reck], [OTk])

            for i_ in range(len(units) + LA):
                if i_ < len(units):
                    stage_a(i_, units[i_])
                if i_ - LA >= 0:
                    stage_b(i_ - LA, units[i_ - LA])
            S.barrier()
            A.release(mkf)
            if mix_stop == "fox":
                continue
            mkg = A.mark()
            qp, qpk = A.alloc([128, 4, 128], BF16)
            kp, kpk = A.alloc([128, 4, 128], BF16)
            Sbh, _ = A.alloc([128, 4, 128], BF16)
            attm, attmk = A.alloc([128, 4, 128], BF16)
            sqg, sqgk = A.alloc([128, 512], F32)
            rsg, rsgk = A.alloc([128, 512], F32)
            t1, t1k = A.alloc([128, 512], F32)
            uc = [A.alloc([128, 258], F32) for _ in range(2)]
            for bi, blk in enumerate(blks):
                bc, nb = bcol(blk)
                lo = bc - c0
                cb_exp(bi, nb, True)
                for hh in range(4):
                    pr, hb = hh // 2, (hh % 2) * 64
                    stt("dve", qp[0:64, hh, 0:nb], qg[hb:hb + 64, pr, lo:lo + nb], 0.125, E1[hb:hb + 64, pr, 0:nb], ALU.mult, ALU.mult, [qgk, (E1k, pr)], [qpk])
                    tt("dve", kp[0:64, hh, 0:nb], kg[hb:hb + 64, pr, lo:lo + nb], E2[hb:hb + 64, pr, 0:nb], ALU.mult, [kgk, (E2k, pr)], [kpk])
                ba = bank()
                for hh in range(4):
                    mm(ps[ba][0:nb, hh * 128:hh * 128 + nb], kp[0:64, hh, 0:nb], qp[0:64, hh, 0:nb], True, True, [kpk, qpk], [PSK(ba)])
                tt("dve", attm[0:nb, :, 0:nb], ps[ba][0:nb, :].rearrange("p (a t) -> p a t", a=4)[:, :, 0:nb],
                   trif[0:nb, 0:nb].unsqueeze(1).to_broadcast([nb, 4, nb]), ALU.mult, [PSK(ba), "constS"], [attmk])
                if blk == 0:
                    memset("dve", Sbh[:, :, :], 0.0, ["Sbh"])
                else:
                    for hh in range(4):
                        pr, hb = hh // 2, (hh % 2) * 64
                        cp("dve", Sbh[0:64, hh, :], Sst[hb:hb + 64, pr, :], ["Sst"], ["Sbh"])
                bo = bank()
                for hh in range(4):
                    mm(ps[bo][:, hh * 128:hh * 128 + nb], Sbh[0:64, hh, :], qp[0:64, hh, 0:nb], True, False, ["Sbh", qpk], [PSK(bo)])
                    mm(ps[bo][:, hh * 128:hh * 128 + nb], vtok[0:nb, bi, hh * 128:(hh + 1) * 128], attm[0:nb, hh, 0:nb], False, True,
                       [(vtokk, bi), attmk], [PSK(bo)])
                if mix_stop == "gla2":
                    continue
                if blk != 0:
                    u, uk = uc[bi % 2]
                    dma("sp", u[:, :], PU[l][(blk - 1) // CH][:, ((blk - 1) % CH) * 258:((blk - 1) % CH + 1) * 258], [("PU", l, (blk - 1) // CH)], [uk])
                    for pr in range(2):
                        stt("dve", Sst[:, pr, :], Sst[:, pr, :], u[:, 256 + pr:257 + pr], u[:, pr * 128:(pr + 1) * 128], ALU.mult, ALU.add,
                            ["Sst", uk], ["Sst"])
                if mix_stop == "gla3":
                    continue
                o4 = ps[bo][:, :].rearrange("p (a t) -> p a t", a=4)[:, :, 0:nb]
                act(sqg[:, 0:4 * nb].rearrange("p (a t) -> p a t", a=4), o4, AF.Square, [PSK(bo)], [sqgk])
                bs = bank()
                mm(ps[bs][:, 0:4 * nb], ones_t[:, :], sqg[:, 0:4 * nb], True, True, [sqgk, "constS2"], [PSK(bs)])
                act(rsg[:, 0:4 * nb], ps[bs][:, 0:4 * nb], AF.Sqrt, [PSK(bs)], [rsgk], bias=epsb[:, 0:1], scale=1.0 / 128)
                recip(rsg[:, 0:4 * nb], rsg[:, 0:4 * nb], [rsgk], [rsgk])
                tt("dve", t1[:, 0:4 * nb].rearrange("p (a t) -> p a t", a=4), o4, rsg[:, 0:4 * nb].rearrange("p (a t) -> p a t", a=4), ALU.mult,
                   [PSK(bo), rsgk], [t1k])
                for hh in range(4):
                    stt("dve", GT[:, hh, lo:lo + nb], t1[:, hh * nb:(hh + 1) * nb], sm(l, 48 + hh, 1), sgr[:, hh, lo:lo + nb], ALU.mult, ALU.mult,
                        [t1k, "smalls", sgrk], [GTk])
            S.barrier()
            A.release(mkg)
            if mix_stop in ("gla", "gla2", "gla3"):
                continue
            mkm = A.mark()
            sga, sgak = A.alloc([128, 512], F32)
            sgbt, sgbk = A.alloc([128, 512], F32)
            t2, t2k = A.alloc([128, 512], F32)
            tm0, tmk0 = A.alloc([128, 512], F32)
            tm1, tmk1 = A.alloc([128, 512], F32)
            for cg in range(2):
                wa, wak = nextw()
                load_w(wa, wak, win, C_MA + cg * 512, 512)
                wbb, wbk = nextw()
                load_w(wbb, wbk, win, C_MB + cg * 512, 512)
                wo, wok = nextw()
                dma("pool", wo[:, 0:4, :], w_fo[l].rearrange("(c p) f -> p c f", p=128)[:, :, cg * 512:(cg + 1) * 512], [], [wok])
                dma("pool", wo[:, 4:8, :], w_go[l].rearrange("(c p) f -> p c f", p=128)[:, :, cg * 512:(cg + 1) * 512], [], [wok])
                for cc in range(4):
                    c = cg * 4 + cc
                    b1, b2, b3, b4 = bank(), bank(), bank(), bank()
                    for k in range(DC):
                        mm(ps[b1][:, 0:n], wa[:, k, cc * 128:(cc + 1) * 128], hn[:, k, 0:n], k == 0, k == DC - 1, [wak, hnk], [PSK(b1)])
                    for k in range(DC):
                        mm(ps[b2][:, 0:n], wbb[:, k, cc * 128:(cc + 1) * 128], hn[:, k, 0:n], k == 0, k == DC - 1, [wbk, hnk], [PSK(b2)])
                    for k in range(4):
                        mm(ps[b3][:, 0:n], wo[:, k, cc * 128:(cc + 1) * 128], OT[:, k, 0:n], k == 0, k == 3, [wok, OTk], [PSK(b3)])
                    for k in range(4):
                        mm(ps[b4][:, 0:n], wo[:, 4 + k, cc * 128:(cc + 1) * 128], GT[:, k, 0:n], k == 0, k == 3, [wok, GTk], [PSK(b4)])
                    act(sga[:, 0:n], ps[b1][:, 0:n], AF.Sigmoid, [PSK(b1)], [sgak])
                    act(sgbt[:, 0:n], ps[b2][:, 0:n], AF.Sigmoid, [PSK(b2)], [sgbk])
                    tt("dve", t2[:, 0:n], sga[:, 0:n], ps[b3][:, 0:n], ALU.mult, [sgak, PSK(b3)], [t2k])
                    tt("dve", sgbt[:, 0:n], sgbt[:, 0:n], ps[b4][:, 0:n], ALU.mult, [sgbk, PSK(b4)], [sgbk])
                    tt("dve", yT[:, c, 0:n], t2[:, 0:n], sgbt[:, 0:n], ALU.add, [t2k, sgbk], [(yTk, c)])
            for c2 in range(DC):
                if c2 % 4 == 0:
                    wo, wok = nextw()
                    load_w(wo, wok, w_out[l], c2 * 128, 512)
                bk = bank()
                for k in range(DC):
                    mm(ps[bk][:, 0:n], wo[:, k, (c2 % 4) * 128:(c2 % 4 + 1) * 128], yT[:, k, 0:n], k == 0, k == DC - 1, [wok, (yTk, k)], [PSK(bk)])
                cp("act", oT[:, c2, 0:n], ps[bk][:, 0:n], [PSK(bk)], [(oTk, "c", c2)])
            postnorm_residual(l, 24, ti, c0, n, oT, oTk, sqb, sqk, rstd, rstdk, [tm0, tm1], [tmk0, tmk1], 1.0)
            S.barrier()
            A.release(mkm)
        S.barrier()
        A.release(mk0)

    for l in range(L):
        if stages is None or f"mix_{l}" in stages:
            mixer(l)
        elif f"ffn0_{l}" in stages:
            ffn(l, 0)
        if stages is None or f"ffn1_{l}" in stages:
            ffn(l, 1)
    finals = []
    for (ti, c0, n, blks) in tiles[1:]:
        finals.append(dma("sp", outT.rearrange("(c p) t -> p c t", p=128)[:, :, c0 - 16:c0 - 16 + n], hT[:, :, c0:c0 + n], [MK(tuple(("hT", ti, c_) for c_ in range(DC)))], [("outT", ti)]))
    if dumps:
        dump("hT", hT[:, :, :], [MK(tuple(("hT", t[0], c_) for c_ in range(DC))) for t in tiles], [128, DC, TL])
    S.emit(final_ops=finals + list(dump_outs.values()))
    return nc, S


def make_consts():
    s = np.arange(128)[:, None]
    t = np.arange(128)[None, :]
    le = (s <= t).astype(np.float32)
    c = np.zeros((128, 5 * 128 + 4 * 512), np.float32)
    c[:, 0:128] = np.eye(128, dtype=np.float32)
    c[:, 128:256] = le * (-1.0 / 16.0)
    c[:, 256:384] = (s > t).astype(np.float32) * (-1.0 / 16.0)
    c[:, 384:512] = -le
    c[:, 512:640] = le
    for r in range(4):
        m = np.zeros((128, 512), np.float32)
        for q in range(4):
            if q == r:
                m[:, q * 128:(q + 1) * 128] = le
            elif q > r:
                m[:, q * 128:(q + 1) * 128] = 1.0
        c[:, 640 + r * 512:640 + (r + 1) * 512] = m
    return c


def make_smalls(inp, L):
    sm = np.zeros((128, L * NSM), np.float32)
    names = ["g_pre_ffn1", "g_post_ffn1", "g_pre_mix", "g_post_mix", "g_pre_ffn2", "g_post_ffn2"]
    for l in range(L):
        o = l * NSM
        for i, nm in enumerate(names):
            sm[:, o + i * 8:o + (i + 1) * 8] = np.asarray(inp[nm], np.float32)[l].reshape(8, 128).T
        sm[:, o + 48:o + 52] = np.asarray(inp["g_gla_out"], np.float32)[l].reshape(4, 128).T
        sm[:, o + 52:o + 308] = np.asarray(inp["b_alpha"], np.float32)[l][None, :]
        sm[:, o + 308:o + 316] = np.asarray(inp["b_f"], np.float32)[l][None, :]
    return sm


def make_cvec(j):
    v = np.zeros((128, 8), np.float32)
    for i in range(3):
        v[:, i] = 1.0 if i < j else 0.0
        v[:, 3 + i] = 0.0 if i < j else -30000.0
    return v


_CACHE = {}


def run(inputs, NB, L=2, dumps=None, stages=None, mix_stop=None):
    key = (NB, L, tuple(dumps) if dumps else None)
    x = np.asarray(inputs["x"], np.float32)
    B, SEQ, _ = x.shape
    assert B == 2 and SEQ == 4 * NB * 128
    nc, S = build_program(NB, L, dumps, stages, mix_stop)
    consts = make_consts()
    smalls = make_smalls(inputs, L)
    metaT = np.ascontiguousarray(np.asarray(inputs["meta_tokens"], np.float32).T)
    shared = dict(consts=consts, smalls=smalls, metaT=metaT)
    for nm in ["w_in", "w_alpha_up", "w_fox_o", "w_gla_o", "w_out", "w_ffn1_gu", "w_ffn1_down", "w_ffn2_gu", "w_ffn2_down"]:
        shared[nm] = np.ascontiguousarray(np.asarray(inputs[nm], np.float32)[:L])
    in_maps = []
    SEG = NB * 128
    for core in range(8):
        b, j = core // 4, core % 4
        m = dict(shared)
        m["xT"] = np.ascontiguousarray(x[b, j * SEG:(j + 1) * SEG, :].T)
        m["cvec"] = make_cvec(j)
        in_maps.append(m)
    res = run_bass_kernel_spmd(nc, in_maps, core_ids=list(range(8)))
    out = np.zeros((B, SEQ, D), np.float32)
    for core in range(8):
        b, j = core // 4, core % 4
        out[b, j * SEG:(j + 1) * SEG, :] = np.asarray(res.results[core]["outT"]).T
    return out, res


def kernel(**inputs):
    out, _ = run(inputs, 16, 2)
    return out
```
